# Optimizing a Trainium2 kernel written in Bass

```python
import math
import jax, jax.numpy as jnp
from jax import lax
import numpy as np

D_MODEL = 1024
BATCH = 8
SEQ = 2048
DEPTH = 1
DEC_BATCH = 128
DEC_SEQ = 1
PAST_LEN = 16384
PAGE_SIZE = 128

D_SSD = D_MODEL
SSD_HEADDIM = 64
H_SSD = D_SSD // SSD_HEADDIM
SSD_GROUPS = 2
SSD_STATE = 128
CONV_K = 4
CONV_DIM = D_SSD + 2 * SSD_GROUPS * SSD_STATE
SSD_CHUNK = 128
D_RWKV = D_MODEL
HD_RWKV = 64
H_RWKV = D_RWKV // HD_RWKV
W_RANK = 64
A_RANK = 64
G_RANK = 128
RWKV_PROJ = 3 * D_RWKV + W_RANK + A_RANK + G_RANK
RWKV_LN_EPS = 64e-5
D_IN = D_SSD + CONV_DIM + H_SSD + RWKV_PROJ + 2 * D_MODEL
N_MEM = 256
MEM_HEADS = 4
MEM_HD = D_MODEL // MEM_HEADS
N_KEYS = 128
N_EXPERTS = N_KEYS * N_KEYS
PEER_HEADS = 8
PEER_DKEY = 256
PEER_HALF = PEER_DKEY // 2
PEER_TOPK = 16
PEER_BLOCK = 128
NORM_EPS = 1e-6

kernel_name = 'hybrid_ssd_rwkv7_peer_decode_step'


def _rmsnorm(x, w):
    xf = x.astype(jnp.float32)
    y = xf * lax.rsqrt(jnp.mean(xf * xf, axis=-1, keepdims=True) + NORM_EPS)
    return (y * w.astype(jnp.float32)).astype(x.dtype)


def _split(x, sizes):
    offs = []
    acc = 0
    for s in sizes[:-1]:
        acc += s
        offs.append(acc)
    return jnp.split(x, offs, axis=-1)


def _segsum(a):
    cs = jnp.cumsum(a, axis=-1)
    t = a.shape[-1]
    mask = jnp.tril(jnp.ones((t, t), dtype=bool))
    return jnp.where(mask, cs[..., :, None] - cs[..., None, :], -jnp.inf)


def _ssd(xh, dt, a_head, bm, cm, init_state):
    b, L = xh.shape[0], xh.shape[1]
    l = SSD_CHUNK if L % SSD_CHUNK == 0 else L
    c = L // l
    g, e = SSD_GROUPS, H_SSD // SSD_GROUPS
    xc = (xh * dt[..., None]).reshape(b, c, l, g, e, SSD_HEADDIM)
    ad = jnp.moveaxis((dt * a_head).reshape(b, c, l, g, e), (1, 2), (3, 4))
    bc = bm.reshape(b, c, l, g, SSD_STATE)
    cc = cm.reshape(b, c, l, g, SSD_STATE)
    a_cs = jnp.cumsum(ad, axis=-1)
    lmat = jnp.exp(_segsum(ad))
    cb = jnp.einsum('bclgn,bcsgn->bgcls', cc, bc)
    y_diag = jnp.einsum('bgcls,bgecls,bcsgep->bclgep', cb, lmat, xc)
    decay_states = jnp.exp(a_cs[..., -1:] - a_cs)
    states = jnp.einsum('bclgn,bgecl,bclgep->bcgepn', bc, decay_states, xc)
    init = init_state.reshape(b, 1, g, e, SSD_HEADDIM, SSD_STATE)
    states = jnp.concatenate([init, states], axis=1)
    chunk_a = jnp.pad(a_cs[..., -1], ((0, 0), (0, 0), (0, 0), (1, 0)))
    decay_chunk = jnp.exp(_segsum(chunk_a))
    new_states = jnp.einsum('bgezc,bcgepn->bzgepn', decay_chunk, states)
    y_off = jnp.einsum('bclgn,bcgepn,bgecl->bclgep', cc, new_states[:, :-1], jnp.exp(a_cs))
    y = (y_diag + y_off).reshape(b, L, H_SSD, SSD_HEADDIM)
    final = new_states[:, -1].reshape(b, H_SSD, SSD_HEADDIM, SSD_STATE)
    return y, final


def _rwkv7_scan(r, decay, k, v, kk_a, kk_b, s0):
    def step(s, inp):
        r_t, w_t, k_t, v_t, a_t, b_t = inp
        sa = jnp.einsum('bhvk,bhk->bhv', s, a_t)
        s = s * w_t[:, :, None, :] + sa[..., None] * b_t[:, :, None, :] + v_t[..., None] * k_t[:, :, None, :]
        return s, jnp.einsum('bhvk,bhk->bhv', s, r_t)
    xs = tuple(jnp.moveaxis(t, 1, 0) for t in (r, decay, k, v, kk_a, kk_b))
    s_final, ys = lax.scan(step, s0, xs)
    return jnp.moveaxis(ys, 0, 1), s_final


def _gated_mixer(h, conv_prev, ssm_prev, shift_prev, wkv_prev, p):
    f32 = jnp.float32
    b, L, _ = h.shape
    proj = h @ p['w_in']
    z, xbc, dt_raw, rw, gate_raw = _split(proj, (D_SSD, CONV_DIM, H_SSD, RWKV_PROJ, 2 * D_MODEL))

    xbc_all = jnp.concatenate([conv_prev.astype(xbc.dtype), xbc], axis=1)
    conv = p['conv_b'] + sum(xbc_all[:, i:i + L] * p['conv_w'][i] for i in range(CONV_K))
    conv_new = xbc_all[:, L:]
    xs, bm, cm = _split(jax.nn.silu(conv.astype(f32)), (D_SSD, SSD_GROUPS * SSD_STATE, SSD_GROUPS * SSD_STATE))
    xs = xs.reshape(b, L, H_SSD, SSD_HEADDIM)
    dt = jax.nn.softplus(dt_raw.astype(f32) + p['dt_bias'].astype(f32))
    a_head = -jnp.exp(p['a_log'].astype(f32))
    y, ssm_new = _ssd(xs, dt, a_head, bm.reshape(b, L, SSD_GROUPS, SSD_STATE),
                      cm.reshape(b, L, SSD_GROUPS, SSD_STATE), ssm_prev.astype(f32))
    y = (y + p['d_skip'].astype(f32)[:, None] * xs).reshape(b, L, D_SSD)
    y_ssd = _rmsnorm(y * jax.nn.silu(z.astype(f32)), p['ssd_norm_w'])

    rw_prev = jnp.concatenate([shift_prev[:, None].astype(rw.dtype), rw[:, :-1]], axis=1)
    rws = (rw + (rw_prev - rw) * p['rwkv_mu']).astype(f32)
    shift_new = rw[:, -1]
    r, k, v, wl, al, gl = _split(rws, (D_RWKV, D_RWKV, D_RWKV, W_RANK, A_RANK, G_RANK))
    wlog = -jax.nn.softplus(-(p['rwkv_w0'] + jnp.tanh(wl) @ p['rwkv_w2'])) - 0.5
    decay = jnp.exp(-jnp.exp(wlog))
    a = jax.nn.sigmoid(p['rwkv_a0'] + al @ p['rwkv_a2'])
    g = jax.nn.sigmoid(gl) @ p['rwkv_g2']
    heads = lambda t: t.reshape(b, L, H_RWKV, HD_RWKV)
    kk = heads(k * p['rwkv_k_k'])
    kk = kk * lax.rsqrt(jnp.maximum(jnp.sum(kk * kk, axis=-1, keepdims=True), 1e-24))
    kh = heads(k * (1.0 + (a - 1.0) * p['rwkv_k_a']))
    rh, vh, ah = heads(r), heads(v), heads(a)
    yr, wkv_new = _rwkv7_scan(rh, heads(decay), kh, vh, -kk, kk * ah, wkv_prev.astype(f32))
    mean = jnp.mean(yr, axis=-1, keepdims=True)
    var = jnp.mean(jnp.square(yr - mean), axis=-1, keepdims=True)
    yr = ((yr - mean) * lax.rsqrt(var + RWKV_LN_EPS)).reshape(b, L, D_RWKV) * p['rwkv_ln_w'] + p['rwkv_ln_b']
    bonus = jnp.sum(rh * kh * p['rwkv_r_k'], axis=-1, keepdims=True) * vh
    y_rwkv = (yr + bonus.reshape(b, L, D_RWKV)) * g

    gate_ssd, gate_rwkv = jnp.split(jax.nn.sigmoid(gate_raw.astype(f32)), 2, axis=-1)
    merged = gate_ssd * y_ssd + gate_rwkv * y_rwkv
    out = merged.astype(h.dtype) @ p['w_out']
    return (out, conv_new.astype(h.dtype), ssm_new.astype(h.dtype),
            shift_new.astype(h.dtype), wkv_new.astype(h.dtype))


def _memory_kv(mem, mem_norm_w, w_mk, w_mv):
    b, n, _ = mem.shape
    m = _rmsnorm(mem, mem_norm_w)
    return ((m @ w_mk).reshape(b, n, MEM_HEADS, MEM_HD), (m @ w_mv).reshape(b, n, MEM_HEADS, MEM_HD))


def _cross_attn(h, mem_k, mem_v, w_mq, w_mo):
    b, L, _ = h.shape
    q = (h @ w_mq).reshape(b, L, MEM_HEADS, MEM_HD)
    s = jnp.einsum('blhd,bmhd->bhlm', q.astype(jnp.float32), mem_k.astype(jnp.float32)) * (MEM_HD ** -0.5)
    prob = jax.nn.softmax(s, axis=-1).astype(h.dtype)
    o = jnp.einsum('bhlm,bmhd->blhd', prob, mem_v.astype(h.dtype)).reshape(b, L, D_MODEL)
    return o @ w_mo


def _peer(h, w_pq, sub_keys, expert_u, expert_v):
    b, L, d = h.shape
    t = b * L
    pad = (-t) % PEER_BLOCK
    xt = jnp.pad(h.reshape(t, d), ((0, pad), (0, 0))).reshape(-1, PEER_BLOCK, d)

    def one_block(xb):
        q = (xb @ w_pq).reshape(PEER_BLOCK, PEER_HEADS, 2, PEER_HALF).astype(jnp.float32)
        s = jnp.einsum('thcd,hckd->thck', q, sub_keys.astype(jnp.float32))
        s1, i1 = lax.top_k(s[:, :, 0], PEER_TOPK)
        s2, i2 = lax.top_k(s[:, :, 1], PEER_TOPK)
        cand_s = (s1[..., :, None] + s2[..., None, :]).reshape(PEER_BLOCK, PEER_HEADS, PEER_TOPK * PEER_TOPK)
        cand_i = (i1[..., :, None] * N_KEYS + i2[..., None, :]).reshape(PEER_BLOCK, PEER_HEADS, PEER_TOPK * PEER_TOPK)
        top_s, pos = lax.top_k(cand_s, PEER_TOPK)
        idx = jnp.take_along_axis(cand_i, pos, axis=-1)
        gate = jax.nn.softmax(top_s, axis=-1)
        act = jax.nn.gelu(jnp.einsum('td,thkd->thk', xb, expert_u[idx]).astype(jnp.float32), approximate=False)
        coef = (gate * act).astype(xb.dtype)
        return jnp.einsum('thk,thkd->td', coef, expert_v[idx])

    out = lax.map(one_block, xt)
    return out.reshape(-1, d)[:t].reshape(b, L, d)


def _block(x, mem_k, mem_v, conv_prev, ssm_prev, shift_prev, wkv_prev, p):
    mix, conv_new, ssm_new, shift_new, wkv_new = _gated_mixer(
        _rmsnorm(x, p['norm_mix_w']), conv_prev, ssm_prev, shift_prev, wkv_prev, p)
    x = x + mix
    x = x + _cross_attn(_rmsnorm(x, p['norm_mem_w']), mem_k, mem_v, p['w_mq'], p['w_mo'])
    x = x + _peer(_rmsnorm(x, p['norm_ffn_w']), p['w_pq'], p['sub_keys'], p['expert_u'], p['expert_v'])
    return x, conv_new, ssm_new, shift_new, wkv_new


def setup_inputs(seed: int = 0) -> dict:
    key = jax.random.key(seed)
    ks = iter(jax.random.split(key, 64))
    f32 = jnp.float32
    L = DEPTH

    def nrm(shape, scale):
        return jax.random.normal(next(ks), shape, f32) * scale

    def unif(shape, lo, hi):
        return jax.random.uniform(next(ks), shape, f32, lo, hi)

    def gain(shape):
        return 1.0 + nrm(shape, 0.02)

    dt0 = jnp.exp(unif((L, H_SSD), math.log(1e-3), math.log(1e-1)))
    return {
        'x_prompt': nrm((BATCH, SEQ, D_MODEL), 1.0),
        'x_sample': nrm((DEC_BATCH, DEC_SEQ, D_MODEL), 1.0),
        'mem_prompt': nrm((BATCH, N_MEM, D_MODEL), 1.0),
        'state_ssm': nrm((L, DEC_BATCH, H_SSD, SSD_HEADDIM, SSD_STATE), 0.3),
        'state_conv': nrm((L, DEC_BATCH, CONV_K - 1, CONV_DIM), 1.0),
        'state_wkv': nrm((L, DEC_BATCH, H_RWKV, HD_RWKV, HD_RWKV), 0.3),
        'state_shift': nrm((L, DEC_BATCH, RWKV_PROJ), 1.0),
        'cache_mem_k': nrm((L, DEC_BATCH, N_MEM, MEM_HEADS, MEM_HD), 1.0),
        'cache_mem_v': nrm((L, DEC_BATCH, N_MEM, MEM_HEADS, MEM_HD), 1.0),
        'norm_mix_w': gain((L, D_MODEL)),
        'w_in': nrm((L, D_MODEL, D_IN), D_MODEL ** -0.5),
        'conv_w': nrm((L, CONV_K, CONV_DIM), CONV_K ** -0.5),
        'conv_b': nrm((L, CONV_DIM), 0.02),
        'dt_bias': dt0 + jnp.log(-jnp.expm1(-dt0)),
        'a_log': jnp.log(unif((L, H_SSD), 1.0, 16.0)),
        'd_skip': gain((L, H_SSD)),
        'ssd_norm_w': gain((L, D_SSD)),
        'rwkv_mu': unif((L, RWKV_PROJ), 0.0, 1.0),
        'rwkv_w0': unif((L, D_RWKV), -4.0, 0.0),
        'rwkv_w2': nrm((L, W_RANK, D_RWKV), 0.1),
        'rwkv_a0': nrm((L, D_RWKV), 0.1),
        'rwkv_a2': nrm((L, A_RANK, D_RWKV), 0.1),
        'rwkv_g2': nrm((L, G_RANK, D_RWKV), G_RANK ** -0.5),
        'rwkv_k_k': 0.85 + nrm((L, D_RWKV), 0.02),
        'rwkv_k_a': gain((L, D_RWKV)),
        'rwkv_r_k': nrm((L, H_RWKV, HD_RWKV), 0.1),
        'rwkv_ln_w': gain((L, D_RWKV)),
        'rwkv_ln_b': nrm((L, D_RWKV), 0.02),
        'w_out': nrm((L, D_MODEL, D_MODEL), D_MODEL ** -0.5),
        'norm_mem_w': gain((L, D_MODEL)),
        'mem_norm_w': gain((L, D_MODEL)),
        'w_mk': nrm((L, D_MODEL, D_MODEL), D_MODEL ** -0.5),
        'w_mv': nrm((L, D_MODEL, D_MODEL), D_MODEL ** -0.5),
        'w_mq': nrm((L, D_MODEL, D_MODEL), D_MODEL ** -0.5),
        'w_mo': nrm((L, D_MODEL, D_MODEL), D_MODEL ** -0.5),
        'norm_ffn_w': gain((L, D_MODEL)),
        'w_pq': nrm((L, D_MODEL, PEER_HEADS * PEER_DKEY), D_MODEL ** -0.5),
        'sub_keys': nrm((L, PEER_HEADS, 2, N_KEYS, PEER_HALF), PEER_HALF ** -0.5),
        'expert_u': nrm((L, N_EXPERTS, D_MODEL), D_MODEL ** -0.5),
        'expert_v': nrm((L, N_EXPERTS, D_MODEL), 0.25),
        'norm_final_w': gain((D_MODEL,)),
    }


def reference(x_prompt, x_sample, mem_prompt, state_ssm, state_conv, state_wkv, state_shift,
              cache_mem_k, cache_mem_v, norm_mix_w, w_in, conv_w, conv_b, dt_bias, a_log, d_skip,
              ssd_norm_w, rwkv_mu, rwkv_w0, rwkv_w2, rwkv_a0, rwkv_a2, rwkv_g2, rwkv_k_k, rwkv_k_a,
              rwkv_r_k, rwkv_ln_w, rwkv_ln_b, w_out, norm_mem_w, mem_norm_w, w_mk, w_mv, w_mq, w_mo,
              norm_ffn_w, w_pq, sub_keys, expert_u, expert_v, norm_final_w):
    bp = x_prompt.shape[0]
    xp, xs = x_prompt, x_sample
    ssm_p, conv_p, wkv_p, shift_p, mk_p, mv_p = [], [], [], [], [], []
    ssm_s, conv_s, wkv_s, shift_s = [], [], [], []
    for l in range(DEPTH):
        p = {
            'norm_mix_w': norm_mix_w[l], 'w_in': w_in[l], 'conv_w': conv_w[l], 'conv_b': conv_b[l],
            'dt_bias': dt_bias[l], 'a_log': a_log[l], 'd_skip': d_skip[l], 'ssd_norm_w': ssd_norm_w[l],
            'rwkv_mu': rwkv_mu[l], 'rwkv_w0': rwkv_w0[l], 'rwkv_w2': rwkv_w2[l], 'rwkv_a0': rwkv_a0[l],
            'rwkv_a2': rwkv_a2[l], 'rwkv_g2': rwkv_g2[l], 'rwkv_k_k': rwkv_k_k[l], 'rwkv_k_a': rwkv_k_a[l],
            'rwkv_r_k': rwkv_r_k[l], 'rwkv_ln_w': rwkv_ln_w[l], 'rwkv_ln_b': rwkv_ln_b[l], 'w_out': w_out[l],
            'norm_mem_w': norm_mem_w[l], 'w_mq': w_mq[l], 'w_mo': w_mo[l], 'norm_ffn_w': norm_ffn_w[l],
            'w_pq': w_pq[l], 'sub_keys': sub_keys[l], 'expert_u': expert_u[l], 'expert_v': expert_v[l],
        }
        mem_k, mem_v = _memory_kv(mem_prompt, mem_norm_w[l], w_mk[l], w_mv[l])
        xp, c_new, s_new, sh_new, w_new = _block(
            xp, mem_k, mem_v,
            jnp.zeros((bp, CONV_K - 1, CONV_DIM), xp.dtype),
            jnp.zeros((bp, H_SSD, SSD_HEADDIM, SSD_STATE), jnp.float32),
            jnp.zeros((bp, RWKV_PROJ), xp.dtype),
            jnp.zeros((bp, H_RWKV, HD_RWKV, HD_RWKV), jnp.float32), p)
        ssm_p.append(s_new)
        conv_p.append(c_new)
        wkv_p.append(w_new)
        shift_p.append(sh_new)
        mk_p.append(mem_k)
        mv_p.append(mem_v)
        xs, c_new, s_new, sh_new, w_new = _block(
            xs, cache_mem_k[l], cache_mem_v[l], state_conv[l], state_ssm[l], state_shift[l], state_wkv[l], p)
        ssm_s.append(s_new)
        conv_s.append(c_new)
        wkv_s.append(w_new)
        shift_s.append(sh_new)
    y_prompt = _rmsnorm(xp, norm_final_w)
    y_sample = _rmsnorm(xs, norm_final_w)
    ssm_prompt = jnp.stack(ssm_p)
    conv_prompt = jnp.stack(conv_p)
    wkv_prompt = jnp.stack(wkv_p)
    shift_prompt = jnp.stack(shift_p)
    mem_k_prompt = jnp.stack(mk_p)
    mem_v_prompt = jnp.stack(mv_p)
    ssm_sample = jnp.stack(ssm_s)
    conv_sample = jnp.stack(conv_s)
    wkv_sample = jnp.stack(wkv_s)
    shift_sample = jnp.stack(shift_s)
    return (y_prompt, y_sample, ssm_prompt, conv_prompt, wkv_prompt, shift_prompt, mem_k_prompt, mem_v_prompt,
            ssm_sample, conv_sample, wkv_sample, shift_sample)
```

```python
import os
import numpy as np
from contextlib import ExitStack
import concourse.bass as bass
import concourse.mybir as mybir
from concourse.bass_utils import run_bass_kernel_spmd

F32 = mybir.dt.float32
BF16 = mybir.dt.bfloat16
U32 = mybir.dt.uint32
ALU = mybir.AluOpType
AF = mybir.ActivationFunctionType
AX = mybir.AxisListType

NCORES = 8
D = 1024
SEQ = 2048
NCH = SEQ // 128
NS = 16
D_IN = 7952
CONV_DIM = 1536
RWP = 3328
NMEM = 256
EPS = 1e-6

SAME_ENG_SYNC = True
NDMA_SEMS = 12


class Prog:
    def __init__(self, nc):
        self.nc = nc
        self.names = ["pe", "act", "dve", "pool", "sp"]
        self.streams = {k: [] for k in self.names}
        self.count = {k: 0 for k in self.names}
        self.waited = {}
        self.res = {}
        self.dma_ring = {k: 0 for k in self.names}
        self.dma_val = {}

    def _deps(self, reads, writes):
        deps = []
        for r in reads:
            e = self.res.get(r)
            if e and e[0] is not None:
                deps.append(e[0])
        for w in writes:
            e = self.res.get(w)
            if e:
                if e[0] is not None:
                    deps.append(e[0])
                deps.extend(e[1])
        return deps

    def _commit(self, tok, reads, writes):
        for r in reads:
            e = self.res.setdefault(r, [None, []])
            e[1].append(tok)
        for w in writes:
            self.res[w] = [tok, []]

    def _waits(self, engine, deps):
        best = {}
        for (sk, v) in deps:
            if sk == engine and (engine == "pe" or not SAME_ENG_SYNC):
                continue
            if self.waited.get((engine, sk), 0) >= v:
                continue
            if best.get(sk, 0) < v:
                best[sk] = v
        for sk, v in best.items():
            self.waited[(engine, sk)] = v
        return list(best.items())

    def op(self, engine, fn, reads=(), writes=()):
        reads = list(reads)
        writes = list(writes)
        waits = self._waits(engine, self._deps(reads, writes))
        self.count[engine] += 1
        tok = (engine, self.count[engine])
        self.streams[engine].append((fn, waits, (engine, 1)))
        self._commit(tok, reads, writes)
        return tok

    def dma(self, engine, fn, reads=(), writes=()):
        reads = list(reads)
        writes = list(writes)
        ring = self.dma_ring[engine]
        self.dma_ring[engine] = (ring + 1) % NDMA_SEMS
        sk = ("dma", engine, ring)
        prev = self.dma_val.get(sk, 0)
        deps = self._deps(reads, writes)
        if prev:
            deps.append((sk, prev))
        waits = self._waits(engine, deps)
        self.dma_val[sk] = prev + 16
        tok = (sk, prev + 16)
        self.streams[engine].append((fn, waits, (sk, 16)))
        self._commit(tok, reads, writes)
        return tok

    def emit(self):
        nc = self.nc
        with ExitStack() as es:
            sems = {}
            for k in self.names:
                sems[k] = es.enter_context(nc.semaphore("s_" + k))
            for sk in self.dma_val:
                sems[sk] = es.enter_context(nc.semaphore("d_%s_%d" % (sk[1], sk[2])))
            final = dict(self.dma_val)
            for k in self.names:
                if k != "sp" and self.count[k]:
                    final[k] = self.count[k]
            block = es.enter_context(nc.Block())

            def run(e, name):
                for fn, waits, inc in self.streams[name]:
                    for sk, v in waits:
                        e.wait_ge(sems[sk], v)
                    fn(e).then_inc(sems[inc[0]], inc[1])
                if name == "sp":
                    for sk, v in final.items():
                        e.wait_ge(sems[sk], v)

            @block.tensor
            def _(e):
                run(e, "pe")

            @block.scalar
            def _(e):
                run(e, "act")

            @block.vector
            def _(e):
                run(e, "dve")

            @block.gpsimd
            def _(e):
                run(e, "pool")

            @block.sync
            def _(e):
                run(e, "sp")


def fm(v, n):
    return np.ascontiguousarray(np.asarray(v, np.float32).reshape(n, 128).T)


STAGE = int(os.environ.get('K_STAGE', 99))
SUB = int(os.environ.get('K_SUB', 99))
SUB2 = int(os.environ.get('K_SUB2', 99))
SUB3 = int(os.environ.get('K_SUB3', 99))


def build_program(stage=None):
    stage = STAGE if stage is None else stage
    nc = bass.Bass("TRN2", target_bir_lowering=False)

    def din(name, shape):
        return nc.dram_tensor(name, list(shape), F32, kind="ExternalInput").ap()

    def dout(name, shape):
        return nc.dram_tensor(name, list(shape), F32, kind="ExternalOutput").ap()

    xp = din("xp", [SEQ, D]); xs_in = din("xs", [NS, D]); memp = din("memp", [NMEM, D])
    st_ssm = din("st_ssm", [NS, 16, 64, 128]); st_conv = din("st_conv", [NS, 3, CONV_DIM])
    st_wkv = din("st_wkv", [NS, 16, 64, 64]); st_shift = din("st_shift", [NS, RWP])
    ck = din("ck", [NS, NMEM, D]); cv = din("cv", [NS, NMEM, D])
    w_in = din("w_in", [D, D_IN]); w_out = din("w_out", [D, D])
    w_mk = din("w_mk", [D, D]); w_mv = din("w_mv", [D, D]); w_mq = din("w_mq", [D, D]); w_mo = din("w_mo", [D, D])
    w_pq = din("w_pq", [D, 2048]); sub_keys = din("sub_keys", [16, 128, 128])
    if stage >= 0:
        exp_u = din("exp_u", [16384, D]); exp_v = din("exp_v", [16384, D])
    cfm = din("cfm", [128, 192])
    ctk = din("ctk", [1, 48 + D])
    wa2 = din("wa2", [128, D]); g2 = din("g2", [128, D])
    cmat = din("cmat", [128, 6 * 128 + 16])

    y_p = dout("y_p", [SEQ, D]); y_s = dout("y_s", [NS, D])
    ssm_p = dout("ssm_p", [16 * 64, 128]); conv_p = dout("conv_p", [3, CONV_DIM])
    wkv_p = dout("wkv_p", [16, 64, 64]); shift_p = dout("shift_p", [1, RWP])
    mk_p = dout("mk_p", [NMEM, D]); mv_p = dout("mv_p", [NMEM, D])
    ssm_s = dout("ssm_s", [NS, 16 * 64, 128]); conv_s = dout("conv_s", [NS, 3, CONV_DIM])
    wkv_s = dout("wkv_s", [NS, 16, 64, 64]); shift_s = dout("shift_s", [NS, RWP])

    es = ExitStack()
    with es:
        def sb(name, shape, dt=F32):
            return es.enter_context(nc.sbuf_tensor(name, list(shape), dt))

        P = Prog(nc)
        es.enter_context(nc.allow_non_contiguous_dma(reason="small transposing stores"))
        ps = [es.enter_context(nc.psum_tensor("ps%d" % i, [128, 512], F32)) for i in range(8)]
        bank_ctr = [0]

        def bank():
            b = bank_ctr[0]
            bank_ctr[0] = (b + 1) % 8
            return b

        CM = sb("CM", [128, 6 * 128 + 16]); CF = sb("CF", [128, 192]); CT = sb("CT", [128, 48 + D])
        P.dma("sp", lambda e: e.dma_start(out=CM[:], in_=cmat), writes=["CM"])
        P.dma("sp", lambda e: e.dma_start(out=CF[:], in_=cfm), writes=["CF"])
        P.dma("sp", lambda e: e.dma_start(out=CT[:], in_=ctk.partition_broadcast(128)), writes=["CT"])
        ident = CM[:, 0:128]; M1 = CM[:, 128:256]; M2 = CM[:, 256:384]; ONES = CM[:, 384:512]
        BONES = CM[:, 512:640]; M3 = CM[:, 640:768]; IOTA = CM[:, 768:784]

        xres = sb("xres", [128, D]); xres2 = sb("xres2", [128, D])
        CUR = [xres, "xres"]
        xn = sb("xn", [128, D])
        junk = sb("junk", [128, D])
        st1 = sb("st1", [128, 8])
        hT = sb("hT", [128, 8, 128])
        WB = [sb("wb%d" % i, [128, 8, 512]) for i in range(2)]
        wb_ctr = [0]
        xbcT = sb("xbcT", [128, 12, 131])
        rwT = sb("rwT", [128, 26, 129])
        gateT = sb("gateT", [128, 16, 128])
        ztok = sb("ztok", [128, D])
        dtraw = sb("dtraw", [128, 16])

        def rmsnorm_T(src, nt, wcol, dstT, src_key, dst_key):
            P.op("act", lambda e: e.activation(out=junk[:nt, :], in_=src[:nt, :], func=AF.Square, accum_out=st1[:nt, 0:1]),
                 reads=[src_key], writes=["junk", "st1"])
            P.op("act", lambda e: e.activation(out=st1[:nt, 1:2], in_=st1[:nt, 0:1], func=AF.Sqrt, scale=1.0 / D, bias=EPS),
                 reads=["st1"], writes=["st1"])
            P.op("dve", lambda e: e.reciprocal(out=st1[:nt, 2:3], in_=st1[:nt, 1:2]), reads=["st1"], writes=["st1"])
            P.op("dve", lambda e: e.tensor_scalar(out=xn[:nt, :], in0=src[:nt, :], scalar1=st1[:nt, 2:3], scalar2=None,
                                                  op0=ALU.mult), reads=["st1", src_key], writes=["xn"])
            for half in range(2):
                b = bank()
                for q in range(4):
                    c = half * 4 + q
                    P.op("pe", lambda e, c=c, q=q, b=b: e.transpose(out=ps[b][:, q * 128:q * 128 + nt],
                                                                     in_=xn[:nt, c * 128:(c + 1) * 128], identity=ident[:nt, :nt]),
                         reads=["xn", "CM"], writes=[("ps", b)])
                P.op("dve", lambda e, half=half, b=b: e.tensor_tensor(
                    out=dstT[:, half * 4:half * 4 + 4, :nt],
                    in0=ps[b][:].rearrange("p (q t) -> p q t", q=4)[:, :, :nt],
                    in1=CF[:, wcol + half * 4:wcol + half * 4 + 4].unsqueeze(2).to_broadcast([128, 4, nt]),
                    op=ALU.mult), reads=[("ps", b), "CF"], writes=[dst_key])

        def load_w(wdram, col0, ncols, eng="sp"):
            i = wb_ctr[0]
            wb_ctr[0] = (i + 1) % 2
            P.dma(eng, lambda e: e.dma_start(out=WB[i][:, :, :ncols],
                                             in_=wdram[:, col0:col0 + ncols].rearrange("(kc p) n -> p kc n", p=128)),
                  writes=[("wb", i)])
            return i

        def proj_fm(wdram, col0, nchunks, srcT, src_key, nt, evac, eng="sp"):
            ncols = nchunks * 128
            i = load_w(wdram, col0, ncols, eng)
            b = bank()
            for q in range(nchunks):
                for kc in range(8):
                    P.op("pe", lambda e, q=q, kc=kc: e.matmul(ps[b][:, q * 128:q * 128 + nt],
                                                                lhsT=WB[i][:, kc, q * 128:(q + 1) * 128], rhs=srcT[:, kc, :nt],
                                                                start=(kc == 0), stop=(kc == 7)),
                         reads=[("wb", i), src_key], writes=[("ps", b)])
            evac(b, nchunks)

        def proj_tok(wdram, col0, ncols, srcT, src_key, nt, evac, eng="sp"):
            i = load_w(wdram, col0, ncols, eng)
            b = bank()
            for kc in range(8):
                P.op("pe", lambda e, kc=kc: e.matmul(ps[b][:nt, :ncols], lhsT=srcT[:, kc, :nt], rhs=WB[i][:, kc, :ncols],
                                                       start=(kc == 0), stop=(kc == 7)),
                     reads=[("wb", i), src_key], writes=[("ps", b)])
            evac(b, ncols)

        def psv(b, nchunks, nt):
            return ps[b][:, :nchunks * 128].rearrange("p (q t) -> p q t", q=nchunks)[:, :, :nt]

        def in_proj(nt, t0_xbc, t0_rw):
            for blk in range(2):
                proj_tok(w_in, blk * 512, 512, hT, "hT", nt,
                         lambda b, n, blk=blk: P.op("act", lambda e: e.activation(out=ztok[:nt, blk * 512:(blk + 1) * 512], in_=ps[b][:nt, :512], func=AF.Silu),
                                                    reads=[("ps", b)], writes=["ztok"]))
            for blk in range(3):
                proj_fm(w_in, 1024 + blk * 512, 4, hT, "hT", nt,
                        lambda b, n, blk=blk: P.op("dve", lambda e: e.tensor_copy(out=xbcT[:, blk * 4:blk * 4 + 4, t0_xbc:t0_xbc + nt], in_=psv(b, 4, nt)),
                                                   reads=[("ps", b)], writes=["xbcT"]))
            proj_tok(w_in, 2560, 16, hT, "hT", nt,
                     lambda b, n: P.op("dve", lambda e: e.tensor_copy(out=dtraw[:nt, :], in_=ps[b][:nt, :16]), reads=[("ps", b)], writes=["dtraw"]))
            for blk in range(7):
                n = 4 if blk < 6 else 2
                proj_fm(w_in, 2576 + blk * 512, n, hT, "hT", nt,
                        lambda b, n, blk=blk: P.op("act", lambda e: e.activation(out=rwT[:, blk * 4:blk * 4 + n, t0_rw:t0_rw + nt], in_=psv(b, n, nt), func=AF.Copy),
                                                   reads=[("ps", b)], writes=["rwT"]))
            for blk in range(4):
                proj_fm(w_in, 5904 + blk * 512, 4, hT, "hT", nt,
                        lambda b, n, blk=blk: P.op("act", lambda e: e.activation(out=gateT[:, blk * 4:blk * 4 + 4, :nt], in_=psv(b, 4, nt), func=AF.Sigmoid),
                                                   reads=[("ps", b)], writes=["gateT"]))


        BIGA = sb("BIGA", [128, 2048]); BIGB = sb("BIGB", [128, 2048]); BIGC = sb("BIGC", [128, 2048])
        convT = sb("convT", [128, 12, 128])
        xtok = sb("xtok", [128, 1280])
        sm = sb("sm", [128, 128])
        ANEG = sb("ANEG", [128, 16])
        cbm = sb("cbm", [128, 2, 128])
        xdt = sb("xdt", [128, D]); xdd = sb("xdd", [128, D])
        stT = sb("stT", [128, D])
        ysb = sb("ysb", [128, D])
        mgT = sb("mgT", [128, 8, 128])
        P.op("act", lambda e: e.activation(out=ANEG[:], in_=CT[:, 16:32], func=AF.Exp), reads=["CT"], writes=["ANEG"])
        P.op("dve", lambda e: e.tensor_scalar(out=ANEG[:], in0=ANEG[:], scalar1=-1.0, scalar2=None, op0=ALU.mult), reads=["ANEG"], writes=["ANEG"])
        CWv = CF[:, 32:80].rearrange("p (c k) -> p c k", k=4)

        def v3(t, a):
            return t.rearrange("p (a b) -> p a b", a=a)

        def conv_silu(nt):
            A = BIGA[:, :12 * nt].rearrange("p (c t) -> p c t", c=12)
            Bv = BIGB[:, :12 * nt].rearrange("p (c t) -> p c t", c=12)
            P.op("dve", lambda e: e.tensor_tensor(out=A, in0=xbcT[:, :, 0:nt], in1=CWv[:, :, 0:1].to_broadcast([128, 12, nt]), op=ALU.mult),
                 reads=["xbcT", "CF"], writes=["BIGA"])
            for k in range(1, 4):
                P.op("pool", lambda e, k=k: e.tensor_tensor(out=Bv, in0=xbcT[:, :, k:k + nt], in1=CWv[:, :, k:k + 1].to_broadcast([128, 12, nt]), op=ALU.mult),
                     reads=["xbcT", "CF"], writes=["BIGB"])
                P.op("dve", lambda e: e.tensor_tensor(out=A, in0=A, in1=Bv, op=ALU.add), reads=["BIGA", "BIGB"], writes=["BIGA"])
            for c in range(12):
                P.op("act", lambda e, c=c: e.activation(out=convT[:, c, :nt], in_=A[:, c, :], func=AF.Silu, bias=CF[:, 80 + c:81 + c]),
                     reads=["BIGA", "CF"], writes=["convT"])

        def dt_softplus(nt):
            P.op("dve", lambda e: e.tensor_tensor(out=sm[:nt, 0:16], in0=dtraw[:nt, :], in1=CT[:nt, 0:16], op=ALU.add), reads=["dtraw", "CT"], writes=["sm"])
            P.op("act", lambda e: e.activation(out=sm[:nt, 16:32], in_=sm[:nt, 0:16], func=AF.Exp), reads=["sm"], writes=["sm"])
            P.op("act", lambda e: e.activation(out=sm[:nt, 32:48], in_=sm[:nt, 16:32], func=AF.Ln, bias=1.0), reads=["sm"], writes=["sm"])
            P.op("dve", lambda e: e.tensor_tensor(out=sm[:nt, 48:64], in0=sm[:nt, 32:48], in1=ANEG[:nt, :], op=ALU.mult), reads=["sm", "ANEG"], writes=["sm"])

        def ssd_post(nt):
            P.op("dve", lambda e: e.tensor_tensor(out=ysb[:nt, :], in0=ysb[:nt, :], in1=ztok[:nt, :], op=ALU.mult), reads=["ysb", "ztok"], writes=["ysb"])
            rmsnorm_T(ysb, nt, 176, mgT, "ysb", "mgT")
            P.op("dve", lambda e: e.tensor_tensor(out=mgT[:, :, :nt], in0=mgT[:, :, :nt], in1=gateT[:, 0:8, :nt], op=ALU.mult), reads=["mgT", "gateT"], writes=["mgT"])

        def ssd_chunk():
            conv_silu(128)
            for (cs, c0) in (((0, 1, 2, 3), 0), ((4, 5, 6, 7), 512), ((8, 9), 1024)):
                b = bank()
                for q, c in enumerate(cs):
                    P.op("pe", lambda e, q=q, c=c, b=b: e.transpose(out=ps[b][:, q * 128:(q + 1) * 128], in_=convT[:, c, :], identity=ident),
                         reads=["convT", "CM"], writes=[("ps", b)])
                n = len(cs) * 128
                P.op("act", lambda e, b=b, c0=c0, n=n: e.activation(out=xtok[:, c0:c0 + n], in_=ps[b][:, :n], func=AF.Copy), reads=[("ps", b)], writes=["xtok"])
            dt_softplus(128)
            dt = sm[:, 32:48]; ad = sm[:, 48:64]
            P.op("dve", lambda e: e.tensor_tensor(out=v3(BIGA[:], 16), in0=ad.unsqueeze(2).to_broadcast([128, 16, 128]),
                                                  in1=M2.unsqueeze(1).to_broadcast([128, 16, 128]), op=ALU.mult), reads=["sm", "CM"], writes=["BIGA"])
            for q in range(4):
                b = bank()
                P.op("pe", lambda e, q=q, b=b: e.matmul(ps[b][:, :512], lhsT=M1, rhs=BIGA[:, q * 512:(q + 1) * 512], start=True, stop=True),
                     reads=["BIGA", "CM"], writes=[("ps", b)])
                P.op("act", lambda e, q=q, b=b: e.activation(out=BIGB[:, q * 512:(q + 1) * 512], in_=ps[b][:, :512], func=AF.Exp), reads=[("ps", b)], writes=["BIGB"])
                b = bank()
                P.op("pe", lambda e, q=q, b=b: e.matmul(ps[b][:, :512], lhsT=ONES, rhs=BIGA[:, q * 512:(q + 1) * 512], start=True, stop=True),
                     reads=["BIGA", "CM"], writes=[("ps", b)])
                P.op("act", lambda e, q=q, b=b: e.activation(out=BIGC[:, q * 512:(q + 1) * 512], in_=ps[b][:, :512], func=AF.Exp), reads=[("ps", b)], writes=["BIGC"])
            b = bank()
            for g in range(2):
                P.op("pe", lambda e, g=g, b=b: e.matmul(ps[b][:, g * 128:(g + 1) * 128], lhsT=convT[:, 8 + g, :], rhs=convT[:, 10 + g, :], start=True, stop=True),
                     reads=["convT"], writes=[("ps", b)])
            P.op("dve", lambda e, b=b: e.tensor_tensor(out=cbm[:], in0=v3(ps[b][:, :256], 2), in1=M2.unsqueeze(1).to_broadcast([128, 2, 128]), op=ALU.mult),
                 reads=[("ps", b), "CM"], writes=["cbm"])
            G4 = BIGB[:].rearrange("p (g h l) -> p g h l", g=2, h=8)
            P.op("dve", lambda e: e.tensor_tensor(out=G4, in0=G4, in1=cbm[:].unsqueeze(2).to_broadcast([128, 2, 8, 128]), op=ALU.mult),
                 reads=["BIGB", "cbm"], writes=["BIGB"])
            E4 = BIGC[:].rearrange("p (g h l) -> p g h l", g=2, h=8)
            P.op("pool", lambda e: e.tensor_tensor(out=E4, in0=E4, in1=convT[:, 10:12, :].unsqueeze(2).to_broadcast([128, 2, 8, 128]), op=ALU.mult),
                 reads=["BIGC", "convT"], writes=["BIGC"])
            P.op("dve", lambda e: e.tensor_tensor(out=v3(xdt[:], 16), in0=v3(xtok[:, :D], 16), in1=dt.unsqueeze(2).to_broadcast([128, 16, 64]), op=ALU.mult),
                 reads=["xtok", "sm"], writes=["xdt"])
            yb = [bank(), bank()]
            for h in range(16):
                b = yb[h // 8]; o = (h % 8) * 64
                P.op("pe", lambda e, h=h, b=b, o=o: e.matmul(ps[b][:, o:o + 64], lhsT=BIGB[:, h * 128:(h + 1) * 128], rhs=xdt[:, h * 64:(h + 1) * 64], start=True, stop=False),
                     reads=["BIGB", "xdt"], writes=[("ps", b)])
                P.op("pe", lambda e, h=h, b=b, o=o: e.matmul(ps[b][:, o:o + 64], lhsT=BIGC[:, h * 128:(h + 1) * 128], rhs=stT[:, h * 64:(h + 1) * 64], start=False, stop=True),
                     reads=["BIGC", "stT"], writes=[("ps", b)])
            P.op("pool", lambda e: e.tensor_tensor(out=v3(junk[:], 16), in0=v3(xtok[:, :D], 16), in1=CT[:, 32:48].unsqueeze(2).to_broadcast([128, 16, 64]), op=ALU.mult),
                 reads=["xtok", "CT"], writes=["junk"])
            for hh in range(2):
                P.op("dve", lambda e, hh=hh, yb=yb: e.tensor_tensor(out=ysb[:, hh * 512:(hh + 1) * 512], in0=junk[:, hh * 512:(hh + 1) * 512], in1=ps[yb[hh]][:, :512], op=ALU.add),
                     reads=["junk", ("ps", yb[hh])], writes=["ysb"])
            b = bank()
            P.op("pe", lambda e, b=b: e.matmul(ps[b][:, 0:16], lhsT=M1, rhs=ad, start=True, stop=True), reads=["sm", "CM"], writes=[("ps", b)])
            P.op("pe", lambda e, b=b: e.matmul(ps[b][:, 16:32], lhsT=ONES, rhs=ad, start=True, stop=True), reads=["sm", "CM"], writes=[("ps", b)])
            P.op("act", lambda e, b=b: e.activation(out=sm[:, 64:96], in_=ps[b][:, 0:32], func=AF.Exp), reads=[("ps", b)], writes=["sm"])
            P.op("dve", lambda e: e.tensor_tensor(out=v3(xdd[:], 16), in0=v3(xdt[:], 16), in1=sm[:, 64:80].unsqueeze(2).to_broadcast([128, 16, 64]), op=ALU.mult),
                 reads=["xdt", "sm"], writes=["xdd"])
            P.op("dve", lambda e: e.tensor_tensor(out=v3(junk[:], 16), in0=v3(stT[:], 16), in1=sm[:, 80:96].unsqueeze(2).to_broadcast([128, 16, 64]), op=ALU.mult),
                 reads=["stT", "sm"], writes=["junk"])
            for g in range(2):
                b = bank()
                P.op("pe", lambda e, g=g, b=b: e.matmul(ps[b][:, :512], lhsT=xtok[:, D + g * 128:D + (g + 1) * 128], rhs=xdd[:, g * 512:(g + 1) * 512], start=True, stop=True),
                     reads=["xtok", "xdd"], writes=[("ps", b)])
                P.op("dve", lambda e, g=g, b=b: e.tensor_tensor(out=stT[:, g * 512:(g + 1) * 512], in0=junk[:, g * 512:(g + 1) * 512], in1=ps[b][:, :512], op=ALU.add),
                     reads=["junk", ("ps", b)], writes=["stT"])
            ssd_post(128)


        RWS = sb("RWS", [128, 26 * 128]); RA = sb("RA", [128, 2048]); BK = sb("BK", [128, 2048]); BKH = sb("BKH", [128, 2048])
        MASK4 = sb("MASK4", [128, 4, 128])
        Hst = sb("Hst", [128, 8, 64]); smw = sb("smw", [128, 64])
        for kd in range(4):
            P.op("pool", lambda e, kd=kd: e.tensor_copy(out=MASK4[:, kd, :], in_=(M2 if kd % 2 == 0 else M3)), reads=["CM"], writes=["MASK4"])
        P.op("pool", lambda e: e.memset(Hst[:], 0.0), writes=["Hst"])
        RWS3 = RWS[:].rearrange("p (c t) -> p c t", c=26)
        RA4 = RA[:].rearrange("p (c k t) -> p c k t", c=8, k=2)
        BK4 = BK[:].rearrange("p (c k t) -> p c k t", c=8, k=2)
        BKH4 = BKH[:].rearrange("p (c k t) -> p c k t", c=8, k=2)
        T1 = BIGA[:, 0:1024]; T6 = BIGA[:, 1024:2048]; T2 = BIGB[:, 0:1024]; T4 = BIGB[:, 1024:2048]
        T5 = BIGC[:, 0:1024]; T7 = BIGC[:, 1024:2048]; T3 = xdt
        vtok = xdd; btok = ysb; ktok = junk
        ATq = RWS[:, 0:2048].rearrange("p (h k t) -> p h k t", h=4, k=4)
        MN = [RWS[:, 2048 + i * 512:2048 + (i + 1) * 512] for i in range(2)]
        NN = [xn[:, 0:512], xn[:, 512:1024]]
        TT = [convT[:].rearrange("p c t -> p (c t)")[:, 0:512], convT[:].rearrange("p c t -> p (c t)")[:, 512:1024]]
        RHSs = xtok[:, 0:1024]; Us = T1; Ys = T6

        def bc3(ap2, n):
            return ap2.unsqueeze(2).to_broadcast([128, ap2.shape[1], n])

        def rwkv_pre(nt, prev, cur):
            R3 = RWS3[:, :, :nt]
            wi = wb_ctr[0]; wb_ctr[0] = (wi + 1) % 2
            WBf = WB[wi][:].rearrange("p a b -> p (a b)")
            WA2 = WBf[:, 0:1024]; G2 = WBf[:, 1024:2048]; wkey = ("wb", wi)
            P.dma("sp", lambda e: e.dma_start(out=WA2, in_=wa2), writes=[wkey])
            P.dma("sp", lambda e: e.dma_start(out=G2, in_=g2), writes=[wkey])
            P.op("dve", lambda e: e.tensor_tensor(out=R3, in0=prev, in1=cur, op=ALU.subtract), reads=["rwT"], writes=["RWS"])
            P.op("pool", lambda e: e.tensor_tensor(out=R3, in0=R3, in1=bc3(CF[:, 92:118], nt), op=ALU.mult), reads=["RWS", "CF"], writes=["RWS"])
            P.op("dve", lambda e: e.tensor_tensor(out=R3, in0=R3, in1=cur, op=ALU.add), reads=["RWS", "rwT"], writes=["RWS"])
            P.op("act", lambda e: e.activation(out=RWS3[0:64, 24, :nt], in_=RWS3[0:64, 24, :nt], func=AF.Tanh), reads=["RWS"], writes=["RWS"])
            P.op("act", lambda e: e.activation(out=RWS3[:, 25, :nt], in_=RWS3[:, 25, :nt], func=AF.Sigmoid), reads=["RWS"], writes=["RWS"])

            def v8(t):
                return t.rearrange("p (c t) -> p c t", c=8)[:, :, :nt]
            rT = RWS3[:, 0:8, :nt]; kT = RWS3[:, 8:16, :nt]; vT = RWS3[:, 16:24, :nt]
            for half in range(2):
                for (lo, hi, col, dst, dkey) in ((0, 64, 118, T1, "BIGA"), (64, 128, 126, T2, "BIGB")):
                    b = bank()
                    for q in range(4):
                        c = half * 4 + q
                        P.op("pe", lambda e, q=q, c=c, b=b, lo=lo, hi=hi: e.matmul(ps[b][:, q * 128:q * 128 + nt], lhsT=WA2[lo:hi, c * 128:(c + 1) * 128],
                                                                                    rhs=RWS3[lo:hi, 24, :nt], start=True, stop=True),
                             reads=[wkey, "RWS"], writes=[("ps", b)])
                    dv = v8(dst)[:, half * 4:half * 4 + 4, :]
                    P.op("dve", lambda e, b=b, dv=dv, col=col, half=half: e.tensor_tensor(out=dv, in0=psv(b, 4, nt), in1=bc3(CF[:, col + half * 4:col + half * 4 + 4], nt), op=ALU.add),
                         reads=[("ps", b), "CF"], writes=[dkey])
                    P.op("act", lambda e, dv=dv: e.activation(out=dv, in_=dv, func=AF.Sigmoid), reads=[dkey], writes=[dkey])
                b = bank()
                for q in range(4):
                    c = half * 4 + q
                    P.op("pe", lambda e, q=q, c=c, b=b: e.matmul(ps[b][:, q * 128:q * 128 + nt], lhsT=G2[:, c * 128:(c + 1) * 128], rhs=RWS3[:, 25, :nt], start=True, stop=True),
                         reads=[wkey, "RWS"], writes=[("ps", b)])
                P.op("act", lambda e, b=b, half=half: e.activation(out=v8(T3[:])[:, half * 4:half * 4 + 4, :], in_=psv(b, 4, nt), func=AF.Copy), reads=[("ps", b)], writes=["xdt"])
            t1 = v8(T1); t2 = v8(T2); t4 = v8(T4); t5 = v8(T5); t6 = v8(T6); t7 = v8(T7)
            P.op("pool", lambda e: e.tensor_scalar(out=t1, in0=t1, scalar1=-0.6065306597126334, scalar2=None, op0=ALU.mult), reads=["BIGA"], writes=["BIGA"])
            P.op("dve", lambda e: e.tensor_tensor(out=t4, in0=kT, in1=bc3(CF[:, 134:142], nt), op=ALU.mult), reads=["RWS", "CF"], writes=["BIGB"])
            P.op("pool", lambda e: e.tensor_tensor(out=t7, in0=t4, in1=t4, op=ALU.mult), reads=["BIGB"], writes=["BIGC"])
            for half in range(2):
                b = bank()
                for q in range(4):
                    c = half * 4 + q
                    P.op("pe", lambda e, q=q, c=c, b=b: e.matmul(ps[b][:, q * 128:q * 128 + nt], lhsT=BONES, rhs=t7[:, c, :], start=True, stop=True),
                         reads=["BIGC", "CM"], writes=[("ps", b)])
                P.op("dve", lambda e, b=b, half=half: e.tensor_scalar(out=t5[:, half * 4:half * 4 + 4, :], in0=psv(b, 4, nt), scalar1=1e-24, scalar2=None, op0=ALU.max),
                     reads=[("ps", b)], writes=["BIGC"])
            P.op("act", lambda e: e.activation(out=t5, in_=t5, func=AF.Sqrt), reads=["BIGC"], writes=["BIGC"])
            P.op("dve", lambda e: e.reciprocal(out=t5, in_=t5), reads=["BIGC"], writes=["BIGC"])
            P.op("dve", lambda e: e.tensor_tensor(out=t4, in0=t4, in1=t5, op=ALU.mult), reads=["BIGB", "BIGC"], writes=["BIGB"])
            P.op("dve", lambda e: e.scalar_tensor_tensor(out=t7, in0=t2, scalar=1.0, in1=bc3(CF[:, 142:150], nt), op0=ALU.subtract, op1=ALU.mult),
                 reads=["BIGB", "CF"], writes=["BIGC"])
            P.op("dve", lambda e: e.scalar_tensor_tensor(out=kT, in0=t7, scalar=1.0, in1=kT, op0=ALU.add, op1=ALU.mult), reads=["BIGC", "RWS"], writes=["RWS"])
            P.op("dve", lambda e: e.tensor_tensor(out=t2, in0=t4, in1=t2, op=ALU.mult), reads=["BIGB"], writes=["BIGB"])
            P.op("dve", lambda e: e.tensor_tensor(out=t7, in0=rT, in1=kT, op=ALU.mult), reads=["RWS"], writes=["BIGC"])
            P.op("pool", lambda e: e.tensor_tensor(out=t7, in0=t7, in1=bc3(CF[:, 150:158], nt), op=ALU.mult), reads=["BIGC", "CF"], writes=["BIGC"])
            for half in range(2):
                b = bank()
                for q in range(4):
                    c = half * 4 + q
                    P.op("pe", lambda e, q=q, c=c, b=b: e.matmul(ps[b][:, q * 128:q * 128 + nt], lhsT=BONES, rhs=t7[:, c, :], start=True, stop=True),
                         reads=["BIGC", "CM"], writes=[("ps", b)])
                P.op("dve", lambda e, b=b, half=half: e.tensor_tensor(out=t5[:, half * 4:half * 4 + 4, :], in0=psv(b, 4, nt), in1=vT[:, half * 4:half * 4 + 4, :], op=ALU.mult),
                     reads=[("ps", b), "RWS"], writes=["BIGC"])
            return rT, kT, vT, t1, t2, t4, t5, t6, t7

        def rwkv_post(nt):
            t5 = T5.rearrange("p (c t) -> p c t", c=8)[:, :, :nt]
            y3 = Ys[:nt, :].rearrange("p (h v) -> p h v", h=16)
            P.op("dve", lambda e: e.tensor_reduce(out=sm[:nt, 0:16], in_=y3, axis=AX.X, op=ALU.add), reads=["BIGA"], writes=["sm"])
            P.op("dve", lambda e: e.tensor_scalar(out=sm[:nt, 0:16], in0=sm[:nt, 0:16], scalar1=1.0 / 64, scalar2=None, op0=ALU.mult), reads=["sm"], writes=["sm"])
            P.op("dve", lambda e: e.tensor_tensor(out=y3, in0=y3, in1=sm[:nt, 0:16].unsqueeze(2).to_broadcast([nt, 16, 64]), op=ALU.subtract), reads=["sm", "BIGA"], writes=["BIGA"])
            RH3 = RHSs[:nt, :].rearrange("p (h v) -> p h v", h=16)
            P.op("pool", lambda e: e.tensor_tensor(out=RH3, in0=y3, in1=y3, op=ALU.mult), reads=["BIGA"], writes=["xtok"])
            P.op("dve", lambda e: e.tensor_reduce(out=sm[:nt, 16:32], in_=RH3, axis=AX.X, op=ALU.add), reads=["xtok"], writes=["sm"])
            P.op("act", lambda e: e.activation(out=sm[:nt, 16:32], in_=sm[:nt, 16:32], func=AF.Sqrt, scale=1.0 / 64, bias=64e-5), reads=["sm"], writes=["sm"])
            P.op("dve", lambda e: e.reciprocal(out=sm[:nt, 16:32], in_=sm[:nt, 16:32]), reads=["sm"], writes=["sm"])
            P.op("dve", lambda e: e.tensor_tensor(out=y3, in0=y3, in1=sm[:nt, 16:32].unsqueeze(2).to_broadcast([nt, 16, 64]), op=ALU.mult), reads=["sm", "BIGA"], writes=["BIGA"])
            YT = RHSs.rearrange("p (c t) -> p c t", c=8)[:, :, :nt]
            for half in range(2):
                b = bank()
                for q in range(4):
                    c = half * 4 + q
                    P.op("pe", lambda e, q=q, c=c, b=b: e.transpose(out=ps[b][:, q * 128:q * 128 + nt], in_=Ys[:nt, c * 128:(c + 1) * 128], identity=ident[:nt, :nt]),
                         reads=["BIGA", "CM"], writes=[("ps", b)])
                yv = YT[:, half * 4:half * 4 + 4, :]; cs = slice(half * 4, half * 4 + 4)
                P.op("dve", lambda e, b=b, yv=yv, half=half: e.tensor_tensor(out=yv, in0=psv(b, 4, nt), in1=bc3(CF[:, 160 + half * 4:164 + half * 4], nt), op=ALU.mult),
                     reads=[("ps", b), "CF"], writes=["xtok"])
                P.op("dve", lambda e, yv=yv, half=half: e.tensor_tensor(out=yv, in0=yv, in1=bc3(CF[:, 168 + half * 4:172 + half * 4], nt), op=ALU.add), reads=["xtok", "CF"], writes=["xtok"])
                P.op("dve", lambda e, yv=yv, cs=cs: e.tensor_tensor(out=yv, in0=yv, in1=t5[:, cs, :], op=ALU.add), reads=["xtok", "BIGC"], writes=["xtok"])
                P.op("dve", lambda e, yv=yv, cs=cs: e.tensor_tensor(out=yv, in0=yv, in1=T3[:].rearrange("p (c t) -> p c t", c=8)[:, cs, :nt], op=ALU.mult), reads=["xtok", "xdt"], writes=["xtok"])
                P.op("dve", lambda e, yv=yv, half=half: e.tensor_tensor(out=yv, in0=yv, in1=gateT[:, 8 + half * 4:12 + half * 4, :nt], op=ALU.mult), reads=["xtok", "gateT"], writes=["xtok"])
                P.op("dve", lambda e, yv=yv, cs=cs: e.tensor_tensor(out=mgT[:, cs, :nt], in0=mgT[:, cs, :nt], in1=yv, op=ALU.add), reads=["xtok", "mgT"], writes=["mgT"])

        def mixer_out(nt):
            xres, xk = CUR[0], CUR[1]
            for blk in range(2):
                proj_tok(w_out, blk * 512, 512, mgT, "mgT", nt,
                         lambda b, n, blk=blk: P.op("dve", lambda e: e.tensor_tensor(out=xres[:nt, blk * 512:(blk + 1) * 512], in0=xres[:nt, blk * 512:(blk + 1) * 512],
                                                                                     in1=ps[b][:nt, :512], op=ALU.add), reads=[("ps", b), xk], writes=[xk]))

        def rwkv_chunk():
            nt = 128
            rT, kT, vT, t1, t2, t4, t5, t6, t7 = rwkv_pre(128, rwT[:, :, 0:128], rwT[:, :, 1:129])
            if SUB < 2:
                return
            for c in range(8):
                P.op("dve", lambda e, c=c: e.tensor_tensor_scan(out=T6[:, c * 128:(c + 1) * 128], data0=ONES, data1=T1[:, c * 128:(c + 1) * 128], initial=0.0,
                                                                op0=ALU.mult, op1=ALU.add), reads=["BIGA", "CM"], writes=["BIGA"])
            P.op("act", lambda e: e.activation(out=t7, in_=t6, func=AF.Exp), reads=["BIGA"], writes=["BIGC"])
            P.op("dve", lambda e: e.tensor_tensor(out=RA4[:, :, 0, :], in0=rT, in1=t7, op=ALU.mult), reads=["RWS", "BIGC"], writes=["RA"])
            P.op("dve", lambda e: e.tensor_tensor(out=t7, in0=t6, in1=t1, op=ALU.subtract), reads=["BIGA", "BIGC"], writes=["BIGC"])
            P.op("act", lambda e: e.activation(out=t7, in_=t7, func=AF.Exp), reads=["BIGC"], writes=["BIGC"])
            P.op("dve", lambda e: e.scalar_tensor_tensor(out=RA4[:, :, 1, :], in0=t4, scalar=-1.0, in1=t7, op0=ALU.mult, op1=ALU.mult), reads=["BIGB", "BIGC"], writes=["RA"])
            P.op("act", lambda e: e.activation(out=t7, in_=t6, func=AF.Exp, scale=-1.0), reads=["BIGA", "RA"], writes=["BIGC"])
            P.op("dve", lambda e: e.tensor_tensor(out=BK4[:, :, 0, :], in0=t2, in1=t7, op=ALU.mult), reads=["BIGB", "BIGC"], writes=["BK"])
            P.op("dve", lambda e: e.tensor_tensor(out=BK4[:, :, 1, :], in0=kT, in1=t7, op=ALU.mult), reads=["RWS", "BIGC"], writes=["BK"])
            P.op("dve", lambda e: e.tensor_tensor(out=t7, in0=t6, in1=t6[:, :, 127:128].to_broadcast([128, 8, 128]), op=ALU.subtract), reads=["BIGA", "BK"], writes=["BIGC"])
            P.op("act", lambda e: e.activation(out=t7, in_=t7, func=AF.Exp, scale=-1.0), reads=["BIGC"], writes=["BIGC"])
            P.op("dve", lambda e: e.tensor_tensor(out=BKH4[:, :, 0, :], in0=t2, in1=t7, op=ALU.mult), reads=["BIGB", "BIGC"], writes=["BKH"])
            P.op("dve", lambda e: e.tensor_tensor(out=BKH4[:, :, 1, :], in0=kT, in1=t7, op=ALU.mult), reads=["RWS", "BIGC"], writes=["BKH"])
            P.op("act", lambda e: e.activation(out=smw[:, 0:8], in_=t6[:, :, 127], func=AF.Exp), reads=["BIGA"], writes=["smw"])
            if SUB < 3:
                return
            for (src_fn, dst, dkey, skey) in ((lambda c: RWS3[:, 16 + c, :], vtok, "xdd", "RWS"), (lambda c: BKH4[:, c, 0, :], btok, "ysb", "BKH"),
                                               (lambda c: BKH4[:, c, 1, :], ktok, "junk", "BKH")):
                for half in range(2):
                    b = bank()
                    for q in range(4):
                        c = half * 4 + q
                        P.op("pe", lambda e, q=q, c=c, b=b, src_fn=src_fn: e.transpose(out=ps[b][:, q * 128:(q + 1) * 128], in_=src_fn(c), identity=ident),
                             reads=[skey, "CM"], writes=[("ps", b)])
                    P.op("act", lambda e, b=b, half=half, dst=dst: e.activation(out=dst[:, half * 512:(half + 1) * 512], in_=ps[b][:, :512], func=AF.Copy),
                         reads=[("ps", b)], writes=[dkey])
            if SUB < 4:
                return
            for qd in range(4):
                hs = [4 * qd + i for i in range(4)]
                for hh, h in enumerate(hs):
                    j = h // 2; r0 = (h % 2) * 64
                    b = bank()
                    P.op("pe", lambda e, b=b, j=j, r0=r0: e.matmul(ps[b][:, 0:256], lhsT=BK4[r0:r0 + 64, j, 0, :], rhs=RA[r0:r0 + 64, j * 256:(j + 1) * 256], start=True, stop=True),
                         reads=["BK", "RA"], writes=[("ps", b)])
                    P.op("pe", lambda e, b=b, j=j, r0=r0: e.matmul(ps[b][:, 256:512], lhsT=BK4[r0:r0 + 64, j, 1, :], rhs=RA[r0:r0 + 64, j * 256:(j + 1) * 256], start=True, stop=True),
                         reads=["BK", "RA"], writes=[("ps", b)])
                    P.op("dve", lambda e, b=b, hh=hh: e.tensor_tensor(out=ATq[:, hh, :, :], in0=ps[b][:, :512].rearrange("p (k t) -> p k t", k=4), in1=MASK4[:], op=ALU.mult),
                         reads=[("ps", b), "MASK4"], writes=["RWS"])
                if SUB2 < 2:
                    continue
                bP = [bank(), bank()]
                for hh, h in enumerate(hs):
                    j = h // 2; r0 = (h % 2) * 64; b = bP[hh % 2]; o = (hh // 2) * 128
                    P.op("pe", lambda e, b=b, j=j, r0=r0, o=o: e.matmul(ps[b][:, o:o + 128], lhsT=RA4[r0:r0 + 64, j, 1, :], rhs=BK4[r0:r0 + 64, j, 0, :], start=True, stop=True),
                         reads=["BK", "RA"], writes=[("ps", b)])
                MN0v = MN[0].rearrange("p (a two t) -> p a two t", two=2, t=128)
                for par in range(2):
                    P.op("dve", lambda e, par=par, bP=bP, MN0v=MN0v: e.tensor_tensor(out=MN0v[:, :, par, :], in0=v3(ps[bP[par]][:, :256], 2), in1=M1.unsqueeze(1).to_broadcast([128, 2, 128]), op=ALU.mult),
                         reads=[("ps", bP[par]), "CM"], writes=["RWS"])
                if os.environ.get("K_DBG") and qd < 2:
                    P.dma("sp", lambda e, qd=qd: e.dma_start(out=y_p[1280 + qd * 128:1408 + qd * 128, 0:512], in_=MN[0]), reads=["RWS"])
                if SUB3 >= 2:
                    P.op("dve", lambda e: e.tensor_tensor(out=v3(TT[0], 4), in0=ATq[:, :, 1, :], in1=ident.unsqueeze(1).to_broadcast([128, 4, 128]), op=ALU.add),
                         reads=["RWS", "CM"], writes=["convT"])
                mi = 0; ni = None; ti = 0
                if SUB2 < 3:
                    continue
                for lvl in range(6):
                    bM = bank()
                    for hh in range(4):
                        nl = ATq[:, hh, 1, :] if ni is None else NN[ni][:, hh * 128:(hh + 1) * 128]
                        P.op("pe", lambda e, bM=bM, hh=hh, nl=nl, mi=mi: e.matmul(ps[bM][:, hh * 128:(hh + 1) * 128], lhsT=nl, rhs=MN[mi][:, hh * 128:(hh + 1) * 128], start=True, stop=True),
                             reads=["RWS", "xn", "RWS"], writes=[("ps", bM)])
                    if lvl < 5:
                        bN = bank()
                        for hh in range(4):
                            nl = ATq[:, hh, 1, :] if ni is None else NN[ni][:, hh * 128:(hh + 1) * 128]
                            P.op("pe", lambda e, bN=bN, hh=hh, nl=nl, mi=mi: e.matmul(ps[bN][:, hh * 128:(hh + 1) * 128], lhsT=MN[mi][:, hh * 128:(hh + 1) * 128], rhs=nl, start=True, stop=True),
                                 reads=["RWS", "xn", "RWS"], writes=[("ps", bN)])
                        nn = 0 if ni is None else 1 - ni
                        P.op("dve", lambda e, bN=bN, nn=nn: e.tensor_copy(out=NN[nn], in_=ps[bN][:, :512]), reads=[("ps", bN)], writes=["xn"])
                        ni = nn
                    mn = 1 - mi
                    P.op("act", lambda e, bM=bM, mn=mn: e.activation(out=MN[mn], in_=ps[bM][:, :512], func=AF.Copy), reads=[("ps", bM)], writes=["RWS"])
                    mi = mn
                    bT = bank()
                    for hh in range(4):
                        P.op("pe", lambda e, bT=bT, hh=hh, mi=mi, ti=ti: e.matmul(ps[bT][:, hh * 128:(hh + 1) * 128], lhsT=MN[mi][:, hh * 128:(hh + 1) * 128],
                                                                                   rhs=TT[ti][:, hh * 128:(hh + 1) * 128], start=True, stop=True),
                             reads=["RWS", "convT"], writes=[("ps", bT)])
                    tn = 1 - ti
                    P.op("dve", lambda e, bT=bT, ti=ti, tn=tn: e.tensor_tensor(out=TT[tn], in0=TT[ti], in1=ps[bT][:, :512], op=ALU.add),
                         reads=[("ps", bT), "convT"], writes=["convT"])
                    ti = tn
                if SUB2 < 4:
                    continue
                if os.environ.get("K_DBG") and qd < 2:
                    P.dma("sp", lambda e, qd=qd, ti=ti: e.dma_start(out=y_p[1280 + qd * 128:1408 + qd * 128, 512:1024], in_=TT[ti]), reads=["convT"])
                    P.dma("sp", lambda e, qd=qd: e.dma_start(out=y_p[1536 + qd * 256:1664 + qd * 256, :], in_=RWS[:, 0:1024]), reads=["RWS"])
                    P.dma("sp", lambda e, qd=qd: e.dma_start(out=y_p[1664 + qd * 256:1792 + qd * 256, :], in_=RWS[:, 1024:2048]), reads=["RWS"])
                bR = [bank(), bank()]
                for hh, h in enumerate(hs):
                    j = h // 2; r0 = (h % 2) * 64; b = bR[hh % 2]; o = (hh // 2) * 64
                    P.op("pe", lambda e, b=b, o=o, j=j, r0=r0: e.matmul(ps[b][:, o:o + 64], lhsT=RA4[r0:r0 + 64, j, 1, :], rhs=Hst[r0:r0 + 64, j, :], start=True, stop=False),
                         reads=["RA", "Hst"], writes=[("ps", b)])
                    P.op("pe", lambda e, b=b, o=o, hh=hh, h=h: e.matmul(ps[b][:, o:o + 64], lhsT=ATq[:, hh, 3, :], rhs=vtok[:, h * 64:(h + 1) * 64], start=False, stop=True),
                         reads=["RWS", "xdd"], writes=[("ps", b)])
                for par in range(2):
                    P.op("act", lambda e, par=par, qd=qd, bR=bR: e.activation(out=RHSs[:, qd * 256:(qd + 1) * 256].rearrange("p (a two v) -> p a two v", two=2, v=64)[:, :, par, :],
                                                                       in_=v3(ps[bR[par]][:, :128], 2), func=AF.Copy), reads=[("ps", bR[par])], writes=["xtok"])
                if SUB2 < 5:
                    continue
                bU = bank()
                for hh, h in enumerate(hs):
                    P.op("pe", lambda e, bU=bU, hh=hh, h=h, ti=ti: e.matmul(ps[bU][:, hh * 64:(hh + 1) * 64], lhsT=TT[ti][:, hh * 128:(hh + 1) * 128], rhs=RHSs[:, h * 64:(h + 1) * 64], start=True, stop=True),
                         reads=["convT", "xtok"], writes=[("ps", bU)])
                P.op("act", lambda e, bU=bU, qd=qd: e.activation(out=Us[:, qd * 256:(qd + 1) * 256], in_=ps[bU][:, :256], func=AF.Copy), reads=[("ps", bU)], writes=["BIGA"])
                bY = [bank(), bank()]
                for hh, h in enumerate(hs):
                    j = h // 2; r0 = (h % 2) * 64; b = bY[hh % 2]; o = (hh // 2) * 64
                    P.op("pe", lambda e, b=b, o=o, j=j, r0=r0: e.matmul(ps[b][:, o:o + 64], lhsT=RA4[r0:r0 + 64, j, 0, :], rhs=Hst[r0:r0 + 64, j, :], start=True, stop=False),
                         reads=["RA", "Hst"], writes=[("ps", b)])
                    P.op("pe", lambda e, b=b, o=o, hh=hh, h=h: e.matmul(ps[b][:, o:o + 64], lhsT=ATq[:, hh, 0, :], rhs=Us[:, h * 64:(h + 1) * 64], start=False, stop=False),
                         reads=["RWS", "BIGA"], writes=[("ps", b)])
                    P.op("pe", lambda e, b=b, o=o, hh=hh, h=h: e.matmul(ps[b][:, o:o + 64], lhsT=ATq[:, hh, 2, :], rhs=vtok[:, h * 64:(h + 1) * 64], start=False, stop=True),
                         reads=["RWS", "xdd"], writes=[("ps", b)])
                for par in range(2):
                    P.op("act", lambda e, par=par, qd=qd, bY=bY: e.activation(out=Ys[:, qd * 256:(qd + 1) * 256].rearrange("p (a two v) -> p a two v", two=2, v=64)[:, :, par, :],
                                                                       in_=v3(ps[bY[par]][:, :128], 2), func=AF.Copy), reads=[("ps", bY[par])], writes=["BIGA"])
            if SUB < 5:
                return
            for half in range(2):
                b = bank()
                for q in range(4):
                    j = half * 4 + q
                    P.op("pe", lambda e, b=b, q=q, j=j: e.matmul(ps[b][:, q * 128:(q + 1) * 128], lhsT=btok[:, j * 128:(j + 1) * 128], rhs=Us[:, j * 128:(j + 1) * 128], start=True, stop=False),
                         reads=["ysb", "BIGA"], writes=[("ps", b)])
                    P.op("pe", lambda e, b=b, q=q, j=j: e.matmul(ps[b][:, q * 128:(q + 1) * 128], lhsT=ktok[:, j * 128:(j + 1) * 128], rhs=vtok[:, j * 128:(j + 1) * 128], start=False, stop=True),
                         reads=["junk", "xdd"], writes=[("ps", b)])
                for hp in range(2):
                    r0 = hp * 64
                    hv = Hst[r0:r0 + 64, half * 4:half * 4 + 4, :]
                    P.op("dve", lambda e, r0=r0, half=half, hv=hv: e.tensor_tensor(out=hv, in0=hv, in1=smw[r0:r0 + 64, half * 4:half * 4 + 4].unsqueeze(2).to_broadcast([64, 4, 64]), op=ALU.mult),
                         reads=["Hst", "smw"], writes=["Hst"])
                    P.op("dve", lambda e, b=b, r0=r0, hp=hp, hv=hv: e.tensor_tensor(out=hv, in0=hv, in1=v3(ps[b][r0:r0 + 64, :512], 4)[:, :, hp * 64:(hp + 1) * 64], op=ALU.add),
                         reads=[("ps", b), "Hst"], writes=["Hst"])
            if os.environ.get("K_DBG"):
                P.dma("sp", lambda e: e.dma_start(out=y_p[0:128, :], in_=RHSs), reads=["xtok"])
                P.dma("sp", lambda e: e.dma_start(out=y_p[128:256, :], in_=Us), reads=["BIGA"])
                P.dma("sp", lambda e: e.dma_start(out=y_p[256:384, :], in_=Ys), reads=["BIGA"])
                P.dma("sp", lambda e: e.dma_start(out=y_p[384:512, :], in_=vtok[:]), reads=["xdd"])
                P.dma("sp", lambda e: e.dma_start(out=y_p[512:640, :], in_=btok[:]), reads=["ysb"])
                P.dma("sp", lambda e: e.dma_start(out=y_p[640:768, :], in_=ktok[:]), reads=["junk"])
                P.dma("sp", lambda e: e.dma_start(out=y_p[768:896, :], in_=RA[:, 0:1024]), reads=["RA"])
                P.dma("sp", lambda e: e.dma_start(out=y_p[896:1024, :], in_=RA[:, 1024:2048]), reads=["RA"])
                P.dma("sp", lambda e: e.dma_start(out=y_p[1024:1152, :], in_=BK[:, 0:1024]), reads=["BK"])
                P.dma("sp", lambda e: e.dma_start(out=y_p[1152:1280, :], in_=BK[:, 1024:2048]), reads=["BK"])
            if SUB < 6:
                return
            rwkv_post(128)


        qT = gateT
        PRB = BIGA
        PTt = BIGB
        OSB = BIGC

        def attn_tail(nt):
            xres, xk = CUR[0], CUR[1]
            oT = OSB[:, 1024:2048].rearrange("p (c t) -> p c t", c=8)
            for half in range(2):
                b = bank()
                for q in range(4):
                    c = half * 4 + q
                    P.op("pe", lambda e, q=q, c=c, b=b: e.transpose(out=ps[b][:, q * 128:q * 128 + nt], in_=OSB[:nt, c * 128:(c + 1) * 128], identity=ident[:nt, :nt]),
                         reads=["BIGC", "CM"], writes=[("ps", b)])
                P.op("act", lambda e, b=b, half=half: e.activation(out=oT[:, half * 4:half * 4 + 4, :nt], in_=psv(b, 4, nt), func=AF.Copy), reads=[("ps", b)], writes=["BIGC"])
            for blk in range(2):
                proj_tok(w_mo, blk * 512, 512, oT, "BIGC", nt,
                         lambda b, n, blk=blk: P.op("dve", lambda e: e.tensor_tensor(out=xres[:nt, blk * 512:(blk + 1) * 512], in0=xres[:nt, blk * 512:(blk + 1) * 512],
                                                                                     in1=ps[b][:nt, :512], op=ALU.add), reads=[("ps", b), xk], writes=[xk]))

        def attn_prompt():
            xres, xk = CUR[0], CUR[1]
            nt = 128
            rmsnorm_T(xres, nt, 8, hT, xk, "hT")
            for blk in range(2):
                proj_fm(w_mq, blk * 512, 4, hT, "hT", nt,
                        lambda b, n, blk=blk: P.op("act", lambda e: e.activation(out=qT[:, blk * 4:blk * 4 + 4, :nt], in_=psv(b, 4, nt), func=AF.Copy), reads=[("ps", b)], writes=["gateT"]))
            sb_ = [bank(), bank()]
            for h in range(4):
                b = sb_[h // 2]; o = (h % 2) * 256
                for dc in range(2):
                    P.op("pe", lambda e, h=h, dc=dc, b=b, o=o: e.matmul(ps[b][:nt, o:o + 256], lhsT=qT[:, 2 * h + dc, :nt], rhs=KT[:, 2 * h + dc, :], start=(dc == 0), stop=(dc == 1)),
                         reads=["gateT", "KT"], writes=[("ps", b)])
            for hb in range(2):
                P.op("dve", lambda e, hb=hb, sb_=sb_: e.tensor_reduce(out=sm[:nt, hb * 2:hb * 2 + 2], in_=ps[sb_[hb]][:nt, :512].rearrange("p (h m) -> p h m", h=2), axis=AX.X, op=ALU.max),
                     reads=[("ps", sb_[hb])], writes=["sm"])
            P.op("dve", lambda e: e.tensor_scalar(out=sm[:nt, 4:8], in0=sm[:nt, 0:4], scalar1=-1.0 / 16, scalar2=None, op0=ALU.mult), reads=["sm"], writes=["sm"])
            for h in range(4):
                b = sb_[h // 2]; o = (h % 2) * 256
                P.op("act", lambda e, h=h, b=b, o=o: e.activation(out=PRB[:nt, h * 256:(h + 1) * 256], in_=ps[b][:nt, o:o + 256], func=AF.Exp, scale=1.0 / 16, bias=sm[:nt, 4 + h:5 + h],
                                                                  accum_out=sm[:nt, 8 + h:9 + h]), reads=[("ps", b), "sm"], writes=["BIGA", "sm"])
            P.op("dve", lambda e: e.reciprocal(out=sm[:nt, 12:16], in_=sm[:nt, 8:12]), reads=["sm"], writes=["sm"])
            PT3 = PTt[:].rearrange("p (c t) -> p c t", c=16)
            for half in range(2):
                b = bank()
                for q in range(4):
                    c = half * 4 + q
                    P.op("pe", lambda e, q=q, c=c, b=b: e.transpose(out=ps[b][:, q * 128:q * 128 + nt], in_=PRB[:nt, c * 128:(c + 1) * 128], identity=ident[:nt, :nt]),
                         reads=["BIGA", "CM"], writes=[("ps", b)])
                P.op("act", lambda e, b=b, half=half: e.activation(out=PT3[:, half * 4:half * 4 + 4, :nt], in_=psv(b, 4, nt), func=AF.Copy), reads=[("ps", b)], writes=["BIGB"])
            ob = [bank(), bank()]
            for h in range(4):
                b = ob[h // 2]; o = (h % 2) * 256
                for mb in range(2):
                    P.op("pe", lambda e, h=h, mb=mb, b=b, o=o: e.matmul(ps[b][:nt, o:o + 256], lhsT=PT3[:, h * 2 + mb, :nt], rhs=Vt[:, mb, h * 256:(h + 1) * 256], start=(mb == 0), stop=(mb == 1)),
                         reads=["BIGB", "Vt"], writes=[("ps", b)])
            for h in range(4):
                b = ob[h // 2]; o = (h % 2) * 256
                P.op("act", lambda e, h=h, b=b, o=o: e.activation(out=OSB[:nt, h * 256:(h + 1) * 256], in_=ps[b][:nt, o:o + 256], func=AF.Copy, scale=sm[:nt, 12 + h:13 + h]),
                     reads=[("ps", b), "sm"], writes=["BIGC"])
            attn_tail(nt)


        NEG = -1.0e30

        def fence(keys):
            P.op("pool", lambda e: e.memset(st1[:, 7:8], 0.0), reads=[], writes=list(keys) + ["st1"])

        def top16(nt, vals_in, key_in, nsets, width, Vout, Iout):
            for r in range(2):
                for s_ in range(nsets):
                    src = vals_in[:nt, s_, :]; vo = Vout[:nt, s_, r * 8:(r + 1) * 8]
                    P.op("dve", lambda e, vo=vo, src=src: e.max(out=vo, in_=src), reads=[key_in, ("tk_in", s_), "pk_v"], writes=[("tk_v", s_, r)])
                for s_ in range(nsets):
                    src = vals_in[:nt, s_, :]; vo = Vout[:nt, s_, r * 8:(r + 1) * 8]; io = Iout[:nt, s_, r * 8:(r + 1) * 8]
                    P.op("dve", lambda e, vo=vo, io=io, src=src: e.max_index(out=io, in_max=vo, in_values=src), reads=[key_in, ("tk_in", s_), ("tk_v", s_, r), "pk_i"], writes=[("tk_i", s_, r)])
                if r == 0:
                    for s_ in range(nsets):
                        src = vals_in[:nt, s_, :]; vo = Vout[:nt, s_, 0:8]
                        P.op("dve", lambda e, vo=vo, src=src: e.match_replace(out=src, in_to_replace=vo, in_values=src, imm_value=NEG),
                             reads=[key_in, ("tk_v", s_, 0), ("tk_i", s_, 0)], writes=[("tk_in", s_)])
            fine = [("tk_in", s_) for s_ in range(nsets)] + [("tk_v", s_, r) for s_ in range(nsets) for r in range(2)] + [("tk_i", s_, r) for s_ in range(nsets) for r in range(2)]
            P.op("pool", lambda e: e.memset(st1[:, 6:7], 0.0), reads=fine, writes=fine + [key_in, "pk_v", "pk_i"])

        def peer(nt):
            xres, xk = CUR[0], CUR[1]
            SM = xdd
            V1 = SM[:, 0:256].rearrange("p (s k) -> p s k", s=16); I1 = SM[:, 256:512].bitcast(U32).rearrange("p (s k) -> p s k", s=16)
            I1f = SM[:, 512:768].rearrange("p (s k) -> p s k", s=16)
            TV = SM[:, 768:896].rearrange("p (h k) -> p h k", h=8); TP = SM[:, 896:1024].bitcast(U32).rearrange("p (h k) -> p h k", h=8)
            S2 = ysb
            TPf = S2[:, 0:128].rearrange("p (h k) -> p h k", h=8); Af = S2[:, 128:256].rearrange("p (h k) -> p h k", h=8)
            Bf = S2[:, 256:384].rearrange("p (h k) -> p h k", h=8); E1 = S2[:, 384:512].rearrange("p (h k) -> p h k", h=8)
            E2 = S2[:, 512:640].rearrange("p (h k) -> p h k", h=8); IDX = S2[:, 640:768].bitcast(U32)
            GATE = S2[:, 768:896].rearrange("p (h k) -> p h k", h=8); ACTV = S2[:, 896:1024]
            fence(["xdd", "ysb", "pk_v", "pk_i", "pk_s"])
            rmsnorm_T(xres, nt, 16, hT, xk, "hT")
            H3 = xdt
            for half in range(2):
                b = bank()
                for q in range(4):
                    c = half * 4 + q
                    P.op("pe", lambda e, q=q, c=c, b=b: e.transpose(out=ps[b][:nt, q * 128:(q + 1) * 128], in_=hT[:, c, :nt], identity=ident), reads=["hT", "CM"], writes=[("ps", b)])
                P.op("act", lambda e, b=b, half=half: e.activation(out=H3[:nt, half * 512:(half + 1) * 512], in_=ps[b][:nt, :512], func=AF.Copy), reads=[("ps", b)], writes=["xdt"])
            QT = gateT
            for blk in range(4):
                proj_fm(w_pq, blk * 512, 4, hT, "hT", nt,
                        lambda b, n, blk=blk: P.op("act", lambda e: e.activation(out=QT[:, blk * 4:blk * 4 + 4, :nt], in_=psv(b, 4, nt), func=AF.Copy), reads=[("ps", b)], writes=["gateT"]))
            SKr = BIGA[:].rearrange("p (s d) -> p s d", s=16); SKT = BIGB[:].rearrange("p (s n) -> p s n", s=16)
            P.dma("sp", lambda e: e.dma_start(out=SKr, in_=sub_keys.rearrange("s n d -> n s d")), writes=["BIGA"])
            for grp in range(4):
                b = bank()
                for q in range(4):
                    hc = grp * 4 + q
                    P.op("pe", lambda e, q=q, hc=hc, b=b: e.transpose(out=ps[b][:, q * 128:(q + 1) * 128], in_=SKr[:, hc, :], identity=ident), reads=["BIGA", "CM"], writes=[("ps", b)])
                P.op("act", lambda e, b=b, grp=grp: e.activation(out=SKT[:, grp * 4:grp * 4 + 4, :], in_=psv(b, 4, 128), func=AF.Copy), reads=[("ps", b)], writes=["BIGB"])
            SC = BIGC[:].rearrange("p (s n) -> p s n", s=16)
            for grp in range(4):
                b = bank()
                for q in range(4):
                    hc = grp * 4 + q
                    P.op("pe", lambda e, q=q, hc=hc, b=b: e.matmul(ps[b][:nt, q * 128:(q + 1) * 128], lhsT=QT[:, hc, :nt], rhs=SKT[:, hc, :], start=True, stop=True),
                         reads=["gateT", "BIGB"], writes=[("ps", b)])
                P.op("act", lambda e, b=b, grp=grp: e.activation(out=SC[:nt, grp * 4:grp * 4 + 4, :], in_=ps[b][:nt, :512].rearrange("p (q n) -> p q n", q=4), func=AF.Copy),
                     reads=[("ps", b)], writes=["BIGC"])
            top16(nt, SC, "BIGC", 16, 128, V1, I1)
            P.op("dve", lambda e: e.tensor_copy(out=I1f[:nt], in_=I1[:nt]), reads=["pk_i"], writes=["pk_s"])
            V1h = SM[:, 0:256].rearrange("p (h c k) -> p h c k", h=8, c=2)
            CAND = BIGA[:].rearrange("p (h a b) -> p h a b", h=8, a=16)
            P.op("dve", lambda e: e.tensor_tensor(out=CAND[:nt], in0=V1h[:nt, :, 0, :].unsqueeze(3).to_broadcast([nt, 8, 16, 16]),
                                                  in1=V1h[:nt, :, 1, :].unsqueeze(2).to_broadcast([nt, 8, 16, 16]), op=ALU.add), reads=["pk_v"], writes=["BIGA"])
            top16(nt, BIGA[:].rearrange("p (h c) -> p h c", h=8), "BIGA", 8, 256, TV, TP)
            Au = S2[:, 384:512].bitcast(U32).rearrange("p (h k) -> p h k", h=8); Bu = S2[:, 512:640].bitcast(U32).rearrange("p (h k) -> p h k", h=8)
            P.op("dve", lambda e: e.tensor_scalar(out=Au[:nt], in0=TP[:nt], scalar1=4, scalar2=None, op0=ALU.logical_shift_right), reads=["pk_i"], writes=["pk_s"])
            P.op("dve", lambda e: e.tensor_scalar(out=Bu[:nt], in0=TP[:nt], scalar1=15, scalar2=None, op0=ALU.bitwise_and), reads=["pk_i"], writes=["pk_s"])
            P.op("dve", lambda e: e.tensor_copy(out=Af[:nt], in_=Au[:nt]), reads=["pk_s"], writes=["pk_s"])
            P.op("dve", lambda e: e.tensor_copy(out=Bf[:nt], in_=Bu[:nt]), reads=["pk_s"], writes=["pk_s"])
            I1h = SM[:, 512:768].rearrange("p (h c k) -> p h c k", h=8, c=2)
            OH = BIGB[:].rearrange("p (h k a) -> p h k a", h=8, k=16)
            for (sel, cc, Eo) in ((Af, 0, E1), (Bf, 1, E2)):
                P.op("dve", lambda e, sel=sel: e.tensor_tensor(out=OH[:nt], in0=sel[:nt].unsqueeze(3).to_broadcast([nt, 8, 16, 16]),
                                                               in1=IOTA[:nt].unsqueeze(1).unsqueeze(1).to_broadcast([nt, 8, 16, 16]), op=ALU.is_equal), reads=["pk_s", "CM"], writes=["BIGB"])
                P.op("dve", lambda e, cc=cc: e.tensor_tensor(out=OH[:nt], in0=OH[:nt], in1=I1h[:nt, :, cc, :].unsqueeze(2).to_broadcast([nt, 8, 16, 16]), op=ALU.mult),
                     reads=["BIGB", "pk_s"], writes=["BIGB"])
                P.op("dve", lambda e, Eo=Eo: e.tensor_reduce(out=Eo[:nt], in_=OH[:nt], axis=AX.X, op=ALU.add), reads=["BIGB"], writes=["pk_s"])
            P.op("dve", lambda e: e.scalar_tensor_tensor(out=E1[:nt], in0=E1[:nt], scalar=128.0, in1=E2[:nt], op0=ALU.mult, op1=ALU.add), reads=["pk_s"], writes=["pk_s"])
            P.op("dve", lambda e: e.tensor_copy(out=IDX[:nt, :], in_=S2[:nt, 384:512]), reads=["pk_s"], writes=["pk_idx"])
            P.op("dve", lambda e: e.tensor_tensor(out=GATE[:nt], in0=TV[:nt], in1=TV[:nt, :, 0:1].to_broadcast([nt, 8, 16]), op=ALU.subtract), reads=["pk_v"], writes=["pk_s"])
            P.op("act", lambda e: e.activation(out=GATE[:nt], in_=GATE[:nt], func=AF.Exp), reads=["pk_s"], writes=["pk_s"])
            P.op("dve", lambda e: e.tensor_reduce(out=sm[:nt, 0:8], in_=GATE[:nt], axis=AX.X, op=ALU.add), reads=["pk_s"], writes=["sm"])
            P.op("dve", lambda e: e.reciprocal(out=sm[:nt, 8:16], in_=sm[:nt, 0:8]), reads=["sm"], writes=["sm"])
            P.op("dve", lambda e: e.tensor_tensor(out=GATE[:nt], in0=GATE[:nt], in1=sm[:nt, 8:16].unsqueeze(2).to_broadcast([nt, 8, 16]), op=ALU.mult), reads=["pk_s", "sm"], writes=["pk_s"])
            def apply():
                GB = [(RA[:, 0:1024], ("g", 0)), (RA[:, 1024:2048], ("g", 1)), (BK[:, 0:1024], ("g", 2)), (BK[:, 1024:2048], ("g", 3)),
                      (BKH[:, 0:1024], ("g", 4)), (BKH[:, 1024:2048], ("g", 5))]
                fence(["RA", "BK", "BKH", "BIGA"] + [k for _, k in GB] + [("dg", i_) for i_ in range(16)])
                gi = 0
                for hk in range(128):
                    buf, gk = GB[gi % len(GB)]; gi += 1
                    P.dma("pool", lambda e, buf=buf, hk=hk: e.indirect_dma_start(out=buf[:nt, :], out_offset=None, in_=exp_u,
                                                                                  in_offset=bass.IndirectOffsetOnAxis(ap=IDX[:nt, hk:hk + 1], axis=0)),
                          reads=["pk_idx"], writes=[gk])
                    P.op("dve", lambda e, buf=buf, hk=hk: e.scalar_tensor_tensor(out=buf[:nt, :], in0=buf[:nt, :], scalar=1.0, in1=H3[:nt, :], op0=ALU.mult, op1=ALU.mult,
                                                                                accum_out=ACTV[:nt, hk:hk + 1]), reads=[gk, "xdt"], writes=[gk, "pk_a"])
                COEF = ACTV
                P.op("act", lambda e: e.activation(out=COEF[:nt, :], in_=ACTV[:nt, :], func=AF.Gelu), reads=["pk_a"], writes=["pk_a"])
                P.op("dve", lambda e: e.tensor_tensor(out=COEF[:nt, :], in0=COEF[:nt, :], in1=S2[:nt, 768:896], op=ALU.mult), reads=["pk_a", "pk_s"], writes=["pk_a"])
                ACC = junk
                P.op("pool", lambda e: e.memset(ACC[:nt, :], 0.0), writes=["junk"])
                DGf = BIGA[:].rearrange("p (s t) -> p s t", s=16)
                ab = [bank(), bank()]
                pe_hks = [hk for hk in range(128) if hk % 3 == 2]
                cnt = 0
                for hk in range(128):
                    buf, gk = GB[gi % len(GB)]; gi += 1
                    P.dma("pool", lambda e, buf=buf, hk=hk: e.indirect_dma_start(out=buf[:nt, :], out_offset=None, in_=exp_v,
                                                                                  in_offset=bass.IndirectOffsetOnAxis(ap=IDX[:nt, hk:hk + 1], axis=0)),
                          reads=["pk_idx"], writes=[gk])
                    if hk % 3 == 2:
                        slot = cnt % 16; cnt += 1
                        dsl = DGf[:nt, slot, :nt]; dk = ("dg", slot)
                        P.op("dve", lambda e, dsl=dsl, hk=hk: e.tensor_scalar(out=dsl, in0=ident[:nt, :nt], scalar1=COEF[:nt, hk:hk + 1], scalar2=None, op0=ALU.mult),
                             reads=["CM", "pk_a"], writes=[dk])
                        for half in range(2):
                            P.op("pe", lambda e, dsl=dsl, buf=buf, half=half, hk=hk, ab=ab: e.matmul(ps[ab[half]][:nt, :512], lhsT=dsl, rhs=buf[:nt, half * 512:(half + 1) * 512],
                                                                                                  start=(hk == pe_hks[0]), stop=(hk == pe_hks[-1])),
                                 reads=[gk, dk], writes=[("ps", ab[half])])
                    else:
                        P.op("dve", lambda e, buf=buf, hk=hk: e.scalar_tensor_tensor(out=ACC[:nt, :], in0=buf[:nt, :], scalar=COEF[:nt, hk:hk + 1], in1=ACC[:nt, :], op0=ALU.mult, op1=ALU.add),
                             reads=[gk, "pk_a", "junk"], writes=["junk"])
                P.op("dve", lambda e: e.tensor_tensor(out=xres[:nt, :], in0=xres[:nt, :], in1=ACC[:nt, :], op=ALU.add), reads=["junk", xk], writes=[xk])
                for half in range(2):
                    P.op("dve", lambda e, half=half, ab=ab: e.tensor_tensor(out=xres[:nt, half * 512:(half + 1) * 512], in0=xres[:nt, half * 512:(half + 1) * 512], in1=ps[ab[half]][:nt, :512], op=ALU.add),
                         reads=[("ps", ab[half]), xk], writes=[xk])
                fence(["RA", "BK", "BKH", "BIGA", "xdd", "ysb"] + [("dg", i_) for i_ in range(16)] + [k for _, k in GB] + ["pk_v", "pk_i", "pk_s", "pk_idx", "pk_a"])

            return apply

        def final_out(nt, dst):
            xres, xk = CUR[0], CUR[1]
            P.op("act", lambda e: e.activation(out=junk[:nt, :], in_=xres[:nt, :], func=AF.Square, accum_out=st1[:nt, 0:1]), reads=[xk], writes=["junk", "st1"])
            P.op("act", lambda e: e.activation(out=st1[:nt, 1:2], in_=st1[:nt, 0:1], func=AF.Sqrt, scale=1.0 / D, bias=EPS), reads=["st1"], writes=["st1"])
            P.op("dve", lambda e: e.reciprocal(out=st1[:nt, 2:3], in_=st1[:nt, 1:2]), reads=["st1"], writes=["st1"])
            P.op("dve", lambda e: e.scalar_tensor_tensor(out=xn[:nt, :], in0=xres[:nt, :], scalar=st1[:nt, 2:3], in1=CT[:nt, 48:48 + D], op0=ALU.mult, op1=ALU.mult),
                 reads=["st1", xk, "CT"], writes=["xn"])
            P.dma("sp", lambda e: e.dma_start(out=dst, in_=xn[:nt, :]), reads=["xn"])


        def attn_sample():
            xres, xk = CUR[0], CUR[1]
            nt = NS
            rmsnorm_T(xres, nt, 8, hT, xk, "hT")
            QS = xdt
            for blk in range(2):
                proj_tok(w_mq, blk * 512, 512, hT, "hT", nt,
                         lambda b, n, blk=blk: P.op("act", lambda e: e.activation(out=QS[:nt, blk * 512:(blk + 1) * 512], in_=ps[b][:nt, :512], func=AF.Copy), reads=[("ps", b)], writes=["xdt"]))
            SELa = BK[:16, 0:2048].rearrange("p (s m) -> p s m", s=16)
            P.op("dve", lambda e: e.tensor_copy(out=SELa, in_=ident[:16, :16].unsqueeze(2).to_broadcast([16, 16, 128])), reads=["CM"], writes=["BK"])
            SCs = xdd[:, 0:128]; PRs = xdd[:, 256:512]; PT2 = xdd[:, 512:640]; OH16 = xdd[:, 640:896].rearrange("p (a b) -> p a b", a=16)
            PZ = RA[:].rearrange("p (s mb h t) -> p s mb h t", s=16, mb=2, h=4)
            PR = BIGC
            for si in range(NS):
                Kb, kkey = (BIGA, "BIGA") if si % 2 == 0 else (BIGB, "BIGB")
                P.dma("sp", lambda e, si=si, Kb=Kb: e.dma_start(out=Kb[:].rearrange("p (mb d) -> p mb d", mb=2), in_=ck[si].rearrange("(mb p) d -> p mb d", p=128)), writes=[kkey])
                qb = [bank(), bank()]
                for half in range(2):
                    P.op("pe", lambda e, half=half, qb=qb, si=si: e.matmul(ps[qb[half]][:, :512], lhsT=SELa[:, si, :], rhs=QS[:nt, half * 512:(half + 1) * 512], start=True, stop=True),
                         reads=["BK", "xdt"], writes=[("ps", qb[half])])
                for mb in range(2):
                    for half in range(2):
                        P.op("dve", lambda e, mb=mb, half=half, qb=qb, Kb=Kb: e.tensor_tensor(out=PR[:, mb * 1024 + half * 512:mb * 1024 + (half + 1) * 512],
                                                                                             in0=Kb[:, mb * 1024 + half * 512:mb * 1024 + (half + 1) * 512], in1=ps[qb[half]][:, :512], op=ALU.mult),
                             reads=[kkey, ("ps", qb[half])], writes=["BIGC"])
                P.op("dve", lambda e, si=si: e.tensor_reduce(out=SCs.rearrange("p (mb s h) -> p mb s h", mb=2, s=16)[:, :, si, :], in_=PR[:].rearrange("p (mb h d) -> p mb h d", mb=2, h=4),
                                                             axis=AX.X, op=ALU.add), reads=["BIGC"], writes=["xdd"])
            b = bank()
            for mb in range(2):
                P.op("pe", lambda e, mb=mb, b=b: e.transpose(out=ps[b][:64, mb * 128:(mb + 1) * 128], in_=SCs[:, mb * 64:(mb + 1) * 64], identity=ident), reads=["xdd", "CM"], writes=[("ps", b)])
            P.op("dve", lambda e, b=b: e.tensor_reduce(out=sm[:64, 0:1], in_=ps[b][:64, :256], axis=AX.X, op=ALU.max), reads=[("ps", b)], writes=["sm"])
            P.op("dve", lambda e: e.tensor_scalar(out=sm[:64, 1:2], in0=sm[:64, 0:1], scalar1=-1.0 / 16, scalar2=None, op0=ALU.mult), reads=["sm"], writes=["sm"])
            P.op("act", lambda e, b=b: e.activation(out=PRs[:64, :], in_=ps[b][:64, :256], func=AF.Exp, scale=1.0 / 16, bias=sm[:64, 1:2], accum_out=sm[:64, 2:3]),
                 reads=[("ps", b), "sm"], writes=["xdd", "sm"])
            P.op("dve", lambda e: e.reciprocal(out=sm[:64, 3:4], in_=sm[:64, 2:3]), reads=["sm"], writes=["sm"])
            P.op("dve", lambda e: e.tensor_scalar(out=PRs[:64, :], in0=PRs[:64, :], scalar1=sm[:64, 3:4], scalar2=None, op0=ALU.mult), reads=["sm", "xdd"], writes=["xdd"])
            b = bank()
            for mb in range(2):
                P.op("pe", lambda e, mb=mb, b=b: e.transpose(out=ps[b][:, mb * 64:(mb + 1) * 64], in_=PRs[:64, mb * 128:(mb + 1) * 128], identity=ident[:64, :64]), reads=["xdd", "CM"], writes=[("ps", b)])
            P.op("act", lambda e, b=b: e.activation(out=PT2, in_=ps[b][:, :128], func=AF.Copy), reads=[("ps", b)], writes=["xdd"])
            P.op("dve", lambda e: e.tensor_tensor(out=OH16, in0=IOTA.unsqueeze(2).to_broadcast([128, 16, 16]), in1=IOTA.unsqueeze(1).to_broadcast([128, 16, 16]), op=ALU.is_equal),
                 reads=["CM"], writes=["xdd"])
            PT4 = PT2.rearrange("p (mb s h) -> p mb s h", mb=2, s=16)
            for mb in range(2):
                for h in range(4):
                    P.op("dve", lambda e, mb=mb, h=h: e.tensor_tensor(out=PZ[:, :, mb, h, :], in0=PT4[:, mb, :, h].unsqueeze(1).to_broadcast([128, 16, 16]), in1=OH16, op=ALU.mult),
                         reads=["xdd"], writes=["RA"])
            ob = [bank(), bank(), bank(), bank()]
            for si in range(NS):
                Vb, vkey = (BIGA, "BIGA") if si % 2 == 0 else (BIGB, "BIGB")
                P.dma("sp", lambda e, si=si, Vb=Vb: e.dma_start(out=Vb[:].rearrange("p (mb d) -> p mb d", mb=2), in_=cv[si].rearrange("(mb p) d -> p mb d", p=128)), writes=[vkey])
                for h in range(4):
                    for mb in range(2):
                        P.op("pe", lambda e, si=si, h=h, mb=mb, ob=ob, Vb=Vb: e.matmul(ps[ob[h]][:nt, 0:256], lhsT=PZ[:, si, mb, h, :],
                                                                                       rhs=Vb[:, mb * 1024 + h * 256:mb * 1024 + (h + 1) * 256],
                                                                                       start=(si == 0 and mb == 0), stop=(si == NS - 1 and mb == 1)),
                             reads=["RA", vkey], writes=[("ps", ob[h])])
            for h in range(4):
                P.op("act", lambda e, h=h, ob=ob: e.activation(out=OSB[:nt, h * 256:(h + 1) * 256], in_=ps[ob[h]][:nt, 0:256], func=AF.Copy), reads=[("ps", ob[h])], writes=["BIGC"])
            attn_tail(nt)


        def store_T(src_fn, nchunks, nrows, dst, skey):
            done = 0
            for (stg, stkey, cap) in ((BIGA, "BIGA", 16), (BIGB, "BIGB", 16)):
                n_here = min(cap, nchunks - done)
                if n_here <= 0:
                    break
                for g in range(0, n_here, 4):
                    n = min(4, n_here - g)
                    b = bank()
                    for q in range(n):
                        c = done + g + q
                        P.op("pe", lambda e, q=q, c=c, b=b: e.transpose(out=ps[b][:nrows, q * 128:(q + 1) * 128], in_=src_fn(c), identity=ident), reads=[skey, "CM"], writes=[("ps", b)])
                    P.op("act", lambda e, b=b, g=g, n=n, stg=stg: e.activation(out=stg[:nrows, g * 128:(g + n) * 128], in_=ps[b][:nrows, :n * 128], func=AF.Copy), reads=[("ps", b)], writes=[stkey])
                P.dma("sp", lambda e, stg=stg, done=done, n_here=n_here: e.dma_start(out=dst[:, done * 128:(done + n_here) * 128], in_=stg[:nrows, 0:n_here * 128]), reads=[stkey])
                done += n_here

        mT = BIGA[:].rearrange("p (c t) -> p c t", c=8)
        KT = sb("KT", [128, 8, 256]); Vt = sb("Vt", [128, 2, D])
        osb = sb("osb", [128, 512])
        for mb in range(2):
            P.dma("sp", lambda e, mb=mb: e.dma_start(out=xres[:], in_=memp[mb * 128:(mb + 1) * 128, :]), writes=["xres"])
            tmpT = hT
            rmsnorm_T(xres, 128, 24, tmpT, "xres", "hT")
            P.op("dve", lambda e, mb=mb: e.tensor_copy(out=mT[:, :, mb * 128:(mb + 1) * 128], in_=hT[:]), reads=["hT"], writes=["BIGA"])
        for (wd, od) in ((w_mk, mk_p), (w_mv, mv_p)):
            for blk in range(2):
                i = load_w(wd, blk * 512, 512)
                for mb in range(2):
                    b = bank()
                    for kc in range(8):
                        P.op("pe", lambda e, kc=kc, mb=mb, b=b, i=i: e.matmul(ps[b][:, :512], lhsT=mT[:, kc, mb * 128:(mb + 1) * 128], rhs=WB[i][:, kc, :],
                                                                               start=(kc == 0), stop=(kc == 7)),
                             reads=[("wb", i), "BIGA"], writes=[("ps", b)])
                    P.op("act", lambda e, b=b: e.activation(out=osb[:], in_=ps[b][:, :512], func=AF.Copy), reads=[("ps", b)], writes=["osb"])
                    if wd is w_mv:
                        P.op("pool", lambda e, mb=mb, blk=blk: e.tensor_copy(out=Vt[:, mb, blk * 512:(blk + 1) * 512], in_=osb[:]), reads=["osb"], writes=["Vt"])
                    P.dma("sp", lambda e, od=od, mb=mb, blk=blk: e.dma_start(out=od[mb * 128:(mb + 1) * 128, blk * 512:(blk + 1) * 512], in_=osb[:]),
                          reads=["osb"])

        for blk in range(0 if os.environ.get("K_NOKT") else 2):
            i = load_w(w_mk, blk * 512, 512)
            for half in range(2):
                b = bank()
                for q in range(2):
                    cc = half * 2 + q
                    for kc in range(8):
                        P.op("pe", lambda e, kc=kc, q=q, cc=cc, b=b, i=i: e.matmul(ps[b][:, q * 256:(q + 1) * 256], lhsT=WB[i][:, kc, cc * 128:(cc + 1) * 128], rhs=mT[:, kc, :],
                                                                                   start=(kc == 0), stop=(kc == 7)), reads=[("wb", i), "BIGA"], writes=[("ps", b)])
                P.op("act", lambda e, b=b, blk=blk, half=half: e.activation(out=KT[:, blk * 4 + half * 2:blk * 4 + half * 2 + 2, :], in_=ps[b][:, :512].rearrange("p (q m) -> p q m", q=2), func=AF.Copy),
                     reads=[("ps", b)], writes=["KT"])

        P.op("pool", lambda e: e.memset(xbcT[:, :, 0:3], 0.0), writes=["xbcT"])
        P.op("pool", lambda e: e.memset(rwT[:, :, 0:1], 0.0), writes=["rwT"])
        P.op("pool", lambda e: e.memset(stT[:], 0.0), writes=["stT"])
        nch = int(os.environ.get('K_NCH', NCH))
        XRS = [[xres, "xres"], [xres2, "xres2"]]

        def stage_a(ch):
            xr, xk_ = CUR[0], CUR[1]
            P.dma("sp", lambda e, ch=ch, xr=xr: e.dma_start(out=xr[:], in_=xp[ch * 128:(ch + 1) * 128, :]), writes=[xk_])
            rmsnorm_T(xr, 128, 0, hT, xk_, "hT")
            in_proj(128, 3, 1)

        CUR[:] = XRS[0]
        stage_a(0)
        for ch in range(nch):
            mine = XRS[ch % 2]; other = XRS[(ch + 1) % 2]
            CUR[:] = mine
            ssd_chunk()
            rwkv_chunk()
            mixer_out(128)
            P.op("dve", lambda e: e.tensor_copy(out=xbcT[:, :, 0:3], in_=xbcT[:, :, 128:131]), reads=["xbcT"], writes=["xbcT"])
            P.op("dve", lambda e: e.tensor_copy(out=rwT[:, :, 0:1], in_=rwT[:, :, 128:129]), reads=["rwT"], writes=["rwT"])
            attn_prompt()
            ap = peer(128)
            if ch + 1 < nch:
                CUR[:] = other
                stage_a(ch + 1)
                CUR[:] = mine
            ap()
            final_out(128, y_p[ch * 128:(ch + 1) * 128, :])
        CUR[:] = XRS[0]
        store_T(lambda c: xbcT[:, c, 0:3], 12, 3, conv_p, "xbcT")
        store_T(lambda c: rwT[:, c, 0:1], 26, 1, shift_p, "rwT")


        if stage >= 4:
            SCT = BKH[:, 0:576].rearrange("p (c t) -> p c t", c=12)
            YST = convT[:, 0:8, 16:32]; YRT = convT[:, 0:8, 32:48]
            SEL = BK[:16, 0:2048].rearrange("p (s m) -> p s m", s=16)
            P.dma("sp", lambda e: e.dma_start(out=xres[:16, :], in_=xs_in), writes=["xres"])
            rmsnorm_T(xres, 16, 0, hT, "xres", "hT")
            in_proj(16, 3, 1)
            P.dma("sp", lambda e: e.dma_start(out=conv_s[:, 0:2, :], in_=st_conv[:, 1:3, :]))
            store_T(lambda c: xbcT[:, c, 3:19], 12, NS, conv_s[:, 2, :], "xbcT")
            store_T(lambda c: rwT[:, c, 1:17], 26, NS, shift_s, "rwT")
            P.dma("sp", lambda e: e.dma_start(out=ysb[:48, :], in_=st_conv.rearrange("s k c -> (s k) c")[:, 0:1024]), writes=["ysb"])
            P.dma("sp", lambda e: e.dma_start(out=xdd[:48, 0:512], in_=st_conv.rearrange("s k c -> (s k) c")[:, 1024:1536]), writes=["xdd"])
            for grp in range(3):
                b = bank()
                for q in range(4):
                    c = grp * 4 + q
                    src = ysb[:48, c * 128:(c + 1) * 128] if c < 8 else xdd[:48, (c - 8) * 128:(c - 7) * 128]
                    P.op("pe", lambda e, q=q, b=b, src=src: e.transpose(out=ps[b][:, q * 48:(q + 1) * 48], in_=src, identity=ident[:48, :48]),
                         reads=["ysb", "xdd", "CM"], writes=[("ps", b)])
                P.op("act", lambda e, b=b, grp=grp: e.activation(out=SCT[:, grp * 4:(grp + 1) * 4, :], in_=ps[b][:, :192].rearrange("p (q t) -> p q t", q=4), func=AF.Copy),
                     reads=[("ps", b)], writes=["BKH"])
            A = BIGA[:, :192].rearrange("p (c t) -> p c t", c=12); Bv = BIGB[:, :192].rearrange("p (c t) -> p c t", c=12)
            SC4 = SCT.rearrange("p c (s k) -> p c s k", k=3)
            P.op("dve", lambda e: e.tensor_tensor(out=A, in0=xbcT[:, :, 3:19], in1=CWv[:, :, 3:4].to_broadcast([128, 12, 16]), op=ALU.mult), reads=["xbcT", "CF"], writes=["BIGA"])
            for k in range(3):
                P.op("dve", lambda e, k=k: e.tensor_tensor(out=Bv, in0=SC4[:, :, :, k], in1=CWv[:, :, k:k + 1].to_broadcast([128, 12, 16]), op=ALU.mult), reads=["BKH", "CF"], writes=["BIGB"])
                P.op("dve", lambda e: e.tensor_tensor(out=A, in0=A, in1=Bv, op=ALU.add), reads=["BIGA", "BIGB"], writes=["BIGA"])
            for c in range(12):
                P.op("act", lambda e, c=c: e.activation(out=convT[:, c, :16], in_=A[:, c, :], func=AF.Silu, bias=CF[:, 80 + c:81 + c]), reads=["BIGA", "CF"], writes=["convT"])
            dt_softplus(16)
            xS = xdt
            xS2 = xn
            for grp in range(3):
                b = bank()
                for q in range(4):
                    c = grp * 4 + q
                    P.op("pe", lambda e, q=q, c=c, b=b: e.transpose(out=ps[b][:16, q * 128:(q + 1) * 128], in_=convT[:, c, :16], identity=ident),
                         reads=["convT", "CM"], writes=[("ps", b)])
                dst = xS[:16, grp * 512:(grp + 1) * 512] if grp < 2 else xS2[:16, 0:512]
                P.op("act", lambda e, b=b, dst=dst: e.activation(out=dst, in_=ps[b][:16, :512], func=AF.Copy), reads=[("ps", b)], writes=["xdt", "xn"])
            P.op("act", lambda e: e.activation(out=sm[:16, 64:80], in_=sm[:16, 48:64], func=AF.Exp), reads=["sm"], writes=["sm"])
            P.op("dve", lambda e: e.tensor_copy(out=v3(junk[:16, :], 16), in_=sm[:16, 64:80].unsqueeze(2).to_broadcast([16, 16, 64])), reads=["sm"], writes=["junk"])
            P.op("dve", lambda e: e.tensor_tensor(out=v3(xtok[:16, 0:1024], 16), in0=v3(xS[:16, :], 16), in1=sm[:16, 32:48].unsqueeze(2).to_broadcast([16, 16, 64]), op=ALU.mult),
                 reads=["xdt", "sm"], writes=["xtok"])
            DT2 = RA[:, 0:256].rearrange("p (w j s) -> p w j s", w=2, j=8)
            b = bank()
            for w, (srct, skey) in enumerate(((junk, "junk"), (xtok, "xtok"))):
                for j in range(8):
                    P.op("pe", lambda e, w=w, j=j, b=b, srct=srct: e.transpose(out=ps[b][:, (w * 8 + j) * 16:(w * 8 + j + 1) * 16], in_=srct[:16, j * 128:(j + 1) * 128], identity=ident[:16, :16]),
                         reads=[skey, "CM"], writes=[("ps", b)])
            P.op("act", lambda e, b=b: e.activation(out=RA[:, 0:256], in_=ps[b][:, :256], func=AF.Copy), reads=[("ps", b)], writes=["RA"])
            P.op("dve", lambda e: e.tensor_copy(out=SEL, in_=ident[:16, :16].unsqueeze(2).to_broadcast([16, 16, 128])), reads=["CM"], writes=["BK"])
            for si in range(NS):
                Sin = BIGA[:, (si % 2) * 1024:(si % 2 + 1) * 1024]; Sout = BIGC[:, (si % 2) * 1024:(si % 2 + 1) * 1024]
                tA = BIGB[:, 0:1024]; tB = BIGB[:, 1024:2048]
                P.dma("sp", lambda e, si=si, Sin=Sin: e.dma_start(out=Sin.rearrange("p (j n) -> p j n", j=8), in_=st_ssm[si].rearrange("h p n -> (h p) n").rearrange("(j q) n -> q j n", q=128)),
                      writes=["BIGA"])
                b = bank()
                P.op("pe", lambda e, b=b, si=si: e.matmul(ps[b][:, :512], lhsT=SEL[:, si, :], rhs=xS2[:16, 0:512], start=True, stop=True), reads=["BK", "xn"], writes=[("ps", b)])
                P.op("dve", lambda e, si=si, Sin=Sin, tA=tA: e.tensor_tensor(out=v3(tA, 8), in0=v3(Sin, 8), in1=DT2[:, 0, :, si:si + 1].to_broadcast([128, 8, 128]), op=ALU.mult),
                     reads=["BIGA", "RA"], writes=["BIGB"])
                P.op("dve", lambda e, si=si, b=b, tB=tB: e.tensor_tensor(out=tB.rearrange("p (g j n) -> p g j n", g=2, j=4),
                                                                           in0=ps[b][:, 0:256].rearrange("p (g n) -> p g n", g=2).unsqueeze(2).to_broadcast([128, 2, 4, 128]),
                                                                           in1=DT2[:, 1, :, si].rearrange("p (g j) -> p g j", g=2).unsqueeze(3).to_broadcast([128, 2, 4, 128]), op=ALU.mult),
                     reads=[("ps", b), "RA"], writes=["BIGB"])
                P.op("dve", lambda e, Sout=Sout, tA=tA, tB=tB: e.tensor_tensor(out=Sout, in0=tA, in1=tB, op=ALU.add), reads=["BIGB"], writes=["BIGC"])
                P.dma("sp", lambda e, si=si, Sout=Sout: e.dma_start(out=ssm_s[si].rearrange("(j q) n -> q j n", q=128), in_=Sout.rearrange("p (j n) -> p j n", j=8)), reads=["BIGC"])
                P.op("dve", lambda e, si=si, b=b, Sout=Sout, tA=tA: e.tensor_tensor(out=tA.rearrange("p (g j n) -> p g j n", g=2, j=4), in0=Sout.rearrange("p (g j n) -> p g j n", g=2, j=4),
                                                                                     in1=ps[b][:, 256:512].rearrange("p (g n) -> p g n", g=2).unsqueeze(2).to_broadcast([128, 2, 4, 128]), op=ALU.mult),
                     reads=[("ps", b), "BIGC"], writes=["BIGB"])
                P.op("dve", lambda e, si=si, tA=tA: e.tensor_reduce(out=YST[:, :, si], in_=v3(tA, 8), axis=AX.X, op=ALU.add), reads=["BIGB"], writes=["convT"])
            P.op("pool", lambda e: e.tensor_tensor(out=v3(junk[:16, :], 16), in0=v3(xS[:16, 0:1024], 16), in1=CT[:16, 32:48].unsqueeze(2).to_broadcast([16, 16, 64]), op=ALU.mult),
                 reads=["xdt", "CT"], writes=["junk"])
            for half in range(2):
                b = bank()
                for q in range(4):
                    j = half * 4 + q
                    P.op("pe", lambda e, q=q, j=j, b=b: e.transpose(out=ps[b][:16, q * 128:(q + 1) * 128], in_=YST[:, j, :], identity=ident), reads=["convT", "CM"], writes=[("ps", b)])
                P.op("dve", lambda e, b=b, half=half: e.tensor_tensor(out=ysb[:16, half * 512:(half + 1) * 512], in0=junk[:16, half * 512:(half + 1) * 512], in1=ps[b][:16, :512], op=ALU.add),
                     reads=[("ps", b), "junk"], writes=["ysb"])
            ssd_post(16)
            P.dma("sp", lambda e: e.dma_start(out=BIGA[:16, 0:2048], in_=st_shift[:, 0:2048]), writes=["BIGA"])
            P.dma("sp", lambda e: e.dma_start(out=BIGB[:16, 0:1280], in_=st_shift[:, 2048:3328]), writes=["BIGB"])
            b = bank()
            for c in range(26):
                src = BIGA[:16, c * 128:(c + 1) * 128] if c < 16 else BIGB[:16, (c - 16) * 128:(c - 15) * 128]
                P.op("pe", lambda e, c=c, b=b, src=src: e.transpose(out=ps[b][:, c * 16:(c + 1) * 16], in_=src, identity=ident[:16, :16]), reads=["BIGA", "BIGB", "CM"], writes=[("ps", b)])
            P.op("act", lambda e, b=b: e.activation(out=rwT[:, :, 32:48], in_=ps[b][:, :416].rearrange("p (c t) -> p c t", c=26), func=AF.Copy), reads=[("ps", b)], writes=["rwT"])
            rT, kT, vT, t1, t2, t4, t5, t6, t7 = rwkv_pre(16, rwT[:, :, 32:48], rwT[:, :, 1:17])
            P.op("act", lambda e: e.activation(out=t1, in_=t1, func=AF.Exp), reads=["BIGA"], writes=["BIGA"])
            TKs = [(t1, xdd, "xdd", "BIGA", 1.0), (t4, ysb, "ysb", "BIGB", -1.0), (t2, junk, "junk", "BIGB", 1.0), (kT, xtok, "xtok", "RWS", 1.0), (rT, xn, "xn", "RWS", 1.0)]
            for (src3, dst, dkey, skey, scl) in TKs:
                for half in range(2):
                    b = bank()
                    for q in range(4):
                        c = half * 4 + q
                        P.op("pe", lambda e, q=q, c=c, b=b, src3=src3: e.transpose(out=ps[b][:16, q * 128:(q + 1) * 128], in_=src3[:, c, :], identity=ident), reads=[skey, "CM"], writes=[("ps", b)])
                    P.op("act", lambda e, b=b, half=half, dst=dst, scl=scl: e.activation(out=dst[:16, half * 512:(half + 1) * 512], in_=ps[b][:16, :512], func=AF.Copy, scale=scl),
                         reads=[("ps", b)], writes=[dkey])
            SELH = [RA[:16, 0:2048].rearrange("p (s m) -> p s m", s=16), BKH[:16, 0:2048].rearrange("p (s m) -> p s m", s=16)]
            for hp, key in ((0, "RA"), (1, "BKH")):
                P.op("pool", lambda e, hp=hp: e.memset(SELH[hp], 0.0), writes=[key])
                P.op("dve", lambda e, hp=hp: e.tensor_copy(out=SELH[hp][:, :, hp * 64:(hp + 1) * 64], in_=ident[:16, :16].unsqueeze(2).to_broadcast([16, 16, 64])), reads=["CM"], writes=[key])
            tmp = BIGA[:, 0:512]
            for si in range(NS):
                SV = BIGB[:, (si % 2) * 512:(si % 2 + 1) * 512]; S1 = BIGB[:, 1024 + (si % 2) * 512:1024 + (si % 2 + 1) * 512]
                for hp in range(2):
                    P.dma("sp", lambda e, si=si, hp=hp, SV=SV: e.dma_start(out=SV[hp * 64:(hp + 1) * 64, :].rearrange("p (j k) -> p j k", j=8),
                                                                           in_=st_wkv[si].rearrange("(j hp) v k -> hp v j k", hp=2)[hp]), writes=["BIGB"])

                def bcast(Xt, xkey, si=si):
                    b = bank()
                    for hp in range(2):
                        P.op("pe", lambda e, hp=hp, b=b, Xt=Xt, si=si: e.matmul(ps[b][:, :512], lhsT=SELH[hp][:, si, :],
                                                                                rhs=Xt[:16, 0:1024].rearrange("s (j hp k) -> s j hp k", hp=2, k=64)[:, :, hp, :],
                                                                                start=(hp == 0), stop=(hp == 1)), reads=["RA", "BKH", xkey], writes=[("ps", b)])
                    return b
                ba = bcast(ysb, "ysb")
                P.op("dve", lambda e, ba=ba, SV=SV: e.tensor_tensor(out=tmp, in0=SV, in1=ps[ba][:, :512], op=ALU.mult), reads=[("ps", ba), "BIGB"], writes=["BIGA"])
                P.op("dve", lambda e: e.tensor_reduce(out=smw[:, 8:16], in_=v3(tmp, 8), axis=AX.X, op=ALU.add), reads=["BIGA"], writes=["smw"])
                bw = bcast(xdd, "xdd")
                P.op("dve", lambda e, bw=bw, SV=SV, S1=S1: e.tensor_tensor(out=S1, in0=SV, in1=ps[bw][:, :512], op=ALU.mult), reads=[("ps", bw), "BIGB"], writes=["BIGB"])
                bb = bcast(junk, "junk")
                P.op("dve", lambda e, bb=bb: e.tensor_tensor(out=v3(tmp, 8), in0=v3(ps[bb][:, :512], 8), in1=smw[:, 8:16].unsqueeze(2).to_broadcast([128, 8, 64]), op=ALU.mult),
                     reads=[("ps", bb), "smw"], writes=["BIGA"])
                P.op("dve", lambda e, S1=S1: e.tensor_tensor(out=S1, in0=S1, in1=tmp, op=ALU.add), reads=["BIGA", "BIGB"], writes=["BIGB"])
                bk = bcast(xtok, "xtok")
                P.op("dve", lambda e, bk=bk, si=si: e.tensor_tensor(out=v3(tmp, 8), in0=v3(ps[bk][:, :512], 8), in1=RWS3[:, 16:24, si:si + 1].to_broadcast([128, 8, 64]), op=ALU.mult),
                     reads=[("ps", bk), "RWS"], writes=["BIGA"])
                P.op("dve", lambda e, S1=S1: e.tensor_tensor(out=S1, in0=S1, in1=tmp, op=ALU.add), reads=["BIGA", "BIGB"], writes=["BIGB"])
                for hp in range(2):
                    P.dma("sp", lambda e, si=si, hp=hp, S1=S1: e.dma_start(out=wkv_s[si].rearrange("(j hp) v k -> hp v j k", hp=2)[hp],
                                                                           in_=S1[hp * 64:(hp + 1) * 64, :].rearrange("p (j k) -> p j k", j=8)), reads=["BIGB"])
                br = bcast(xn, "xn")
                P.op("dve", lambda e, br=br, S1=S1: e.tensor_tensor(out=tmp, in0=S1, in1=ps[br][:, :512], op=ALU.mult), reads=[("ps", br), "BIGB"], writes=["BIGA"])
                P.op("dve", lambda e, si=si: e.tensor_reduce(out=YRT[:, :, si], in_=v3(tmp, 8), axis=AX.X, op=ALU.add), reads=["BIGA"], writes=["convT"])
            for half in range(2):
                b = bank()
                for q in range(4):
                    j = half * 4 + q
                    P.op("pe", lambda e, q=q, j=j, b=b: e.transpose(out=ps[b][:16, q * 128:(q + 1) * 128], in_=YRT[:, j, :], identity=ident), reads=["convT", "CM"], writes=[("ps", b)])
                P.op("act", lambda e, b=b, half=half: e.activation(out=Ys[:16, half * 512:(half + 1) * 512], in_=ps[b][:16, :512], func=AF.Copy), reads=[("ps", b)], writes=["BIGA"])
            rwkv_post(16)
            mixer_out(16)
            if os.environ.get("K_DBGS") == "1":
                P.dma("sp", lambda e: e.dma_start(out=y_s, in_=xres[:NS, :]), reads=["xres"])
            if stage >= 5:
                attn_sample()
            if os.environ.get("K_DBGS") == "2":
                P.dma("sp", lambda e: e.dma_start(out=y_s, in_=xres[:NS, :]), reads=["xres"])
            if stage >= 6:
                peer(NS)()
            if stage >= 7:
                final_out(NS, y_s)
        if stage >= 2:
            for half in range(2):
                b = bank()
                for q in range(4):
                    c = half * 4 + q
                    P.op("pe", lambda e, c=c, q=q, b=b: e.transpose(out=ps[b][:, q * 128:(q + 1) * 128], in_=stT[:, c * 128:(c + 1) * 128], identity=ident),
                         reads=["stT", "CM"], writes=[("ps", b)])
                P.op("act", lambda e, b=b: e.activation(out=osb[:], in_=ps[b][:, :512], func=AF.Copy), reads=[("ps", b)], writes=["osb"])
                P.dma("sp", lambda e, half=half: e.dma_start(out=ssm_p[half * 512:(half + 1) * 512, :].rearrange("(q p) n -> p q n", p=128),
                                                             in_=osb[:].rearrange("p (q n) -> p q n", q=4)), reads=["osb"])
        if stage >= 3:
            for half in range(2):
                b = bank()
                for q in range(4):
                    j = half * 4 + q
                    P.op("pe", lambda e, j=j, q=q, b=b: e.transpose(out=ps[b][:64, q * 128:(q + 1) * 128], in_=Hst[:, j, :], identity=ident),
                         reads=["Hst", "CM"], writes=[("ps", b)])
                P.op("act", lambda e, b=b: e.activation(out=osb[:64, :], in_=ps[b][:64, :512], func=AF.Copy), reads=[("ps", b)], writes=["osb"])
                P.dma("sp", lambda e, half=half: e.dma_start(out=wkv_p[half * 8:(half + 1) * 8].rearrange("h v k -> v h k"),
                                                             in_=osb[:64, :].rearrange("p (h k) -> p h k", h=8)), reads=["osb"])
        print('SBUF bytes remaining', nc.sbuf_bytes_remaining)
        P.emit()
    return nc


_CACHE = {}


def kernel(**inp):
    f = lambda k: np.asarray(inp[k], np.float32)
    if "nc" not in _CACHE:
        _CACHE["nc"] = build_program()
    nc = _CACHE["nc"]
    cfm = np.zeros((128, 192), np.float32)
    cfm[:, 0:8] = fm(f("norm_mix_w")[0], 8); cfm[:, 8:16] = fm(f("norm_mem_w")[0], 8)
    cfm[:, 16:24] = fm(f("norm_ffn_w")[0], 8); cfm[:, 24:32] = fm(f("mem_norm_w")[0], 8)
    cw = f("conv_w")[0]
    cfm[:, 32:80] = np.stack([fm(cw[k], 12) for k in range(4)], axis=2).reshape(128, 48)
    cfm[:, 80:92] = fm(f("conv_b")[0], 12)
    cfm[:, 92:118] = fm(f("rwkv_mu")[0], 26)
    cfm[:, 118:126] = fm(f("rwkv_w0")[0], 8); cfm[:, 126:134] = fm(f("rwkv_a0")[0], 8)
    cfm[:, 134:142] = fm(f("rwkv_k_k")[0], 8); cfm[:, 142:150] = fm(f("rwkv_k_a")[0], 8)
    cfm[:, 150:158] = fm(f("rwkv_r_k")[0].reshape(-1), 8)
    cfm[:, 160:168] = fm(f("rwkv_ln_w")[0], 8); cfm[:, 168:176] = fm(f("rwkv_ln_b")[0], 8)
    cfm[:, 176:184] = fm(f("ssd_norm_w")[0], 8)
    ctk = np.zeros((1, 48 + D), np.float32)
    ctk[0, 0:16] = f("dt_bias")[0]; ctk[0, 16:32] = f("a_log")[0]; ctk[0, 32:48] = f("d_skip")[0]
    ctk[0, 48:] = f("norm_final_w")
    r = np.arange(128)
    cmat = np.zeros((128, 6 * 128 + 16), np.float32)
    cmat[:, 768:784] = np.arange(16)[None, :]
    cmat[:, 0:128] = np.eye(128)
    cmat[:, 128:256] = (r[:, None] > r[None, :])
    cmat[:, 256:384] = (r[:, None] <= r[None, :])
    cmat[:, 384:512] = 1.0
    cmat[:, 512:640] = ((r[:, None] // 64) == (r[None, :] // 64))
    cmat[:, 640:768] = (r[:, None] < r[None, :])
    shared = {
        "w_in": f("w_in")[0], "w_out": f("w_out")[0], "w_mk": f("w_mk")[0], "w_mv": f("w_mv")[0],
        "w_mq": f("w_mq")[0], "w_mo": f("w_mo")[0], "w_pq": f("w_pq")[0],
        "sub_keys": f("sub_keys")[0].reshape(16, 128, 128),
        "cfm": cfm, "ctk": ctk, "cmat": cmat,
        "wa2": np.concatenate([f("rwkv_w2")[0], f("rwkv_a2")[0]], axis=0), "g2": f("rwkv_g2")[0],
    }
    if True:
        shared["exp_u"] = f("expert_u")[0]; shared["exp_v"] = f("expert_v")[0]
    in_maps = []
    for c in range(NCORES):
        s = slice(c * NS, (c + 1) * NS)
        m = dict(shared)
        m.update({
            "xp": f("x_prompt")[c], "xs": f("x_sample")[s, 0], "memp": f("mem_prompt")[c],
            "st_ssm": f("state_ssm")[0, s], "st_conv": f("state_conv")[0, s], "st_wkv": f("state_wkv")[0, s],
            "st_shift": f("state_shift")[0, s],
            "ck": f("cache_mem_k")[0, s].reshape(NS, NMEM, D), "cv": f("cache_mem_v")[0, s].reshape(NS, NMEM, D),
        })
        in_maps.append(m)
    res = run_bass_kernel_spmd(nc, in_maps, core_ids=list(range(NCORES))).results
    cat = lambda k: np.stack([r_[k] for r_ in res], axis=0)
    y_prompt = cat("y_p")
    y_sample = np.concatenate([r_["y_s"] for r_ in res], axis=0).reshape(128, 1, D)
    ssm_prompt = cat("ssm_p").reshape(1, 8, 16, 64, 128)
    conv_prompt = cat("conv_p").reshape(1, 8, 3, CONV_DIM)
    wkv_prompt = cat("wkv_p").reshape(1, 8, 16, 64, 64)
    shift_prompt = cat("shift_p").reshape(1, 8, RWP)
    mem_k_prompt = cat("mk_p").reshape(1, 8, NMEM, 4, 256)
    mem_v_prompt = cat("mv_p").reshape(1, 8, NMEM, 4, 256)
    ssm_sample = np.concatenate([r_["ssm_s"] for r_ in res], axis=0).reshape(1, 128, 16, 64, 128)
    conv_sample = np.concatenate([r_["conv_s"] for r_ in res], axis=0).reshape(1, 128, 3, CONV_DIM)
    wkv_sample = np.concatenate([r_["wkv_s"] for r_ in res], axis=0).reshape(1, 128, 16, 64, 64)
    shift_sample = np.concatenate([r_["shift_s"] for r_ in res], axis=0).reshape(1, 128, RWP)
    return (y_prompt, y_sample, ssm_prompt, conv_prompt, wkv_prompt, shift_prompt, mem_k_prompt, mem_v_prompt,
            ssm_sample, conv_sample, wkv_sample, shift_sample)
```

```python
import os
import numpy as np
from contextlib import ExitStack
import concourse.bass as bass
import concourse.mybir as mybir
from concourse.bass_utils import run_bass_kernel_spmd

F32 = mybir.dt.float32
BF16 = mybir.dt.bfloat16
U32 = mybir.dt.uint32
ALU = mybir.AluOpType
AF = mybir.ActivationFunctionType
AX = mybir.AxisListType

NCORES = 8
D = 1024
SEQ = 2048
NCH = SEQ // 128
NS = 16
D_IN = 7952
CONV_DIM = 1536
RWP = 3328
NMEM = 256
EPS = 1e-6

SAME_ENG_SYNC = True
NDMA_SEMS = 12


class Prog:
    def __init__(self, nc):
        self.nc = nc
        self.names = ["pe", "act", "dve", "pool", "sp"]
        self.streams = {k: [] for k in self.names}
        self.count = {k: 0 for k in self.names}
        self.waited = {}
        self.res = {}
        self.dma_ring = {k: 0 for k in self.names}
        self.dma_val = {}

    def _deps(self, reads, writes):
        deps = []
        for r in reads:
            e = self.res.get(r)
            if e and e[0] is not None:
                deps.append(e[0])
        for w in writes:
            e = self.res.get(w)
            if e:
                if e[0] is not None:
                    deps.append(e[0])
                deps.extend(e[1])
        return deps

    def _commit(self, tok, reads, writes):
        for r in reads:
            e = self.res.setdefault(r, [None, []])
            e[1].append(tok)
        for w in writes:
            self.res[w] = [tok, []]

    def _waits(self, engine, deps):
        best = {}
        for (sk, v) in deps:
            if sk == engine and (engine == "pe" or not SAME_ENG_SYNC):
                continue
            if self.waited.get((engine, sk), 0) >= v:
                continue
            if best.get(sk, 0) < v:
                best[sk] = v
        for sk, v in best.items():
            self.waited[(engine, sk)] = v
        return list(best.items())

    def op(self, engine, fn, reads=(), writes=()):
        reads = list(reads)
        writes = list(writes)
        waits = self._waits(engine, self._deps(reads, writes))
        self.count[engine] += 1
        tok = (engine, self.count[engine])
        self.streams[engine].append((fn, waits, (engine, 1)))
        self._commit(tok, reads, writes)
        return tok

    def dma(self, engine, fn, reads=(), writes=()):
        reads = list(reads)
        writes = list(writes)
        ring = self.dma_ring[engine]
        self.dma_ring[engine] = (ring + 1) % NDMA_SEMS
        sk = ("dma", engine, ring)
        prev = self.dma_val.get(sk, 0)
        deps = self._deps(reads, writes)
        if prev:
            deps.append((sk, prev))
        waits = self._waits(engine, deps)
        self.dma_val[sk] = prev + 16
        tok = (sk, prev + 16)
        self.streams[engine].append((fn, waits, (sk, 16)))
        self._commit(tok, reads, writes)
        return tok

    def emit(self):
        nc = self.nc
        with ExitStack() as es:
            sems = {}
            for k in self.names:
                sems[k] = es.enter_context(nc.semaphore("s_" + k))
            for sk in self.dma_val:
                sems[sk] = es.enter_context(nc.semaphore("d_%s_%d" % (sk[1], sk[2])))
            final = dict(self.dma_val)
            for k in self.names:
                if k != "sp" and self.count[k]:
                    final[k] = self.count[k]
            block = es.enter_context(nc.Block())

            def run(e, name):
                for fn, waits, inc in self.streams[name]:
                    for sk, v in waits:
                        e.wait_ge(sems[sk], v)
                    fn(e).then_inc(sems[inc[0]], inc[1])
                if name == "sp":
                    for sk, v in final.items():
                        e.wait_ge(sems[sk], v)

            @block.tensor
            def _(e):
                run(e, "pe")

            @block.scalar
            def _(e):
                run(e, "act")

            @block.vector
            def _(e):
                run(e, "dve")

            @block.gpsimd
            def _(e):
                run(e, "pool")

            @block.sync
            def _(e):
                run(e, "sp")


def fm(v, n):
    return np.ascontiguousarray(np.asarray(v, np.float32).reshape(n, 128).T)


STAGE = int(os.environ.get('K_STAGE', 99))
SUB = int(os.environ.get('K_SUB', 99))
SUB2 = int(os.environ.get('K_SUB2', 99))
SUB3 = int(os.environ.get('K_SUB3', 99))


def build_program(stage=None):
    stage = STAGE if stage is None else stage
    nc = bass.Bass("TRN2", target_bir_lowering=False)

    def din(name, shape):
        return nc.dram_tensor(name, list(shape), F32, kind="ExternalInput").ap()

    def dout(name, shape):
        return nc.dram_tensor(name, list(shape), F32, kind="ExternalOutput").ap()

    xp = din("xp", [SEQ, D]); xs_in = din("xs", [NS, D]); memp = din("memp", [NMEM, D])
    st_ssm = din("st_ssm", [NS, 16, 64, 128]); st_conv = din("st_conv", [NS, 3, CONV_DIM])
    st_wkv = din("st_wkv", [NS, 16, 64, 64]); st_shift = din("st_shift", [NS, RWP])
    ck = din("ck", [NS, NMEM, D]); cv = din("cv", [NS, NMEM, D])
    w_in = din("w_in", [D, D_IN]); w_out = din("w_out", [D, D])
    w_mk = din("w_mk", [D, D]); w_mv = din("w_mv", [D, D]); w_mq = din("w_mq", [D, D]); w_mo = din("w_mo", [D, D])
    w_pq = din("w_pq", [D, 2048]); sub_keys = din("sub_keys", [16, 128, 128])
    if stage >= 0:
        exp_u = din("exp_u", [16384, D]); exp_v = din("exp_v", [16384, D])
    cfm = din("cfm", [128, 192])
    ctk = din("ctk", [1, 48 + D])
    wa2 = din("wa2", [128, D]); g2 = din("g2", [128, D])
    cmat = din("cmat", [128, 6 * 128 + 16])

    y_p = dout("y_p", [SEQ, D]); y_s = dout("y_s", [NS, D])
    ssm_p = dout("ssm_p", [16 * 64, 128]); conv_p = dout("conv_p", [3, CONV_DIM])
    wkv_p = dout("wkv_p", [16, 64, 64]); shift_p = dout("shift_p", [1, RWP])
    mk_p = dout("mk_p", [NMEM, D]); mv_p = dout("mv_p", [NMEM, D])
    ssm_s = dout("ssm_s", [NS, 16 * 64, 128]); conv_s = dout("conv_s", [NS, 3, CONV_DIM])
    wkv_s = dout("wkv_s", [NS, 16, 64, 64]); shift_s = dout("shift_s", [NS, RWP])

    es = ExitStack()
    with es:
        def sb(name, shape, dt=F32):
            return es.enter_context(nc.sbuf_tensor(name, list(shape), dt))

        P = Prog(nc)
        es.enter_context(nc.allow_non_contiguous_dma(reason="small transposing stores"))
        ps = [es.enter_context(nc.psum_tensor("ps%d" % i, [128, 512], F32)) for i in range(8)]
        bank_ctr = [0]

        def bank():
            b = bank_ctr[0]
            bank_ctr[0] = (b + 1) % 8
            return b

        CM = sb("CM", [128, 6 * 128 + 16]); CF = sb("CF", [128, 192]); CT = sb("CT", [128, 48 + D])
        P.dma("sp", lambda e: e.dma_start(out=CM[:], in_=cmat), writes=["CM"])
        P.dma("sp", lambda e: e.dma_start(out=CF[:], in_=cfm), writes=["CF"])
        P.dma("sp", lambda e: e.dma_start(out=CT[:], in_=ctk.partition_broadcast(128)), writes=["CT"])
        ident = CM[:, 0:128]; M1 = CM[:, 128:256]; M2 = CM[:, 256:384]; ONES = CM[:, 384:512]
        BONES = CM[:, 512:640]; M3 = CM[:, 640:768]; IOTA = CM[:, 768:784]

        xres = sb("xres", [128, D]); xres2 = sb("xres2", [128, D])
        CUR = [xres, "xres"]
        xn = sb("xn", [128, D])
        junk = sb("junk", [128, D])
        st1 = sb("st1", [128, 8])
        hT = sb("hT", [128, 8, 128])
        WB = [sb("wb%d" % i, [128, 8, 512]) for i in range(2)]
        wb_ctr = [0]
        xbcT = sb("xbcT", [128, 12, 131])
        rwT = sb("rwT", [128, 26, 129])
        gateT = sb("gateT", [128, 16, 128])
        ztok = sb("ztok", [128, D])
        dtraw = sb("dtraw", [128, 16])

        def rmsnorm_T(src, nt, wcol, dstT, src_key, dst_key):
            P.op("act", lambda e: e.activation(out=junk[:nt, :], in_=src[:nt, :], func=AF.Square, accum_out=st1[:nt, 0:1]),
                 reads=[src_key], writes=["junk", "st1"])
            P.op("act", lambda e: e.activation(out=st1[:nt, 1:2], in_=st1[:nt, 0:1], func=AF.Sqrt, scale=1.0 / D, bias=EPS),
                 reads=["st1"], writes=["st1"])
            P.op("dve", lambda e: e.reciprocal(out=st1[:nt, 2:3], in_=st1[:nt, 1:2]), reads=["st1"], writes=["st1"])
            P.op("dve", lambda e: e.tensor_scalar(out=xn[:nt, :], in0=src[:nt, :], scalar1=st1[:nt, 2:3], scalar2=None,
                                                  op0=ALU.mult), reads=["st1", src_key], writes=["xn"])
            for half in range(2):
                b = bank()
                for q in range(4):
                    c = half * 4 + q
                    P.op("pe", lambda e, c=c, q=q, b=b: e.transpose(out=ps[b][:, q * 128:q * 128 + nt],
                                                                     in_=xn[:nt, c * 128:(c + 1) * 128], identity=ident[:nt, :nt]),
                         reads=["xn", "CM"], writes=[("ps", b)])
                P.op("dve", lambda e, half=half, b=b: e.tensor_tensor(
                    out=dstT[:, half * 4:half * 4 + 4, :nt],
                    in0=ps[b][:].rearrange("p (q t) -> p q t", q=4)[:, :, :nt],
                    in1=CF[:, wcol + half * 4:wcol + half * 4 + 4].unsqueeze(2).to_broadcast([128, 4, nt]),
                    op=ALU.mult), reads=[("ps", b), "CF"], writes=[dst_key])

        def load_w(wdram, col0, ncols, eng="sp"):
            i = wb_ctr[0]
            wb_ctr[0] = (i + 1) % 2
            P.dma(eng, lambda e: e.dma_start(out=WB[i][:, :, :ncols],
                                             in_=wdram[:, col0:col0 + ncols].rearrange("(kc p) n -> p kc n", p=128)),
                  writes=[("wb", i)])
            return i

        def proj_fm(wdram, col0, nchunks, srcT, src_key, nt, evac, eng="sp"):
            ncols = nchunks * 128
            i = load_w(wdram, col0, ncols, eng)
            b = bank()
            for q in range(nchunks):
                for kc in range(8):
                    P.op("pe", lambda e, q=q, kc=kc: e.matmul(ps[b][:, q * 128:q * 128 + nt],
                                                                lhsT=WB[i][:, kc, q * 128:(q + 1) * 128], rhs=srcT[:, kc, :nt],
                                                                start=(kc == 0), stop=(kc == 7)),
                         reads=[("wb", i), src_key], writes=[("ps", b)])
            evac(b, nchunks)

        def proj_tok(wdram, col0, ncols, srcT, src_key, nt, evac, eng="sp"):
            i = load_w(wdram, col0, ncols, eng)
            b = bank()
            for kc in range(8):
                P.op("pe", lambda e, kc=kc: e.matmul(ps[b][:nt, :ncols], lhsT=srcT[:, kc, :nt], rhs=WB[i][:, kc, :ncols],
                                                       start=(kc == 0), stop=(kc == 7)),
                     reads=[("wb", i), src_key], writes=[("ps", b)])
            evac(b, ncols)

        def psv(b, nchunks, nt):
            return ps[b][:, :nchunks * 128].rearrange("p (q t) -> p q t", q=nchunks)[:, :, :nt]

        def in_proj(nt, t0_xbc, t0_rw):
            for blk in range(2):
                proj_tok(w_in, blk * 512, 512, hT, "hT", nt,
                         lambda b, n, blk=blk: P.op("act", lambda e: e.activation(out=ztok[:nt, blk * 512:(blk + 1) * 512], in_=ps[b][:nt, :512], func=AF.Silu),
                                                    reads=[("ps", b)], writes=["ztok"]))
            for blk in range(3):
                proj_fm(w_in, 1024 + blk * 512, 4, hT, "hT", nt,
                        lambda b, n, blk=blk: P.op("dve", lambda e: e.tensor_copy(out=xbcT[:, blk * 4:blk * 4 + 4, t0_xbc:t0_xbc + nt], in_=psv(b, 4, nt)),
                                                   reads=[("ps", b)], writes=["xbcT"]))
            proj_tok(w_in, 2560, 16, hT, "hT", nt,
                     lambda b, n: P.op("dve", lambda e: e.tensor_copy(out=dtraw[:nt, :], in_=ps[b][:nt, :16]), reads=[("ps", b)], writes=["dtraw"]))
            for blk in range(7):
                n = 4 if blk < 6 else 2
                proj_fm(w_in, 2576 + blk * 512, n, hT, "hT", nt,
                        lambda b, n, blk=blk: P.op("act", lambda e: e.activation(out=rwT[:, blk * 4:blk * 4 + n, t0_rw:t0_rw + nt], in_=psv(b, n, nt), func=AF.Copy),
                                                   reads=[("ps", b)], writes=["rwT"]))
            for blk in range(4):
                proj_fm(w_in, 5904 + blk * 512, 4, hT, "hT", nt,
                        lambda b, n, blk=blk: P.op("act", lambda e: e.activation(out=gateT[:, blk * 4:blk * 4 + 4, :nt], in_=psv(b, 4, nt), func=AF.Sigmoid),
                                                   reads=[("ps", b)], writes=["gateT"]))


        BIGA = sb("BIGA", [128, 2048]); BIGB = sb("BIGB", [128, 2048]); BIGC = sb("BIGC", [128, 2048])
        convT = sb("convT", [128, 12, 128])
        xtok = sb("xtok", [128, 1280])
        sm = sb("sm", [128, 128])
        ANEG = sb("ANEG", [128, 16])
        cbm = sb("cbm", [128, 2, 128])
        xdt = sb("xdt", [128, D]); xdd = sb("xdd", [128, D])
        stT = sb("stT", [128, D])
        ysb = sb("ysb", [128, D])
        mgT = sb("mgT", [128, 8, 128])
        P.op("act", lambda e: e.activation(out=ANEG[:], in_=CT[:, 16:32], func=AF.Exp), reads=["CT"], writes=["ANEG"])
        P.op("dve", lambda e: e.tensor_scalar(out=ANEG[:], in0=ANEG[:], scalar1=-1.0, scalar2=None, op0=ALU.mult), reads=["ANEG"], writes=["ANEG"])
        CWv = CF[:, 32:80].rearrange("p (c k) -> p c k", k=4)

        def v3(t, a):
            return t.rearrange("p (a b) -> p a b", a=a)

        def conv_silu(nt):
            A = BIGA[:, :12 * nt].rearrange("p (c t) -> p c t", c=12)
            Bv = BIGB[:, :12 * nt].rearrange("p (c t) -> p c t", c=12)
            P.op("dve", lambda e: e.tensor_tensor(out=A, in0=xbcT[:, :, 0:nt], in1=CWv[:, :, 0:1].to_broadcast([128, 12, nt]), op=ALU.mult),
                 reads=["xbcT", "CF"], writes=["BIGA"])
            for k in range(1, 4):
                P.op("pool", lambda e, k=k: e.tensor_tensor(out=Bv, in0=xbcT[:, :, k:k + nt], in1=CWv[:, :, k:k + 1].to_broadcast([128, 12, nt]), op=ALU.mult),
                     reads=["xbcT", "CF"], writes=["BIGB"])
                P.op("dve", lambda e: e.tensor_tensor(out=A, in0=A, in1=Bv, op=ALU.add), reads=["BIGA", "BIGB"], writes=["BIGA"])
            for c in range(12):
                P.op("act", lambda e, c=c: e.activation(out=convT[:, c, :nt], in_=A[:, c, :], func=AF.Silu, bias=CF[:, 80 + c:81 + c]),
                     reads=["BIGA", "CF"], writes=["convT"])

        def dt_softplus(nt):
            P.op("dve", lambda e: e.tensor_tensor(out=sm[:nt, 0:16], in0=dtraw[:nt, :], in1=CT[:nt, 0:16], op=ALU.add), reads=["dtraw", "CT"], writes=["sm"])
            P.op("act", lambda e: e.activation(out=sm[:nt, 16:32], in_=sm[:nt, 0:16], func=AF.Exp), reads=["sm"], writes=["sm"])
            P.op("act", lambda e: e.activation(out=sm[:nt, 32:48], in_=sm[:nt, 16:32], func=AF.Ln, bias=1.0), reads=["sm"], writes=["sm"])
            P.op("dve", lambda e: e.tensor_tensor(out=sm[:nt, 48:64], in0=sm[:nt, 32:48], in1=ANEG[:nt, :], op=ALU.mult), reads=["sm", "ANEG"], writes=["sm"])

        def ssd_post(nt):
            P.op("dve", lambda e: e.tensor_tensor(out=ysb[:nt, :], in0=ysb[:nt, :], in1=ztok[:nt, :], op=ALU.mult), reads=["ysb", "ztok"], writes=["ysb"])
            rmsnorm_T(ysb, nt, 176, mgT, "ysb", "mgT")
            P.op("dve", lambda e: e.tensor_tensor(out=mgT[:, :, :nt], in0=mgT[:, :, :nt], in1=gateT[:, 0:8, :nt], op=ALU.mult), reads=["mgT", "gateT"], writes=["mgT"])

        def ssd_chunk():
            conv_silu(128)
            for (cs, c0) in (((0, 1, 2, 3), 0), ((4, 5, 6, 7), 512), ((8, 9), 1024)):
                b = bank()
                for q, c in enumerate(cs):
                    P.op("pe", lambda e, q=q, c=c, b=b: e.transpose(out=ps[b][:, q * 128:(q + 1) * 128], in_=convT[:, c, :], identity=ident),
                         reads=["convT", "CM"], writes=[("ps", b)])
                n = len(cs) * 128
                P.op("act", lambda e, b=b, c0=c0, n=n: e.activation(out=xtok[:, c0:c0 + n], in_=ps[b][:, :n], func=AF.Copy), reads=[("ps", b)], writes=["xtok"])
            dt_softplus(128)
            dt = sm[:, 32:48]; ad = sm[:, 48:64]
            P.op("dve", lambda e: e.tensor_tensor(out=v3(BIGA[:], 16), in0=ad.unsqueeze(2).to_broadcast([128, 16, 128]),
                                                  in1=M2.unsqueeze(1).to_broadcast([128, 16, 128]), op=ALU.mult), reads=["sm", "CM"], writes=["BIGA"])
            for q in range(4):
                b = bank()
                P.op("pe", lambda e, q=q, b=b: e.matmul(ps[b][:, :512], lhsT=M1, rhs=BIGA[:, q * 512:(q + 1) * 512], start=True, stop=True),
                     reads=["BIGA", "CM"], writes=[("ps", b)])
                P.op("act", lambda e, q=q, b=b: e.activation(out=BIGB[:, q * 512:(q + 1) * 512], in_=ps[b][:, :512], func=AF.Exp), reads=[("ps", b)], writes=["BIGB"])
                b = bank()
                P.op("pe", lambda e, q=q, b=b: e.matmul(ps[b][:, :512], lhsT=ONES, rhs=BIGA[:, q * 512:(q + 1) * 512], start=True, stop=True),
                     reads=["BIGA", "CM"], writes=[("ps", b)])
                P.op("act", lambda e, q=q, b=b: e.activation(out=BIGC[:, q * 512:(q + 1) * 512], in_=ps[b][:, :512], func=AF.Exp), reads=[("ps", b)], writes=["BIGC"])
            b = bank()
            for g in range(2):
                P.op("pe", lambda e, g=g, b=b: e.matmul(ps[b][:, g * 128:(g + 1) * 128], lhsT=convT[:, 8 + g, :], rhs=convT[:, 10 + g, :], start=True, stop=True),
                     reads=["convT"], writes=[("ps", b)])
            P.op("dve", lambda e, b=b: e.tensor_tensor(out=cbm[:], in0=v3(ps[b][:, :256], 2), in1=M2.unsqueeze(1).to_broadcast([128, 2, 128]), op=ALU.mult),
                 reads=[("ps", b), "CM"], writes=["cbm"])
            G4 = BIGB[:].rearrange("p (g h l) -> p g h l", g=2, h=8)
            P.op("dve", lambda e: e.tensor_tensor(out=G4, in0=G4, in1=cbm[:].unsqueeze(2).to_broadcast([128, 2, 8, 128]), op=ALU.mult),
                 reads=["BIGB", "cbm"], writes=["BIGB"])
            E4 = BIGC[:].rearrange("p (g h l) -> p g h l", g=2, h=8)
            P.op("pool", lambda e: e.tensor_tensor(out=E4, in0=E4, in1=convT[:, 10:12, :].unsqueeze(2).to_broadcast([128, 2, 8, 128]), op=ALU.mult),
                 reads=["BIGC", "convT"], writes=["BIGC"])
            P.op("dve", lambda e: e.tensor_tensor(out=v3(xdt[:], 16), in0=v3(xtok[:, :D], 16), in1=dt.unsqueeze(2).to_broadcast([128, 16, 64]), op=ALU.mult),
                 reads=["xtok", "sm"], writes=["xdt"])
            yb = [bank(), bank()]
            for h in range(16):
                b = yb[h // 8]; o = (h % 8) * 64
                P.op("pe", lambda e, h=h, b=b, o=o: e.matmul(ps[b][:, o:o + 64], lhsT=BIGB[:, h * 128:(h + 1) * 128], rhs=xdt[:, h * 64:(h + 1) * 64], start=True, stop=False),
                     reads=["BIGB", "xdt"], writes=[("ps", b)])
                P.op("pe", lambda e, h=h, b=b, o=o: e.matmul(ps[b][:, o:o + 64], lhsT=BIGC[:, h * 128:(h + 1) * 128], rhs=stT[:, h * 64:(h + 1) * 64], start=False, stop=True),
                     reads=["BIGC", "stT"], writes=[("ps", b)])
            P.op("pool", lambda e: e.tensor_tensor(out=v3(junk[:], 16), in0=v3(xtok[:, :D], 16), in1=CT[:, 32:48].unsqueeze(2).to_broadcast([128, 16, 64]), op=ALU.mult),
                 reads=["xtok", "CT"], writes=["junk"])
            for hh in range(2):
                P.op("dve", lambda e, hh=hh, yb=yb: e.tensor_tensor(out=ysb[:, hh * 512:(hh + 1) * 512], in0=junk[:, hh * 512:(hh + 1) * 512], in1=ps[yb[hh]][:, :512], op=ALU.add),
                     reads=["junk", ("ps", yb[hh])], writes=["ysb"])
            b = bank()
            P.op("pe", lambda e, b=b: e.matmul(ps[b][:, 0:16], lhsT=M1, rhs=ad, start=True, stop=True), reads=["sm", "CM"], writes=[("ps", b)])
            P.op("pe", lambda e, b=b: e.matmul(ps[b][:, 16:32], lhsT=ONES, rhs=ad, start=True, stop=True), reads=["sm", "CM"], writes=[("ps", b)])
            P.op("act", lambda e, b=b: e.activation(out=sm[:, 64:96], in_=ps[b][:, 0:32], func=AF.Exp), reads=[("ps", b)], writes=["sm"])
            P.op("dve", lambda e: e.tensor_tensor(out=v3(xdd[:], 16), in0=v3(xdt[:], 16), in1=sm[:, 64:80].unsqueeze(2).to_broadcast([128, 16, 64]), op=ALU.mult),
                 reads=["xdt", "sm"], writes=["xdd"])
            P.op("dve", lambda e: e.tensor_tensor(out=v3(junk[:], 16), in0=v3(stT[:], 16), in1=sm[:, 80:96].unsqueeze(2).to_broadcast([128, 16, 64]), op=ALU.mult),
                 reads=["stT", "sm"], writes=["junk"])
            for g in range(2):
                b = bank()
                P.op("pe", lambda e, g=g, b=b: e.matmul(ps[b][:, :512], lhsT=xtok[:, D + g * 128:D + (g + 1) * 128], rhs=xdd[:, g * 512:(g + 1) * 512], start=True, stop=True),
                     reads=["xtok", "xdd"], writes=[("ps", b)])
                P.op("dve", lambda e, g=g, b=b: e.tensor_tensor(out=stT[:, g * 512:(g + 1) * 512], in0=junk[:, g * 512:(g + 1) * 512], in1=ps[b][:, :512], op=ALU.add),
                     reads=["junk", ("ps", b)], writes=["stT"])
            ssd_post(128)


        RWS = sb("RWS", [128, 26 * 128]); RA = sb("RA", [128, 2048]); BK = sb("BK", [128, 2048]); BKH = sb("BKH", [128, 2048])
        MASK4 = sb("MASK4", [128, 4, 128])
        Hst = sb("Hst", [128, 8, 64]); smw = sb("smw", [128, 64])
        for kd in range(4):
            P.op("pool", lambda e, kd=kd: e.tensor_copy(out=MASK4[:, kd, :], in_=(M2 if kd % 2 == 0 else M3)), reads=["CM"], writes=["MASK4"])
        P.op("pool", lambda e: e.memset(Hst[:], 0.0), writes=["Hst"])
        RWS3 = RWS[:].rearrange("p (c t) -> p c t", c=26)
        RA4 = RA[:].rearrange("p (c k t) -> p c k t", c=8, k=2)
        BK4 = BK[:].rearrange("p (c k t) -> p c k t", c=8, k=2)
        BKH4 = BKH[:].rearrange("p (c k t) -> p c k t", c=8, k=2)
        T1 = BIGA[:, 0:1024]; T6 = BIGA[:, 1024:2048]; T2 = BIGB[:, 0:1024]; T4 = BIGB[:, 1024:2048]
        T5 = BIGC[:, 0:1024]; T7 = BIGC[:, 1024:2048]; T3 = xdt
        vtok = xdd; btok = ysb; ktok = junk
        ATq = RWS[:, 0:2048].rearrange("p (h k t) -> p h k t", h=4, k=4)
        MN = [RWS[:, 2048 + i * 512:2048 + (i + 1) * 512] for i in range(2)]
        NN = [xn[:, 0:512], xn[:, 512:1024]]
        TT = [convT[:].rearrange("p c t -> p (c t)")[:, 0:512], convT[:].rearrange("p c t -> p (c t)")[:, 512:1024]]
        RHSs = xtok[:, 0:1024]; Us = T1; Ys = T6

        def bc3(ap2, n):
            return ap2.unsqueeze(2).to_broadcast([128, ap2.shape[1], n])

        def rwkv_pre(nt, prev, cur):
            R3 = RWS3[:, :, :nt]
            wi = wb_ctr[0]; wb_ctr[0] = (wi + 1) % 2
            WBf = WB[wi][:].rearrange("p a b -> p (a b)")
            WA2 = WBf[:, 0:1024]; G2 = WBf[:, 1024:2048]; wkey = ("wb", wi)
            P.dma("sp", lambda e: e.dma_start(out=WA2, in_=wa2), writes=[wkey])
            P.dma("sp", lambda e: e.dma_start(out=G2, in_=g2), writes=[wkey])
            P.op("dve", lambda e: e.tensor_tensor(out=R3, in0=prev, in1=cur, op=ALU.subtract), reads=["rwT"], writes=["RWS"])
            P.op("pool", lambda e: e.tensor_tensor(out=R3, in0=R3, in1=bc3(CF[:, 92:118], nt), op=ALU.mult), reads=["RWS", "CF"], writes=["RWS"])
            P.op("dve", lambda e: e.tensor_tensor(out=R3, in0=R3, in1=cur, op=ALU.add), reads=["RWS", "rwT"], writes=["RWS"])
            P.op("act", lambda e: e.activation(out=RWS3[0:64, 24, :nt], in_=RWS3[0:64, 24, :nt], func=AF.Tanh), reads=["RWS"], writes=["RWS"])
            P.op("act", lambda e: e.activation(out=RWS3[:, 25, :nt], in_=RWS3[:, 25, :nt], func=AF.Sigmoid), reads=["RWS"], writes=["RWS"])

            def v8(t):
                return t.rearrange("p (c t) -> p c t", c=8)[:, :, :nt]
            rT = RWS3[:, 0:8, :nt]; kT = RWS3[:, 8:16, :nt]; vT = RWS3[:, 16:24, :nt]
            for half in range(2):
                for (lo, hi, col, dst, dkey) in ((0, 64, 118, T1, "BIGA"), (64, 128, 126, T2, "BIGB")):
                    b = bank()
                    for q in range(4):
                        c = half * 4 + q
                        P.op("pe", lambda e, q=q, c=c, b=b, lo=lo, hi=hi: e.matmul(ps[b][:, q * 128:q * 128 + nt], lhsT=WA2[lo:hi, c * 128:(c + 1) * 128],
                                                                                    rhs=RWS3[lo:hi, 24, :nt], start=True, stop=True),
                             reads=[wkey, "RWS"], writes=[("ps", b)])
                    dv = v8(dst)[:, half * 4:half * 4 + 4, :]
                    P.op("dve", lambda e, b=b, dv=dv, col=col, half=half: e.tensor_tensor(out=dv, in0=psv(b, 4, nt), in1=bc3(CF[:, col + half * 4:col + half * 4 + 4], nt), op=ALU.add),
                         reads=[("ps", b), "CF"], writes=[dkey])
                    P.op("act", lambda e, dv=dv: e.activation(out=dv, in_=dv, func=AF.Sigmoid), reads=[dkey], writes=[dkey])
                b = bank()
                for q in range(4):
                    c = half * 4 + q
                    P.op("pe", lambda e, q=q, c=c, b=b: e.matmul(ps[b][:, q * 128:q * 128 + nt], lhsT=G2[:, c * 128:(c + 1) * 128], rhs=RWS3[:, 25, :nt], start=True, stop=True),
                         reads=[wkey, "RWS"], writes=[("ps", b)])
                P.op("act", lambda e, b=b, half=half: e.activation(out=v8(T3[:])[:, half * 4:half * 4 + 4, :], in_=psv(b, 4, nt), func=AF.Copy), reads=[("ps", b)], writes=["xdt"])
            t1 = v8(T1); t2 = v8(T2); t4 = v8(T4); t5 = v8(T5); t6 = v8(T6); t7 = v8(T7)
            P.op("pool", lambda e: e.tensor_scalar(out=t1, in0=t1, scalar1=-0.6065306597126334, scalar2=None, op0=ALU.mult), reads=["BIGA"], writes=["BIGA"])
            P.op("dve", lambda e: e.tensor_tensor(out=t4, in0=kT, in1=bc3(CF[:, 134:142], nt), op=ALU.mult), reads=["RWS", "CF"], writes=["BIGB"])
            P.op("pool", lambda e: e.tensor_tensor(out=t7, in0=t4, in1=t4, op=ALU.mult), reads=["BIGB"], writes=["BIGC"])
            for half in range(2):
                b = bank()
                for q in range(4):
                    c = half * 4 + q
                    P.op("pe", lambda e, q=q, c=c, b=b: e.matmul(ps[b][:, q * 128:q * 128 + nt], lhsT=BONES, rhs=t7[:, c, :], start=True, stop=True),
                         reads=["BIGC", "CM"], writes=[("ps", b)])
                P.op("dve", lambda e, b=b, half=half: e.tensor_scalar(out=t5[:, half * 4:half * 4 + 4, :], in0=psv(b, 4, nt), scalar1=1e-24, scalar2=None, op0=ALU.max),
                     reads=[("ps", b)], writes=["BIGC"])
            P.op("act", lambda e: e.activation(out=t5, in_=t5, func=AF.Sqrt), reads=["BIGC"], writes=["BIGC"])
            P.op("dve", lambda e: e.reciprocal(out=t5, in_=t5), reads=["BIGC"], writes=["BIGC"])
            P.op("dve", lambda e: e.tensor_tensor(out=t4, in0=t4, in1=t5, op=ALU.mult), reads=["BIGB", "BIGC"], writes=["BIGB"])
            P.op("dve", lambda e: e.scalar_tensor_tensor(out=t7, in0=t2, scalar=1.0, in1=bc3(CF[:, 142:150], nt), op0=ALU.subtract, op1=ALU.mult),
                 reads=["BIGB", "CF"], writes=["BIGC"])
            P.op("dve", lambda e: e.scalar_tensor_tensor(out=kT, in0=t7, scalar=1.0, in1=kT, op0=ALU.add, op1=ALU.mult), reads=["BIGC", "RWS"], writes=["RWS"])
            P.op("dve", lambda e: e.tensor_tensor(out=t2, in0=t4, in1=t2, op=ALU.mult), reads=["BIGB"], writes=["BIGB"])
            P.op("dve", lambda e: e.tensor_tensor(out=t7, in0=rT, in1=kT, op=ALU.mult), reads=["RWS"], writes=["BIGC"])
            P.op("pool", lambda e: e.tensor_tensor(out=t7, in0=t7, in1=bc3(CF[:, 150:158], nt), op=ALU.mult), reads=["BIGC", "CF"], writes=["BIGC"])
            for half in range(2):
                b = bank()
                for q in range(4):
                    c = half * 4 + q
                    P.op("pe", lambda e, q=q, c=c, b=b: e.matmul(ps[b][:, q * 128:q * 128 + nt], lhsT=BONES, rhs=t7[:, c, :], start=True, stop=True),
                         reads=["BIGC", "CM"], writes=[("ps", b)])
                P.op("dve", lambda e, b=b, half=half: e.tensor_tensor(out=t5[:, half * 4:half * 4 + 4, :], in0=psv(b, 4, nt), in1=vT[:, half * 4:half * 4 + 4, :], op=ALU.mult),
                     reads=[("ps", b), "RWS"], writes=["BIGC"])
            return rT, kT, vT, t1, t2, t4, t5, t6, t7

        def rwkv_post(nt):
            t5 = T5.rearrange("p (c t) -> p c t", c=8)[:, :, :nt]
            y3 = Ys[:nt, :].rearrange("p (h v) -> p h v", h=16)
            P.op("dve", lambda e: e.tensor_reduce(out=sm[:nt, 0:16], in_=y3, axis=AX.X, op=ALU.add), reads=["BIGA"], writes=["sm"])
            P.op("dve", lambda e: e.tensor_scalar(out=sm[:nt, 0:16], in0=sm[:nt, 0:16], scalar1=1.0 / 64, scalar2=None, op0=ALU.mult), reads=["sm"], writes=["sm"])
            P.op("dve", lambda e: e.tensor_tensor(out=y3, in0=y3, in1=sm[:nt, 0:16].unsqueeze(2).to_broadcast([nt, 16, 64]), op=ALU.subtract), reads=["sm", "BIGA"], writes=["BIGA"])
            RH3 = RHSs[:nt, :].rearrange("p (h v) -> p h v", h=16)
            P.op("pool", lambda e: e.tensor_tensor(out=RH3, in0=y3, in1=y3, op=ALU.mult), reads=["BIGA"], writes=["xtok"])
            P.op("dve", lambda e: e.tensor_reduce(out=sm[:nt, 16:32], in_=RH3, axis=AX.X, op=ALU.add), reads=["xtok"], writes=["sm"])
            P.op("act", lambda e: e.activation(out=sm[:nt, 16:32], in_=sm[:nt, 16:32], func=AF.Sqrt, scale=1.0 / 64, bias=64e-5), reads=["sm"], writes=["sm"])
            P.op("dve", lambda e: e.reciprocal(out=sm[:nt, 16:32], in_=sm[:nt, 16:32]), reads=["sm"], writes=["sm"])
            P.op("dve", lambda e: e.tensor_tensor(out=y3, in0=y3, in1=sm[:nt, 16:32].unsqueeze(2).to_broadcast([nt, 16, 64]), op=ALU.mult), reads=["sm", "BIGA"], writes=["BIGA"])
            YT = RHSs.rearrange("p (c t) -> p c t", c=8)[:, :, :nt]
            for half in range(2):
                b = bank()
                for q in range(4):
                    c = half * 4 + q
                    P.op("pe", lambda e, q=q, c=c, b=b: e.transpose(out=ps[b][:, q * 128:q * 128 + nt], in_=Ys[:nt, c * 128:(c + 1) * 128], identity=ident[:nt, :nt]),
                         reads=["BIGA", "CM"], writes=[("ps", b)])
                yv = YT[:, half * 4:half * 4 + 4, :]; cs = slice(half * 4, half * 4 + 4)
                P.op("dve", lambda e, b=b, yv=yv, half=half: e.tensor_tensor(out=yv, in0=psv(b, 4, nt), in1=bc3(CF[:, 160 + half * 4:164 + half * 4], nt), op=ALU.mult),
                     reads=[("ps", b), "CF"], writes=["xtok"])
                P.op("dve", lambda e, yv=yv, half=half: e.tensor_tensor(out=yv, in0=yv, in1=bc3(CF[:, 168 + half * 4:172 + half * 4], nt), op=ALU.add), reads=["xtok", "CF"], writes=["xtok"])
                P.op("dve", lambda e, yv=yv, cs=cs: e.tensor_tensor(out=yv, in0=yv, in1=t5[:, cs, :], op=ALU.add), reads=["xtok", "BIGC"], writes=["xtok"])
                P.op("dve", lambda e, yv=yv, cs=cs: e.tensor_tensor(out=yv, in0=yv, in1=T3[:].rearrange("p (c t) -> p c t", c=8)[:, cs, :nt], op=ALU.mult), reads=["xtok", "xdt"], writes=["xtok"])
                P.op("dve", lambda e, yv=yv, half=half: e.tensor_tensor(out=yv, in0=yv, in1=gateT[:, 8 + half * 4:12 + half * 4, :nt], op=ALU.mult), reads=["xtok", "gateT"], writes=["xtok"])
                P.op("dve", lambda e, yv=yv, cs=cs: e.tensor_tensor(out=mgT[:, cs, :nt], in0=mgT[:, cs, :nt], in1=yv, op=ALU.add), reads=["xtok", "mgT"], writes=["mgT"])

        def mixer_out(nt):
            xres, xk = CUR[0], CUR[1]
            for blk in range(2):
                proj_tok(w_out, blk * 512, 512, mgT, "mgT", nt,
                         lambda b, n, blk=blk: P.op("dve", lambda e: e.tensor_tensor(out=xres[:nt, blk * 512:(blk + 1) * 512], in0=xres[:nt, blk * 512:(blk + 1) * 512],
                                                                                     in1=ps[b][:nt, :512], op=ALU.add), reads=[("ps", b), xk], writes=[xk]))

        def rwkv_chunk():
            nt = 128
            rT, kT, vT, t1, t2, t4, t5, t6, t7 = rwkv_pre(128, rwT[:, :, 0:128], rwT[:, :, 1:129])
            if SUB < 2:
                return
            for c in range(8):
                P.op("dve", lambda e, c=c: e.tensor_tensor_scan(out=T6[:, c * 128:(c + 1) * 128], data0=ONES, data1=T1[:, c * 128:(c + 1) * 128], initial=0.0,
                                                                op0=ALU.mult, op1=ALU.add), reads=["BIGA", "CM"], writes=["BIGA"])
            P.op("act", lambda e: e.activation(out=t7, in_=t6, func=AF.Exp), reads=["BIGA"], writes=["BIGC"])
            P.op("dve", lambda e: e.tensor_tensor(out=RA4[:, :, 0, :], in0=rT, in1=t7, op=ALU.mult), reads=["RWS", "BIGC"], writes=["RA"])
            P.op("dve", lambda e: e.tensor_tensor(out=t7, in0=t6, in1=t1, op=ALU.subtract), reads=["BIGA", "BIGC"], writes=["BIGC"])
            P.op("act", lambda e: e.activation(out=t7, in_=t7, func=AF.Exp), reads=["BIGC"], writes=["BIGC"])
            P.op("dve", lambda e: e.scalar_tensor_tensor(out=RA4[:, :, 1, :], in0=t4, scalar=-1.0, in1=t7, op0=ALU.mult, op1=ALU.mult), reads=["BIGB", "BIGC"], writes=["RA"])
            P.op("act", lambda e: e.activation(out=t7, in_=t6, func=AF.Exp, scale=-1.0), reads=["BIGA", "RA"], writes=["BIGC"])
            P.op("dve", lambda e: e.tensor_tensor(out=BK4[:, :, 0, :], in0=t2, in1=t7, op=ALU.mult), reads=["BIGB", "BIGC"], writes=["BK"])
            P.op("dve", lambda e: e.tensor_tensor(out=BK4[:, :, 1, :], in0=kT, in1=t7, op=ALU.mult), reads=["RWS", "BIGC"], writes=["BK"])
            P.op("dve", lambda e: e.tensor_tensor(out=t7, in0=t6, in1=t6[:, :, 127:128].to_broadcast([128, 8, 128]), op=ALU.subtract), reads=["BIGA", "BK"], writes=["BIGC"])
            P.op("act", lambda e: e.activation(out=t7, in_=t7, func=AF.Exp, scale=-1.0), reads=["BIGC"], writes=["BIGC"])
            P.op("dve", lambda e: e.tensor_tensor(out=BKH4[:, :, 0, :], in0=t2, in1=t7, op=ALU.mult), reads=["BIGB", "BIGC"], writes=["BKH"])
            P.op("dve", lambda e: e.tensor_tensor(out=BKH4[:, :, 1, :], in0=kT, in1=t7, op=ALU.mult), reads=["RWS", "BIGC"], writes=["BKH"])
            P.op("act", lambda e: e.activation(out=smw[:, 0:8], in_=t6[:, :, 127], func=AF.Exp), reads=["BIGA"], writes=["smw"])
            if SUB < 3:
                return
            for (src_fn, dst, dkey, skey) in ((lambda c: RWS3[:, 16 + c, :], vtok, "xdd", "RWS"), (lambda c: BKH4[:, c, 0, :], btok, "ysb", "BKH"),
                                               (lambda c: BKH4[:, c, 1, :], ktok, "junk", "BKH")):
                for half in range(2):
                    b = bank()
                    for q in range(4):
                        c = half * 4 + q
                        P.op("pe", lambda e, q=q, c=c, b=b, src_fn=src_fn: e.transpose(out=ps[b][:, q * 128:(q + 1) * 128], in_=src_fn(c), identity=ident),
                             reads=[skey, "CM"], writes=[("ps", b)])
                    P.op("act", lambda e, b=b, half=half, dst=dst: e.activation(out=dst[:, half * 512:(half + 1) * 512], in_=ps[b][:, :512], func=AF.Copy),
                         reads=[("ps", b)], writes=[dkey])
            if SUB < 4:
                return
            for qd in range(4):
                hs = [4 * qd + i for i in range(4)]
                for hh, h in enumerate(hs):
                    j = h // 2; r0 = (h % 2) * 64
                    b = bank()
                    P.op("pe", lambda e, b=b, j=j, r0=r0: e.matmul(ps[b][:, 0:256], lhsT=BK4[r0:r0 + 64, j, 0, :], rhs=RA[r0:r0 + 64, j * 256:(j + 1) * 256], start=True, stop=True),
                         reads=["BK", "RA"], writes=[("ps", b)])
                    P.op("pe", lambda e, b=b, j=j, r0=r0: e.matmul(ps[b][:, 256:512], lhsT=BK4[r0:r0 + 64, j, 1, :], rhs=RA[r0:r0 + 64, j * 256:(j + 1) * 256], start=True, stop=True),
                         reads=["BK", "RA"], writes=[("ps", b)])
                    P.op("dve", lambda e, b=b, hh=hh: e.tensor_tensor(out=ATq[:, hh, :, :], in0=ps[b][:, :512].rearrange("p (k t) -> p k t", k=4), in1=MASK4[:], op=ALU.mult),
                         reads=[("ps", b), "MASK4"], writes=["RWS"])
                if SUB2 < 2:
                    continue
                bP = [bank(), bank()]
                for hh, h in enumerate(hs):
                    j = h // 2; r0 = (h % 2) * 64; b = bP[hh % 2]; o = (hh // 2) * 128
                    P.op("pe", lambda e, b=b, j=j, r0=r0, o=o: e.matmul(ps[b][:, o:o + 128], lhsT=RA4[r0:r0 + 64, j, 1, :], rhs=BK4[r0:r0 + 64, j, 0, :], start=True, stop=True),
                         reads=["BK", "RA"], writes=[("ps", b)])
                MN0v = MN[0].rearrange("p (a two t) -> p a two t", two=2, t=128)
                for par in range(2):
                    P.op("dve", lambda e, par=par, bP=bP, MN0v=MN0v: e.tensor_tensor(out=MN0v[:, :, par, :], in0=v3(ps[bP[par]][:, :256], 2), in1=M1.unsqueeze(1).to_broadcast([128, 2, 128]), op=ALU.mult),
                         reads=[("ps", bP[par]), "CM"], writes=["RWS"])
                if os.environ.get("K_DBG") and qd < 2:
                    P.dma("sp", lambda e, qd=qd: e.dma_start(out=y_p[1280 + qd * 128:1408 + qd * 128, 0:512], in_=MN[0]), reads=["RWS"])
                if SUB3 >= 2:
                    P.op("dve", lambda e: e.tensor_tensor(out=v3(TT[0], 4), in0=ATq[:, :, 1, :], in1=ident.unsqueeze(1).to_broadcast([128, 4, 128]), op=ALU.add),
                         reads=["RWS", "CM"], writes=["convT"])
                mi = 0; ni = None; ti = 0
                if SUB2 < 3:
                    continue
                for lvl in range(6):
                    bM = bank()
                    for hh in range(4):
                        nl = ATq[:, hh, 1, :] if ni is None else NN[ni][:, hh * 128:(hh + 1) * 128]
                        P.op("pe", lambda e, bM=bM, hh=hh, nl=nl, mi=mi: e.matmul(ps[bM][:, hh * 128:(hh + 1) * 128], lhsT=nl, rhs=MN[mi][:, hh * 128:(hh + 1) * 128], start=True, stop=True),
                             reads=["RWS", "xn", "RWS"], writes=[("ps", bM)])
                    if lvl < 5:
                        bN = bank()
                        for hh in range(4):
                            nl = ATq[:, hh, 1, :] if ni is None else NN[ni][:, hh * 128:(hh + 1) * 128]
                            P.op("pe", lambda e, bN=bN, hh=hh, nl=nl, mi=mi: e.matmul(ps[bN][:, hh * 128:(hh + 1) * 128], lhsT=MN[mi][:, hh * 128:(hh + 1) * 128], rhs=nl, start=True, stop=True),
                                 reads=["RWS", "xn", "RWS"], writes=[("ps", bN)])
                        nn = 0 if ni is None else 1 - ni
                        P.op("dve", lambda e, bN=bN, nn=nn: e.tensor_copy(out=NN[nn], in_=ps[bN][:, :512]), reads=[("ps", bN)], writes=["xn"])
                        ni = nn
                    mn = 1 - mi
                    P.op("act", lambda e, bM=bM, mn=mn: e.activation(out=MN[mn], in_=ps[bM][:, :512], func=AF.Copy), reads=[("ps", bM)], writes=["RWS"])
                    mi = mn
                    bT = bank()
                    for hh in range(4):
                        P.op("pe", lambda e, bT=bT, hh=hh, mi=mi, ti=ti: e.matmul(ps[bT][:, hh * 128:(hh + 1) * 128], lhsT=MN[mi][:, hh * 128:(hh + 1) * 128],
                                                                                   rhs=TT[ti][:, hh * 128:(hh + 1) * 128], start=True, stop=True),
                             reads=["RWS", "convT"], writes=[("ps", bT)])
                    tn = 1 - ti
                    P.op("dve", lambda e, bT=bT, ti=ti, tn=tn: e.tensor_tensor(out=TT[tn], in0=TT[ti], in1=ps[bT][:, :512], op=ALU.add),
                         reads=[("ps", bT), "convT"], writes=["convT"])
                    ti = tn
                if SUB2 < 4:
                    continue
                if os.environ.get("K_DBG") and qd < 2:
                    P.dma("sp", lambda e, qd=qd, ti=ti: e.dma_start(out=y_p[1280 + qd * 128:1408 + qd * 128, 512:1024], in_=TT[ti]), reads=["convT"])
                    P.dma("sp", lambda e, qd=qd: e.dma_start(out=y_p[1536 + qd * 256:1664 + qd * 256, :], in_=RWS[:, 0:1024]), reads=["RWS"])
                    P.dma("sp", lambda e, qd=qd: e.dma_start(out=y_p[1664 + qd * 256:1792 + qd * 256, :], in_=RWS[:, 1024:2048]), reads=["RWS"])
                bR = [bank(), bank()]
                for hh, h in enumerate(hs):
                    j = h // 2; r0 = (h % 2) * 64; b = bR[hh % 2]; o = (hh // 2) * 64
                    P.op("pe", lambda e, b=b, o=o, j=j, r0=r0: e.matmul(ps[b][:, o:o + 64], lhsT=RA4[r0:r0 + 64, j, 1, :], rhs=Hst[r0:r0 + 64, j, :], start=True, stop=False),
                         reads=["RA", "Hst"], writes=[("ps", b)])
                    P.op("pe", lambda e, b=b, o=o, hh=hh, h=h: e.matmul(ps[b][:, o:o + 64], lhsT=ATq[:, hh, 3, :], rhs=vtok[:, h * 64:(h + 1) * 64], start=False, stop=True),
                         reads=["RWS", "xdd"], writes=[("ps", b)])
                for par in range(2):
                    P.op("act", lambda e, par=par, qd=qd, bR=bR: e.activation(out=RHSs[:, qd * 256:(qd + 1) * 256].rearrange("p (a two v) -> p a two v", two=2, v=64)[:, :, par, :],
                                                                       in_=v3(ps[bR[par]][:, :128], 2), func=AF.Copy), reads=[("ps", bR[par])], writes=["xtok"])
                if SUB2 < 5:
                    continue
                bU = bank()
                for hh, h in enumerate(hs):
                    P.op("pe", lambda e, bU=bU, hh=hh, h=h, ti=ti: e.matmul(ps[bU][:, hh * 64:(hh + 1) * 64], lhsT=TT[ti][:, hh * 128:(hh + 1) * 128], rhs=RHSs[:, h * 64:(h + 1) * 64], start=True, stop=True),
                         reads=["convT", "xtok"], writes=[("ps", bU)])
                P.op("act", lambda e, bU=bU, qd=qd: e.activation(out=Us[:, qd * 256:(qd + 1) * 256], in_=ps[bU][:, :256], func=AF.Copy), reads=[("ps", bU)], writes=["BIGA"])
                bY = [bank(), bank()]
                for hh, h in enumerate(hs):
                    j = h // 2; r0 = (h % 2) * 64; b = bY[hh % 2]; o = (hh // 2) * 64
                    P.op("pe", lambda e, b=b, o=o, j=j, r0=r0: e.matmul(ps[b][:, o:o + 64], lhsT=RA4[r0:r0 + 64, j, 0, :], rhs=Hst[r0:r0 + 64, j, :], start=True, stop=False),
                         reads=["RA", "Hst"], writes=[("ps", b)])
                    P.op("pe", lambda e, b=b, o=o, hh=hh, h=h: e.matmul(ps[b][:, o:o + 64], lhsT=ATq[:, hh, 0, :], rhs=Us[:, h * 64:(h + 1) * 64], start=False, stop=False),
                         reads=["RWS", "BIGA"], writes=[("ps", b)])
                    P.op("pe", lambda e, b=b, o=o, hh=hh, h=h: e.matmul(ps[b][:, o:o + 64], lhsT=ATq[:, hh, 2, :], rhs=vtok[:, h * 64:(h + 1) * 64], start=False, stop=True),
                         reads=["RWS", "xdd"], writes=[("ps", b)])
                for par in range(2):
                    P.op("act", lambda e, par=par, qd=qd, bY=bY: e.activation(out=Ys[:, qd * 256:(qd + 1) * 256].rearrange("p (a two v) -> p a two v", two=2, v=64)[:, :, par, :],
                                                                       in_=v3(ps[bY[par]][:, :128], 2), func=AF.Copy), reads=[("ps", bY[par])], writes=["BIGA"])
            if SUB < 5:
                return
            for half in range(2):
                b = bank()
                for q in range(4):
                    j = half * 4 + q
                    P.op("pe", lambda e, b=b, q=q, j=j: e.matmul(ps[b][:, q * 128:(q + 1) * 128], lhsT=btok[:, j * 128:(j + 1) * 128], rhs=Us[:, j * 128:(j + 1) * 128], start=True, stop=False),
                         reads=["ysb", "BIGA"], writes=[("ps", b)])
                    P.op("pe", lambda e, b=b, q=q, j=j: e.matmul(ps[b][:, q * 128:(q + 1) * 128], lhsT=ktok[:, j * 128:(j + 1) * 128], rhs=vtok[:, j * 128:(j + 1) * 128], start=False, stop=True),
                         reads=["junk", "xdd"], writes=[("ps", b)])
                for hp in range(2):
                    r0 = hp * 64
                    hv = Hst[r0:r0 + 64, half * 4:half * 4 + 4, :]
                    P.op("dve", lambda e, r0=r0, half=half, hv=hv: e.tensor_tensor(out=hv, in0=hv, in1=smw[r0:r0 + 64, half * 4:half * 4 + 4].unsqueeze(2).to_broadcast([64, 4, 64]), op=ALU.mult),
                         reads=["Hst", "smw"], writes=["Hst"])
                    P.op("dve", lambda e, b=b, r0=r0, hp=hp, hv=hv: e.tensor_tensor(out=hv, in0=hv, in1=v3(ps[b][r0:r0 + 64, :512], 4)[:, :, hp * 64:(hp + 1) * 64], op=ALU.add),
                         reads=[("ps", b), "Hst"], writes=["Hst"])
            if os.environ.get("K_DBG"):
                P.dma("sp", lambda e: e.dma_start(out=y_p[0:128, :], in_=RHSs), reads=["xtok"])
                P.dma("sp", lambda e: e.dma_start(out=y_p[128:256, :], in_=Us), reads=["BIGA"])
                P.dma("sp", lambda e: e.dma_start(out=y_p[256:384, :], in_=Ys), reads=["BIGA"])
                P.dma("sp", lambda e: e.dma_start(out=y_p[384:512, :], in_=vtok[:]), reads=["xdd"])
                P.dma("sp", lambda e: e.dma_start(out=y_p[512:640, :], in_=btok[:]), reads=["ysb"])
                P.dma("sp", lambda e: e.dma_start(out=y_p[640:768, :], in_=ktok[:]), reads=["junk"])
                P.dma("sp", lambda e: e.dma_start(out=y_p[768:896, :], in_=RA[:, 0:1024]), reads=["RA"])
                P.dma("sp", lambda e: e.dma_start(out=y_p[896:1024, :], in_=RA[:, 1024:2048]), reads=["RA"])
                P.dma("sp", lambda e: e.dma_start(out=y_p[1024:1152, :], in_=BK[:, 0:1024]), reads=["BK"])
                P.dma("sp", lambda e: e.dma_start(out=y_p[1152:1280, :], in_=BK[:, 1024:2048]), reads=["BK"])
            if SUB < 6:
                return
            rwkv_post(128)


        qT = gateT
        PRB = BIGA
        PTt = BIGB
        OSB = BIGC

        def attn_tail(nt):
            xres, xk = CUR[0], CUR[1]
            oT = OSB[:, 1024:2048].rearrange("p (c t) -> p c t", c=8)
            for half in range(2):
                b = bank()
                for q in range(4):
                    c = half * 4 + q
                    P.op("pe", lambda e, q=q, c=c, b=b: e.transpose(out=ps[b][:, q * 128:q * 128 + nt], in_=OSB[:nt, c * 128:(c + 1) * 128], identity=ident[:nt, :nt]),
                         reads=["BIGC", "CM"], writes=[("ps", b)])
                P.op("act", lambda e, b=b, half=half: e.activation(out=oT[:, half * 4:half * 4 + 4, :nt], in_=psv(b, 4, nt), func=AF.Copy), reads=[("ps", b)], writes=["BIGC"])
            for blk in range(2):
                proj_tok(w_mo, blk * 512, 512, oT, "BIGC", nt,
                         lambda b, n, blk=blk: P.op("dve", lambda e: e.tensor_tensor(out=xres[:nt, blk * 512:(blk + 1) * 512], in0=xres[:nt, blk * 512:(blk + 1) * 512],
                                                                                     in1=ps[b][:nt, :512], op=ALU.add), reads=[("ps", b), xk], writes=[xk]))

        def attn_prompt():
            xres, xk = CUR[0], CUR[1]
            nt = 128
            rmsnorm_T(xres, nt, 8, hT, xk, "hT")
            for blk in range(2):
                proj_fm(w_mq, blk * 512, 4, hT, "hT", nt,
                        lambda b, n, blk=blk: P.op("act", lambda e: e.activation(out=qT[:, blk * 4:blk * 4 + 4, :nt], in_=psv(b, 4, nt), func=AF.Copy), reads=[("ps", b)], writes=["gateT"]))
            sb_ = [bank(), bank()]
            for h in range(4):
                b = sb_[h // 2]; o = (h % 2) * 256
                for dc in range(2):
                    P.op("pe", lambda e, h=h, dc=dc, b=b, o=o: e.matmul(ps[b][:nt, o:o + 256], lhsT=qT[:, 2 * h + dc, :nt], rhs=KT[:, 2 * h + dc, :], start=(dc == 0), stop=(dc == 1)),
                         reads=["gateT", "KT"], writes=[("ps", b)])
            for hb in range(2):
                P.op("dve", lambda e, hb=hb, sb_=sb_: e.tensor_reduce(out=sm[:nt, hb * 2:hb * 2 + 2], in_=ps[sb_[hb]][:nt, :512].rearrange("p (h m) -> p h m", h=2), axis=AX.X, op=ALU.max),
                     reads=[("ps", sb_[hb])], writes=["sm"])
            P.op("dve", lambda e: e.tensor_scalar(out=sm[:nt, 4:8], in0=sm[:nt, 0:4], scalar1=-1.0 / 16, scalar2=None, op0=ALU.mult), reads=["sm"], writes=["sm"])
            for h in range(4):
                b = sb_[h // 2]; o = (h % 2) * 256
                P.op("act", lambda e, h=h, b=b, o=o: e.activation(out=PRB[:nt, h * 256:(h + 1) * 256], in_=ps[b][:nt, o:o + 256], func=AF.Exp, scale=1.0 / 16, bias=sm[:nt, 4 + h:5 + h],
                                                                  accum_out=sm[:nt, 8 + h:9 + h]), reads=[("ps", b), "sm"], writes=["BIGA", "sm"])
            P.op("dve", lambda e: e.reciprocal(out=sm[:nt, 12:16], in_=sm[:nt, 8:12]), reads=["sm"], writes=["sm"])
            PT3 = PTt[:].rearrange("p (c t) -> p c t", c=16)
            for half in range(2):
                b = bank()
                for q in range(4):
                    c = half * 4 + q
                    P.op("pe", lambda e, q=q, c=c, b=b: e.transpose(out=ps[b][:, q * 128:q * 128 + nt], in_=PRB[:nt, c * 128:(c + 1) * 128], identity=ident[:nt, :nt]),
                         reads=["BIGA", "CM"], writes=[("ps", b)])
                P.op("act", lambda e, b=b, half=half: e.activation(out=PT3[:, half * 4:half * 4 + 4, :nt], in_=psv(b, 4, nt), func=AF.Copy), reads=[("ps", b)], writes=["BIGB"])
            ob = [bank(), bank()]
            for h in range(4):
                b = ob[h // 2]; o = (h % 2) * 256
                for mb in range(2):
                    P.op("pe", lambda e, h=h, mb=mb, b=b, o=o: e.matmul(ps[b][:nt, o:o + 256], lhsT=PT3[:, h * 2 + mb, :nt], rhs=Vt[:, mb, h * 256:(h + 1) * 256], start=(mb == 0), stop=(mb == 1)),
                         reads=["BIGB", "Vt"], writes=[("ps", b)])
            for h in range(4):
                b = ob[h // 2]; o = (h % 2) * 256
                P.op("act", lambda e, h=h, b=b, o=o: e.activation(out=OSB[:nt, h * 256:(h + 1) * 256], in_=ps[b][:nt, o:o + 256], func=AF.Copy, scale=sm[:nt, 12 + h:13 + h]),
                     reads=[("ps", b), "sm"], writes=["BIGC"])
            attn_tail(nt)


        NEG = -1.0e30

        def fence(keys):
            P.op("pool", lambda e: e.memset(st1[:, 7:8], 0.0), reads=[], writes=list(keys) + ["st1"])

        def top16(nt, vals_in, key_in, nsets, width, Vout, Iout):
            for r in range(2):
                for s_ in range(nsets):
                    src = vals_in[:nt, s_, :]; vo = Vout[:nt, s_, r * 8:(r + 1) * 8]
                    P.op("dve", lambda e, vo=vo, src=src: e.max(out=vo, in_=src), reads=[key_in, ("tk_in", s_), "pk_v"], writes=[("tk_v", s_, r)])
                for s_ in range(nsets):
                    src = vals_in[:nt, s_, :]; vo = Vout[:nt, s_, r * 8:(r + 1) * 8]; io = Iout[:nt, s_, r * 8:(r + 1) * 8]
                    P.op("dve", lambda e, vo=vo, io=io, src=src: e.max_index(out=io, in_max=vo, in_values=src), reads=[key_in, ("tk_in", s_), ("tk_v", s_, r), "pk_i"], writes=[("tk_i", s_, r)])
                if r == 0:
                    for s_ in range(nsets):
                        src = vals_in[:nt, s_, :]; vo = Vout[:nt, s_, 0:8]
                        P.op("dve", lambda e, vo=vo, src=src: e.match_replace(out=src, in_to_replace=vo, in_values=src, imm_value=NEG),
                             reads=[key_in, ("tk_v", s_, 0), ("tk_i", s_, 0)], writes=[("tk_in", s_)])
            fine = [("tk_in", s_) for s_ in range(nsets)] + [("tk_v", s_, r) for s_ in range(nsets) for r in range(2)] + [("tk_i", s_, r) for s_ in range(nsets) for r in range(2)]
            P.op("pool", lambda e: e.memset(st1[:, 6:7], 0.0), reads=fine, writes=fine + [key_in, "pk_v", "pk_i"])

        def peer(nt):
            xres, xk = CUR[0], CUR[1]
            SM = xdd
            V1 = SM[:, 0:256].rearrange("p (s k) -> p s k", s=16); I1 = SM[:, 256:512].bitcast(U32).rearrange("p (s k) -> p s k", s=16)
            I1f = SM[:, 512:768].rearrange("p (s k) -> p s k", s=16)
            TV = SM[:, 768:896].rearrange("p (h k) -> p h k", h=8); TP = SM[:, 896:1024].bitcast(U32).rearrange("p (h k) -> p h k", h=8)
            S2 = ysb
            TPf = S2[:, 0:128].rearrange("p (h k) -> p h k", h=8); Af = S2[:, 128:256].rearrange("p (h k) -> p h k", h=8)
            Bf = S2[:, 256:384].rearrange("p (h k) -> p h k", h=8); E1 = S2[:, 384:512].rearrange("p (h k) -> p h k", h=8)
            E2 = S2[:, 512:640].rearrange("p (h k) -> p h k", h=8); IDX = S2[:, 640:768].bitcast(U32)
            GATE = S2[:, 768:896].rearrange("p (h k) -> p h k", h=8); ACTV = S2[:, 896:1024]
            fence(["xdd", "ysb", "pk_v", "pk_i", "pk_s"])
            rmsnorm_T(xres, nt, 16, hT, xk, "hT")
            H3 = xdt
            for half in range(2):
                b = bank()
                for q in range(4):
                    c = half * 4 + q
                    P.op("pe", lambda e, q=q, c=c, b=b: e.transpose(out=ps[b][:nt, q * 128:(q + 1) * 128], in_=hT[:, c, :nt], identity=ident), reads=["hT", "CM"], writes=[("ps", b)])
                P.op("act", lambda e, b=b, half=half: e.activation(out=H3[:nt, half * 512:(half + 1) * 512], in_=ps[b][:nt, :512], func=AF.Copy), reads=[("ps", b)], writes=["xdt"])
            QT = gateT
            for blk in range(4):
                proj_fm(w_pq, blk * 512, 4, hT, "hT", nt,
                        lambda b, n, blk=blk: P.op("act", lambda e: e.activation(out=QT[:, blk * 4:blk * 4 + 4, :nt], in_=psv(b, 4, nt), func=AF.Copy), reads=[("ps", b)], writes=["gateT"]))
            SKr = BIGA[:].rearrange("p (s d) -> p s d", s=16); SKT = BIGB[:].rearrange("p (s n) -> p s n", s=16)
            P.dma("sp", lambda e: e.dma_start(out=SKr, in_=sub_keys.rearrange("s n d -> n s d")), writes=["BIGA"])
            for grp in range(4):
                b = bank()
                for q in range(4):
                    hc = grp * 4 + q
                    P.op("pe", lambda e, q=q, hc=hc, b=b: e.transpose(out=ps[b][:, q * 128:(q + 1) * 128], in_=SKr[:, hc, :], identity=ident), reads=["BIGA", "CM"], writes=[("ps", b)])
                P.op("act", lambda e, b=b, grp=grp: e.activation(out=SKT[:, grp * 4:grp * 4 + 4, :], in_=psv(b, 4, 128), func=AF.Copy), reads=[("ps", b)], writes=["BIGB"])
            SC = BIGC[:].rearrange("p (s n) -> p s n", s=16)
            for grp in range(4):
                b = bank()
                for q in range(4):
                    hc = grp * 4 + q
                    P.op("pe", lambda e, q=q, hc=hc, b=b: e.matmul(ps[b][:nt, q * 128:(q + 1) * 128], lhsT=QT[:, hc, :nt], rhs=SKT[:, hc, :], start=True, stop=True),
                         reads=["gateT", "BIGB"], writes=[("ps", b)])
                P.op("act", lambda e, b=b, grp=grp: e.activation(out=SC[:nt, grp * 4:grp * 4 + 4, :], in_=ps[b][:nt, :512].rearrange("p (q n) -> p q n", q=4), func=AF.Copy),
                     reads=[("ps", b)], writes=["BIGC"])
            top16(nt, SC, "BIGC", 16, 128, V1, I1)
            P.op("dve", lambda e: e.tensor_copy(out=I1f[:nt], in_=I1[:nt]), reads=["pk_i"], writes=["pk_s"])
            V1h = SM[:, 0:256].rearrange("p (h c k) -> p h c k", h=8, c=2)
            CAND = BIGA[:].rearrange("p (h a b) -> p h a b", h=8, a=16)
            P.op("dve", lambda e: e.tensor_tensor(out=CAND[:nt], in0=V1h[:nt, :, 0, :].unsqueeze(3).to_broadcast([nt, 8, 16, 16]),
                                                  in1=V1h[:nt, :, 1, :].unsqueeze(2).to_broadcast([nt, 8, 16, 16]), op=ALU.add), reads=["pk_v"], writes=["BIGA"])
            top16(nt, BIGA[:].rearrange("p (h c) -> p h c", h=8), "BIGA", 8, 256, TV, TP)
            Au = S2[:, 384:512].bitcast(U32).rearrange("p (h k) -> p h k", h=8); Bu = S2[:, 512:640].bitcast(U32).rearrange("p (h k) -> p h k", h=8)
            P.op("dve", lambda e: e.tensor_scalar(out=Au[:nt], in0=TP[:nt], scalar1=4, scalar2=None, op0=ALU.logical_shift_right), reads=["pk_i"], writes=["pk_s"])
            P.op("dve", lambda e: e.tensor_scalar(out=Bu[:nt], in0=TP[:nt], scalar1=15, scalar2=None, op0=ALU.bitwise_and), reads=["pk_i"], writes=["pk_s"])
            P.op("dve", lambda e: e.tensor_copy(out=Af[:nt], in_=Au[:nt]), reads=["pk_s"], writes=["pk_s"])
            P.op("dve", lambda e: e.tensor_copy(out=Bf[:nt], in_=Bu[:nt]), reads=["pk_s"], writes=["pk_s"])
            I1h = SM[:, 512:768].rearrange("p (h c k) -> p h c k", h=8, c=2)
            OH = BIGB[:].rearrange("p (h k a) -> p h k a", h=8, k=16)
            for (sel, cc, Eo) in ((Af, 0, E1), (Bf, 1, E2)):
                P.op("dve", lambda e, sel=sel: e.tensor_tensor(out=OH[:nt], in0=sel[:nt].unsqueeze(3).to_broadcast([nt, 8, 16, 16]),
                                                               in1=IOTA[:nt].unsqueeze(1).unsqueeze(1).to_broadcast([nt, 8, 16, 16]), op=ALU.is_equal), reads=["pk_s", "CM"], writes=["BIGB"])
                P.op("dve", lambda e, cc=cc: e.tensor_tensor(out=OH[:nt], in0=OH[:nt], in1=I1h[:nt, :, cc, :].unsqueeze(2).to_broadcast([nt, 8, 16, 16]), op=ALU.mult),
                     reads=["BIGB", "pk_s"], writes=["BIGB"])
                P.op("dve", lambda e, Eo=Eo: e.tensor_reduce(out=Eo[:nt], in_=OH[:nt], axis=AX.X, op=ALU.add), reads=["BIGB"], writes=["pk_s"])
            P.op("dve", lambda e: e.scalar_tensor_tensor(out=E1[:nt], in0=E1[:nt], scalar=128.0, in1=E2[:nt], op0=ALU.mult, op1=ALU.add), reads=["pk_s"], writes=["pk_s"])
            P.op("dve", lambda e: e.tensor_copy(out=IDX[:nt, :], in_=S2[:nt, 384:512]), reads=["pk_s"], writes=["pk_idx"])
            P.op("dve", lambda e: e.tensor_tensor(out=GATE[:nt], in0=TV[:nt], in1=TV[:nt, :, 0:1].to_broadcast([nt, 8, 16]), op=ALU.subtract), reads=["pk_v"], writes=["pk_s"])
            P.op("act", lambda e: e.activation(out=GATE[:nt], in_=GATE[:nt], func=AF.Exp), reads=["pk_s"], writes=["pk_s"])
            P.op("dve", lambda e: e.tensor_reduce(out=sm[:nt, 0:8], in_=GATE[:nt], axis=AX.X, op=ALU.add), reads=["pk_s"], writes=["sm"])
            P.op("dve", lambda e: e.reciprocal(out=sm[:nt, 8:16], in_=sm[:nt, 0:8]), reads=["sm"], writes=["sm"])
            P.op("dve", lambda e: e.tensor_tensor(out=GATE[:nt], in0=GATE[:nt], in1=sm[:nt, 8:16].unsqueeze(2).to_broadcast([nt, 8, 16]), op=ALU.mult), reads=["pk_s", "sm"], writes=["pk_s"])
            def apply():
                GB = [(RA[:, 0:1024], ("g", 0)), (RA[:, 1024:2048], ("g", 1)), (BK[:, 0:1024], ("g", 2)), (BK[:, 1024:2048], ("g", 3)),
                      (BKH[:, 0:1024], ("g", 4)), (BKH[:, 1024:2048], ("g", 5))]
                DUM = BIGB[:].bitcast(BF16)[:, 0:1024]
                fence(["RA", "BK", "BKH", "BIGB"] + [k for _, k in GB] + [("dum", i_) for i_ in range(4)])
                gi = 0
                for hk in range(128):
                    buf, gk = GB[gi % len(GB)]; gi += 1
                    P.dma("pool", lambda e, buf=buf, hk=hk: e.indirect_dma_start(out=buf[:nt, :], out_offset=None, in_=exp_u,
                                                                                  in_offset=bass.IndirectOffsetOnAxis(ap=IDX[:nt, hk:hk + 1], axis=0)),
                          reads=["pk_idx"], writes=[gk])
                    P.op("dve", lambda e, buf=buf, hk=hk: e.scalar_tensor_tensor(out=DUM[:nt, :], in0=buf[:nt, :], scalar=1.0, in1=H3[:nt, :], op0=ALU.mult, op1=ALU.mult,
                                                                                accum_out=ACTV[:nt, hk:hk + 1]), reads=[gk, "xdt"], writes=[("dum", hk % 4), ("pa", hk)])
                COEF = ACTV
                P.op("act", lambda e: e.activation(out=COEF[:nt, :], in_=ACTV[:nt, :], func=AF.Gelu), reads=[("pa", k_) for k_ in range(128)], writes=["pk_a"])
                P.op("dve", lambda e: e.tensor_tensor(out=COEF[:nt, :], in0=COEF[:nt, :], in1=S2[:nt, 768:896], op=ALU.mult), reads=["pk_a", "pk_s"], writes=["pk_a"])
                ACC = junk
                P.op("pool", lambda e: e.memset(ACC[:nt, :], 0.0), writes=["junk"])
                for hk in range(128):
                    buf, gk = GB[gi % len(GB)]; gi += 1
                    P.dma("pool", lambda e, buf=buf, hk=hk: e.indirect_dma_start(out=buf[:nt, :], out_offset=None, in_=exp_v,
                                                                                  in_offset=bass.IndirectOffsetOnAxis(ap=IDX[:nt, hk:hk + 1], axis=0)),
                          reads=["pk_idx"], writes=[gk])
                    P.op("dve", lambda e, buf=buf, hk=hk: e.scalar_tensor_tensor(out=ACC[:nt, :], in0=buf[:nt, :], scalar=COEF[:nt, hk:hk + 1], in1=ACC[:nt, :], op0=ALU.mult, op1=ALU.add),
                         reads=[gk, "pk_a", "junk"], writes=["junk"])
                P.op("dve", lambda e: e.tensor_tensor(out=xres[:nt, :], in0=xres[:nt, :], in1=ACC[:nt, :], op=ALU.add), reads=["junk", xk], writes=[xk])
                fence(["RA", "BK", "BKH", "BIGB", "xdd", "ysb"] + [("dum", i_) for i_ in range(4)] + [("pa", k_) for k_ in range(128)] + [k for _, k in GB] + ["pk_v", "pk_i", "pk_s", "pk_idx", "pk_a"])

            return apply

        def final_out(nt, dst):
            xres, xk = CUR[0], CUR[1]
            P.op("act", lambda e: e.activation(out=junk[:nt, :], in_=xres[:nt, :], func=AF.Square, accum_out=st1[:nt, 0:1]), reads=[xk], writes=["junk", "st1"])
            P.op("act", lambda e: e.activation(out=st1[:nt, 1:2], in_=st1[:nt, 0:1], func=AF.Sqrt, scale=1.0 / D, bias=EPS), reads=["st1"], writes=["st1"])
            P.op("dve", lambda e: e.reciprocal(out=st1[:nt, 2:3], in_=st1[:nt, 1:2]), reads=["st1"], writes=["st1"])
            P.op("dve", lambda e: e.scalar_tensor_tensor(out=xn[:nt, :], in0=xres[:nt, :], scalar=st1[:nt, 2:3], in1=CT[:nt, 48:48 + D], op0=ALU.mult, op1=ALU.mult),
                 reads=["st1", xk, "CT"], writes=["xn"])
            P.dma("sp", lambda e: e.dma_start(out=dst, in_=xn[:nt, :]), reads=["xn"])


        def attn_sample():
            xres, xk = CUR[0], CUR[1]
            nt = NS
            rmsnorm_T(xres, nt, 8, hT, xk, "hT")
            QS = xdt
            for blk in range(2):
                proj_tok(w_mq, blk * 512, 512, hT, "hT", nt,
                         lambda b, n, blk=blk: P.op("act", lambda e: e.activation(out=QS[:nt, blk * 512:(blk + 1) * 512], in_=ps[b][:nt, :512], func=AF.Copy), reads=[("ps", b)], writes=["xdt"]))
            SELa = BK[:16, 0:2048].rearrange("p (s m) -> p s m", s=16)
            P.op("dve", lambda e: e.tensor_copy(out=SELa, in_=ident[:16, :16].unsqueeze(2).to_broadcast([16, 16, 128])), reads=["CM"], writes=["BK"])
            SCs = xdd[:, 0:128]; PRs = xdd[:, 256:512]; PT2 = xdd[:, 512:640]; OH16 = xdd[:, 640:896].rearrange("p (a b) -> p a b", a=16)
            PZ = RA[:].rearrange("p (s mb h t) -> p s mb h t", s=16, mb=2, h=4)
            PR = BIGC
            for si in range(NS):
                Kb, kkey = (BIGA, "BIGA") if si % 2 == 0 else (BIGB, "BIGB")
                P.dma("sp", lambda e, si=si, Kb=Kb: e.dma_start(out=Kb[:].rearrange("p (mb d) -> p mb d", mb=2), in_=ck[si].rearrange("(mb p) d -> p mb d", p=128)), writes=[kkey])
                qb = [bank(), bank()]
                for half in range(2):
                    P.op("pe", lambda e, half=half, qb=qb, si=si: e.matmul(ps[qb[half]][:, :512], lhsT=SELa[:, si, :], rhs=QS[:nt, half * 512:(half + 1) * 512], start=True, stop=True),
                         reads=["BK", "xdt"], writes=[("ps", qb[half])])
                for mb in range(2):
                    for half in range(2):
                        P.op("dve", lambda e, mb=mb, half=half, qb=qb, Kb=Kb: e.tensor_tensor(out=PR[:, mb * 1024 + half * 512:mb * 1024 + (half + 1) * 512],
                                                                                             in0=Kb[:, mb * 1024 + half * 512:mb * 1024 + (half + 1) * 512], in1=ps[qb[half]][:, :512], op=ALU.mult),
                             reads=[kkey, ("ps", qb[half])], writes=["BIGC"])
                P.op("dve", lambda e, si=si: e.tensor_reduce(out=SCs.rearrange("p (mb s h) -> p mb s h", mb=2, s=16)[:, :, si, :], in_=PR[:].rearrange("p (mb h d) -> p mb h d", mb=2, h=4),
                                                             axis=AX.X, op=ALU.add), reads=["BIGC"], writes=["xdd"])
            b = bank()
            for mb in range(2):
                P.op("pe", lambda e, mb=mb, b=b: e.transpose(out=ps[b][:64, mb * 128:(mb + 1) * 128], in_=SCs[:, mb * 64:(mb + 1) * 64], identity=ident), reads=["xdd", "CM"], writes=[("ps", b)])
            P.op("dve", lambda e, b=b: e.tensor_reduce(out=sm[:64, 0:1], in_=ps[b][:64, :256], axis=AX.X, op=ALU.max), reads=[("ps", b)], writes=["sm"])
            P.op("dve", lambda e: e.tensor_scalar(out=sm[:64, 1:2], in0=sm[:64, 0:1], scalar1=-1.0 / 16, scalar2=None, op0=ALU.mult), reads=["sm"], writes=["sm"])
            P.op("act", lambda e, b=b: e.activation(out=PRs[:64, :], in_=ps[b][:64, :256], func=AF.Exp, scale=1.0 / 16, bias=sm[:64, 1:2], accum_out=sm[:64, 2:3]),
                 reads=[("ps", b), "sm"], writes=["xdd", "sm"])
            P.op("dve", lambda e: e.reciprocal(out=sm[:64, 3:4], in_=sm[:64, 2:3]), reads=["sm"], writes=["sm"])
            P.op("dve", lambda e: e.tensor_scalar(out=PRs[:64, :], in0=PRs[:64, :], scalar1=sm[:64, 3:4], scalar2=None, op0=ALU.mult), reads=["sm", "xdd"], writes=["xdd"])
            b = bank()
            for mb in range(2):
                P.op("pe", lambda e, mb=mb, b=b: e.transpose(out=ps[b][:, mb * 64:(mb + 1) * 64], in_=PRs[:64, mb * 128:(mb + 1) * 128], identity=ident[:64, :64]), reads=["xdd", "CM"], writes=[("ps", b)])
            P.op("act", lambda e, b=b: e.activation(out=PT2, in_=ps[b][:, :128], func=AF.Copy), reads=[("ps", b)], writes=["xdd"])
            P.op("dve", lambda e: e.tensor_tensor(out=OH16, in0=IOTA.unsqueeze(2).to_broadcast([128, 16, 16]), in1=IOTA.unsqueeze(1).to_broadcast([128, 16, 16]), op=ALU.is_equal),
                 reads=["CM"], writes=["xdd"])
            PT4 = PT2.rearrange("p (mb s h) -> p mb s h", mb=2, s=16)
            for mb in range(2):
                for h in range(4):
                    P.op("dve", lambda e, mb=mb, h=h: e.tensor_tensor(out=PZ[:, :, mb, h, :], in0=PT4[:, mb, :, h].unsqueeze(1).to_broadcast([128, 16, 16]), in1=OH16, op=ALU.mult),
                         reads=["xdd"], writes=["RA"])
            ob = [bank(), bank(), bank(), bank()]
            for si in range(NS):
                Vb, vkey = (BIGA, "BIGA") if si % 2 == 0 else (BIGB, "BIGB")
                P.dma("sp", lambda e, si=si, Vb=Vb: e.dma_start(out=Vb[:].rearrange("p (mb d) -> p mb d", mb=2), in_=cv[si].rearrange("(mb p) d -> p mb d", p=128)), writes=[vkey])
                for h in range(4):
                    for mb in range(2):
                        P.op("pe", lambda e, si=si, h=h, mb=mb, ob=ob, Vb=Vb: e.matmul(ps[ob[h]][:nt, 0:256], lhsT=PZ[:, si, mb, h, :],
                                                                                       rhs=Vb[:, mb * 1024 + h * 256:mb * 1024 + (h + 1) * 256],
                                                                                       start=(si == 0 and mb == 0), stop=(si == NS - 1 and mb == 1)),
                             reads=["RA", vkey], writes=[("ps", ob[h])])
            for h in range(4):
                P.op("act", lambda e, h=h, ob=ob: e.activation(out=OSB[:nt, h * 256:(h + 1) * 256], in_=ps[ob[h]][:nt, 0:256], func=AF.Copy), reads=[("ps", ob[h])], writes=["BIGC"])
            attn_tail(nt)


        def store_T(src_fn, nchunks, nrows, dst, skey):
            done = 0
            for (stg, stkey, cap) in ((BIGA, "BIGA", 16), (BIGB, "BIGB", 16)):
                n_here = min(cap, nchunks - done)
                if n_here <= 0:
                    break
                for g in range(0, n_here, 4):
                    n = min(4, n_here - g)
                    b = bank()
                    for q in range(n):
                        c = done + g + q
                        P.op("pe", lambda e, q=q, c=c, b=b: e.transpose(out=ps[b][:nrows, q * 128:(q + 1) * 128], in_=src_fn(c), identity=ident), reads=[skey, "CM"], writes=[("ps", b)])
                    P.op("act", lambda e, b=b, g=g, n=n, stg=stg: e.activation(out=stg[:nrows, g * 128:(g + n) * 128], in_=ps[b][:nrows, :n * 128], func=AF.Copy), reads=[("ps", b)], writes=[stkey])
                P.dma("sp", lambda e, stg=stg, done=done, n_here=n_here: e.dma_start(out=dst[:, done * 128:(done + n_here) * 128], in_=stg[:nrows, 0:n_here * 128]), reads=[stkey])
                done += n_here

        mT = BIGA[:].rearrange("p (c t) -> p c t", c=8)
        KT = sb("KT", [128, 8, 256]); Vt = sb("Vt", [128, 2, D])
        osb = sb("osb", [128, 512])
        for mb in range(2):
            P.dma("sp", lambda e, mb=mb: e.dma_start(out=xres[:], in_=memp[mb * 128:(mb + 1) * 128, :]), writes=["xres"])
            tmpT = hT
            rmsnorm_T(xres, 128, 24, tmpT, "xres", "hT")
            P.op("dve", lambda e, mb=mb: e.tensor_copy(out=mT[:, :, mb * 128:(mb + 1) * 128], in_=hT[:]), reads=["hT"], writes=["BIGA"])
        for (wd, od) in ((w_mk, mk_p), (w_mv, mv_p)):
            for blk in range(2):
                i = load_w(wd, blk * 512, 512)
                for mb in range(2):
                    b = bank()
                    for kc in range(8):
                        P.op("pe", lambda e, kc=kc, mb=mb, b=b, i=i: e.matmul(ps[b][:, :512], lhsT=mT[:, kc, mb * 128:(mb + 1) * 128], rhs=WB[i][:, kc, :],
                                                                               start=(kc == 0), stop=(kc == 7)),
                             reads=[("wb", i), "BIGA"], writes=[("ps", b)])
                    P.op("act", lambda e, b=b: e.activation(out=osb[:], in_=ps[b][:, :512], func=AF.Copy), reads=[("ps", b)], writes=["osb"])
                    if wd is w_mv:
                        P.op("pool", lambda e, mb=mb, blk=blk: e.tensor_copy(out=Vt[:, mb, blk * 512:(blk + 1) * 512], in_=osb[:]), reads=["osb"], writes=["Vt"])
                    P.dma("sp", lambda e, od=od, mb=mb, blk=blk: e.dma_start(out=od[mb * 128:(mb + 1) * 128, blk * 512:(blk + 1) * 512], in_=osb[:]),
                          reads=["osb"])

        for blk in range(0 if os.environ.get("K_NOKT") else 2):
            i = load_w(w_mk, blk * 512, 512)
            for half in range(2):
                b = bank()
                for q in range(2):
                    cc = half * 2 + q
                    for kc in range(8):
                        P.op("pe", lambda e, kc=kc, q=q, cc=cc, b=b, i=i: e.matmul(ps[b][:, q * 256:(q + 1) * 256], lhsT=WB[i][:, kc, cc * 128:(cc + 1) * 128], rhs=mT[:, kc, :],
                                                                                   start=(kc == 0), stop=(kc == 7)), reads=[("wb", i), "BIGA"], writes=[("ps", b)])
                P.op("act", lambda e, b=b, blk=blk, half=half: e.activation(out=KT[:, blk * 4 + half * 2:blk * 4 + half * 2 + 2, :], in_=ps[b][:, :512].rearrange("p (q m) -> p q m", q=2), func=AF.Copy),
                     reads=[("ps", b)], writes=["KT"])

        P.op("pool", lambda e: e.memset(xbcT[:, :, 0:3], 0.0), writes=["xbcT"])
        P.op("pool", lambda e: e.memset(rwT[:, :, 0:1], 0.0), writes=["rwT"])
        P.op("pool", lambda e: e.memset(stT[:], 0.0), writes=["stT"])
        nch = int(os.environ.get('K_NCH', NCH))
        XRS = [[xres, "xres"], [xres2, "xres2"]]

        def stage_a(ch):
            xr, xk_ = CUR[0], CUR[1]
            P.dma("sp", lambda e, ch=ch, xr=xr: e.dma_start(out=xr[:], in_=xp[ch * 128:(ch + 1) * 128, :]), writes=[xk_])
            rmsnorm_T(xr, 128, 0, hT, xk_, "hT")
            in_proj(128, 3, 1)

        CUR[:] = XRS[0]
        stage_a(0)
        for ch in range(nch):
            mine = XRS[ch % 2]; other = XRS[(ch + 1) % 2]
            CUR[:] = mine
            ssd_chunk()
            rwkv_chunk()
            mixer_out(128)
            P.op("dve", lambda e: e.tensor_copy(out=xbcT[:, :, 0:3], in_=xbcT[:, :, 128:131]), reads=["xbcT"], writes=["xbcT"])
            P.op("dve", lambda e: e.tensor_copy(out=rwT[:, :, 0:1], in_=rwT[:, :, 128:129]), reads=["rwT"], writes=["rwT"])
            attn_prompt()
            ap = peer(128)
            if ch + 1 < nch:
                CUR[:] = other
                stage_a(ch + 1)
                CUR[:] = mine
            ap()
            final_out(128, y_p[ch * 128:(ch + 1) * 128, :])
        CUR[:] = XRS[0]
        store_T(lambda c: xbcT[:, c, 0:3], 12, 3, conv_p, "xbcT")
        store_T(lambda c: rwT[:, c, 0:1], 26, 1, shift_p, "rwT")


        if stage >= 4:
            SCT = BKH[:, 0:576].rearrange("p (c t) -> p c t", c=12)
            YST = convT[:, 0:8, 16:32]; YRT = convT[:, 0:8, 32:48]
            SEL = BK[:16, 0:2048].rearrange("p (s m) -> p s m", s=16)
            P.dma("sp", lambda e: e.dma_start(out=xres[:16, :], in_=xs_in), writes=["xres"])
            rmsnorm_T(xres, 16, 0, hT, "xres", "hT")
            in_proj(16, 3, 1)
            P.dma("sp", lambda e: e.dma_start(out=conv_s[:, 0:2, :], in_=st_conv[:, 1:3, :]))
            store_T(lambda c: xbcT[:, c, 3:19], 12, NS, conv_s[:, 2, :], "xbcT")
            store_T(lambda c: rwT[:, c, 1:17], 26, NS, shift_s, "rwT")
            P.dma("sp", lambda e: e.dma_start(out=ysb[:48, :], in_=st_conv.rearrange("s k c -> (s k) c")[:, 0:1024]), writes=["ysb"])
            P.dma("sp", lambda e: e.dma_start(out=xdd[:48, 0:512], in_=st_conv.rearrange("s k c -> (s k) c")[:, 1024:1536]), writes=["xdd"])
            for grp in range(3):
                b = bank()
                for q in range(4):
                    c = grp * 4 + q
                    src = ysb[:48, c * 128:(c + 1) * 128] if c < 8 else xdd[:48, (c - 8) * 128:(c - 7) * 128]
                    P.op("pe", lambda e, q=q, b=b, src=src: e.transpose(out=ps[b][:, q * 48:(q + 1) * 48], in_=src, identity=ident[:48, :48]),
                         reads=["ysb", "xdd", "CM"], writes=[("ps", b)])
                P.op("act", lambda e, b=b, grp=grp: e.activation(out=SCT[:, grp * 4:(grp + 1) * 4, :], in_=ps[b][:, :192].rearrange("p (q t) -> p q t", q=4), func=AF.Copy),
                     reads=[("ps", b)], writes=["BKH"])
            A = BIGA[:, :192].rearrange("p (c t) -> p c t", c=12); Bv = BIGB[:, :192].rearrange("p (c t) -> p c t", c=12)
            SC4 = SCT.rearrange("p c (s k) -> p c s k", k=3)
            P.op("dve", lambda e: e.tensor_tensor(out=A, in0=xbcT[:, :, 3:19], in1=CWv[:, :, 3:4].to_broadcast([128, 12, 16]), op=ALU.mult), reads=["xbcT", "CF"], writes=["BIGA"])
            for k in range(3):
                P.op("dve", lambda e, k=k: e.tensor_tensor(out=Bv, in0=SC4[:, :, :, k], in1=CWv[:, :, k:k + 1].to_broadcast([128, 12, 16]), op=ALU.mult), reads=["BKH", "CF"], writes=["BIGB"])
                P.op("dve", lambda e: e.tensor_tensor(out=A, in0=A, in1=Bv, op=ALU.add), reads=["BIGA", "BIGB"], writes=["BIGA"])
            for c in range(12):
                P.op("act", lambda e, c=c: e.activation(out=convT[:, c, :16], in_=A[:, c, :], func=AF.Silu, bias=CF[:, 80 + c:81 + c]), reads=["BIGA", "CF"], writes=["convT"])
            dt_softplus(16)
            xS = xdt
            xS2 = xn
            for grp in range(3):
                b = bank()
                for q in range(4):
                    c = grp * 4 + q
                    P.op("pe", lambda e, q=q, c=c, b=b: e.transpose(out=ps[b][:16, q * 128:(q + 1) * 128], in_=convT[:, c, :16], identity=ident),
                         reads=["convT", "CM"], writes=[("ps", b)])
                dst = xS[:16, grp * 512:(grp + 1) * 512] if grp < 2 else xS2[:16, 0:512]
                P.op("act", lambda e, b=b, dst=dst: e.activation(out=dst, in_=ps[b][:16, :512], func=AF.Copy), reads=[("ps", b)], writes=["xdt", "xn"])
            P.op("act", lambda e: e.activation(out=sm[:16, 64:80], in_=sm[:16, 48:64], func=AF.Exp), reads=["sm"], writes=["sm"])
            P.op("dve", lambda e: e.tensor_copy(out=v3(junk[:16, :], 16), in_=sm[:16, 64:80].unsqueeze(2).to_broadcast([16, 16, 64])), reads=["sm"], writes=["junk"])
            P.op("dve", lambda e: e.tensor_tensor(out=v3(xtok[:16, 0:1024], 16), in0=v3(xS[:16, :], 16), in1=sm[:16, 32:48].unsqueeze(2).to_broadcast([16, 16, 64]), op=ALU.mult),
                 reads=["xdt", "sm"], writes=["xtok"])
            DT2 = RA[:, 0:256].rearrange("p (w j s) -> p w j s", w=2, j=8)
            b = bank()
            for w, (srct, skey) in enumerate(((junk, "junk"), (xtok, "xtok"))):
                for j in range(8):
                    P.op("pe", lambda e, w=w, j=j, b=b, srct=srct: e.transpose(out=ps[b][:, (w * 8 + j) * 16:(w * 8 + j + 1) * 16], in_=srct[:16, j * 128:(j + 1) * 128], identity=ident[:16, :16]),
                         reads=[skey, "CM"], writes=[("ps", b)])
            P.op("act", lambda e, b=b: e.activation(out=RA[:, 0:256], in_=ps[b][:, :256], func=AF.Copy), reads=[("ps", b)], writes=["RA"])
            P.op("dve", lambda e: e.tensor_copy(out=SEL, in_=ident[:16, :16].unsqueeze(2).to_broadcast([16, 16, 128])), reads=["CM"], writes=["BK"])
            for si in range(NS):
                Sin = BIGA[:, (si % 2) * 1024:(si % 2 + 1) * 1024]; Sout = BIGC[:, (si % 2) * 1024:(si % 2 + 1) * 1024]
                tA = BIGB[:, 0:1024]; tB = BIGB[:, 1024:2048]
                P.dma("sp", lambda e, si=si, Sin=Sin: e.dma_start(out=Sin.rearrange("p (j n) -> p j n", j=8), in_=st_ssm[si].rearrange("h p n -> (h p) n").rearrange("(j q) n -> q j n", q=128)),
                      writes=["BIGA"])
                b = bank()
                P.op("pe", lambda e, b=b, si=si: e.matmul(ps[b][:, :512], lhsT=SEL[:, si, :], rhs=xS2[:16, 0:512], start=True, stop=True), reads=["BK", "xn"], writes=[("ps", b)])
                P.op("dve", lambda e, si=si, Sin=Sin, tA=tA: e.tensor_tensor(out=v3(tA, 8), in0=v3(Sin, 8), in1=DT2[:, 0, :, si:si + 1].to_broadcast([128, 8, 128]), op=ALU.mult),
                     reads=["BIGA", "RA"], writes=["BIGB"])
                P.op("dve", lambda e, si=si, b=b, tB=tB: e.tensor_tensor(out=tB.rearrange("p (g j n) -> p g j n", g=2, j=4),
                                                                           in0=ps[b][:, 0:256].rearrange("p (g n) -> p g n", g=2).unsqueeze(2).to_broadcast([128, 2, 4, 128]),
                                                                           in1=DT2[:, 1, :, si].rearrange("p (g j) -> p g j", g=2).unsqueeze(3).to_broadcast([128, 2, 4, 128]), op=ALU.mult),
                     reads=[("ps", b), "RA"], writes=["BIGB"])
                P.op("dve", lambda e, Sout=Sout, tA=tA, tB=tB: e.tensor_tensor(out=Sout, in0=tA, in1=tB, op=ALU.add), reads=["BIGB"], writes=["BIGC"])
                P.dma("sp", lambda e, si=si, Sout=Sout: e.dma_start(out=ssm_s[si].rearrange("(j q) n -> q j n", q=128), in_=Sout.rearrange("p (j n) -> p j n", j=8)), reads=["BIGC"])
                P.op("dve", lambda e, si=si, b=b, Sout=Sout, tA=tA: e.tensor_tensor(out=tA.rearrange("p (g j n) -> p g j n", g=2, j=4), in0=Sout.rearrange("p (g j n) -> p g j n", g=2, j=4),
                                                                                     in1=ps[b][:, 256:512].rearrange("p (g n) -> p g n", g=2).unsqueeze(2).to_broadcast([128, 2, 4, 128]), op=ALU.mult),
                     reads=[("ps", b), "BIGC"], writes=["BIGB"])
                P.op("dve", lambda e, si=si, tA=tA: e.tensor_reduce(out=YST[:, :, si], in_=v3(tA, 8), axis=AX.X, op=ALU.add), reads=["BIGB"], writes=["convT"])
            P.op("pool", lambda e: e.tensor_tensor(out=v3(junk[:16, :], 16), in0=v3(xS[:16, 0:1024], 16), in1=CT[:16, 32:48].unsqueeze(2).to_broadcast([16, 16, 64]), op=ALU.mult),
                 reads=["xdt", "CT"], writes=["junk"])
            for half in range(2):
                b = bank()
                for q in range(4):
                    j = half * 4 + q
                    P.op("pe", lambda e, q=q, j=j, b=b: e.transpose(out=ps[b][:16, q * 128:(q + 1) * 128], in_=YST[:, j, :], identity=ident), reads=["convT", "CM"], writes=[("ps", b)])
                P.op("dve", lambda e, b=b, half=half: e.tensor_tensor(out=ysb[:16, half * 512:(half + 1) * 512], in0=junk[:16, half * 512:(half + 1) * 512], in1=ps[b][:16, :512], op=ALU.add),
                     reads=[("ps", b), "junk"], writes=["ysb"])
            ssd_post(16)
            P.dma("sp", lambda e: e.dma_start(out=BIGA[:16, 0:2048], in_=st_shift[:, 0:2048]), writes=["BIGA"])
            P.dma("sp", lambda e: e.dma_start(out=BIGB[:16, 0:1280], in_=st_shift[:, 2048:3328]), writes=["BIGB"])
            b = bank()
            for c in range(26):
                src = BIGA[:16, c * 128:(c + 1) * 128] if c < 16 else BIGB[:16, (c - 16) * 128:(c - 15) * 128]
                P.op("pe", lambda e, c=c, b=b, src=src: e.transpose(out=ps[b][:, c * 16:(c + 1) * 16], in_=src, identity=ident[:16, :16]), reads=["BIGA", "BIGB", "CM"], writes=[("ps", b)])
            P.op("act", lambda e, b=b: e.activation(out=rwT[:, :, 32:48], in_=ps[b][:, :416].rearrange("p (c t) -> p c t", c=26), func=AF.Copy), reads=[("ps", b)], writes=["rwT"])
            rT, kT, vT, t1, t2, t4, t5, t6, t7 = rwkv_pre(16, rwT[:, :, 32:48], rwT[:, :, 1:17])
            P.op("act", lambda e: e.activation(out=t1, in_=t1, func=AF.Exp), reads=["BIGA"], writes=["BIGA"])
            TKs = [(t1, xdd, "xdd", "BIGA", 1.0), (t4, ysb, "ysb", "BIGB", -1.0), (t2, junk, "junk", "BIGB", 1.0), (kT, xtok, "xtok", "RWS", 1.0), (rT, xn, "xn", "RWS", 1.0)]
            for (src3, dst, dkey, skey, scl) in TKs:
                for half in range(2):
                    b = bank()
                    for q in range(4):
                        c = half * 4 + q
                        P.op("pe", lambda e, q=q, c=c, b=b, src3=src3: e.transpose(out=ps[b][:16, q * 128:(q + 1) * 128], in_=src3[:, c, :], identity=ident), reads=[skey, "CM"], writes=[("ps", b)])
                    P.op("act", lambda e, b=b, half=half, dst=dst, scl=scl: e.activation(out=dst[:16, half * 512:(half + 1) * 512], in_=ps[b][:16, :512], func=AF.Copy, scale=scl),
                         reads=[("ps", b)], writes=[dkey])
            SELH = [RA[:16, 0:2048].rearrange("p (s m) -> p s m", s=16), BKH[:16, 0:2048].rearrange("p (s m) -> p s m", s=16)]
            for hp, key in ((0, "RA"), (1, "BKH")):
                P.op("pool", lambda e, hp=hp: e.memset(SELH[hp], 0.0), writes=[key])
                P.op("dve", lambda e, hp=hp: e.tensor_copy(out=SELH[hp][:, :, hp * 64:(hp + 1) * 64], in_=ident[:16, :16].unsqueeze(2).to_broadcast([16, 16, 64])), reads=["CM"], writes=[key])
            tmp = BIGA[:, 0:512]
            for si in range(NS):
                SV = BIGB[:, (si % 2) * 512:(si % 2 + 1) * 512]; S1 = BIGB[:, 1024 + (si % 2) * 512:1024 + (si % 2 + 1) * 512]
                for hp in range(2):
                    P.dma("sp", lambda e, si=si, hp=hp, SV=SV: e.dma_start(out=SV[hp * 64:(hp + 1) * 64, :].rearrange("p (j k) -> p j k", j=8),
                                                                           in_=st_wkv[si].rearrange("(j hp) v k -> hp v j k", hp=2)[hp]), writes=["BIGB"])

                def bcast(Xt, xkey, si=si):
                    b = bank()
                    for hp in range(2):
                        P.op("pe", lambda e, hp=hp, b=b, Xt=Xt, si=si: e.matmul(ps[b][:, :512], lhsT=SELH[hp][:, si, :],
                                                                                rhs=Xt[:16, 0:1024].rearrange("s (j hp k) -> s j hp k", hp=2, k=64)[:, :, hp, :],
                                                                                start=(hp == 0), stop=(hp == 1)), reads=["RA", "BKH", xkey], writes=[("ps", b)])
                    return b
                ba = bcast(ysb, "ysb")
                P.op("dve", lambda e, ba=ba, SV=SV: e.tensor_tensor(out=tmp, in0=SV, in1=ps[ba][:, :512], op=ALU.mult), reads=[("ps", ba), "BIGB"], writes=["BIGA"])
                P.op("dve", lambda e: e.tensor_reduce(out=smw[:, 8:16], in_=v3(tmp, 8), axis=AX.X, op=ALU.add), reads=["BIGA"], writes=["smw"])
                bw = bcast(xdd, "xdd")
                P.op("dve", lambda e, bw=bw, SV=SV, S1=S1: e.tensor_tensor(out=S1, in0=SV, in1=ps[bw][:, :512], op=ALU.mult), reads=[("ps", bw), "BIGB"], writes=["BIGB"])
                bb = bcast(junk, "junk")
                P.op("dve", lambda e, bb=bb: e.tensor_tensor(out=v3(tmp, 8), in0=v3(ps[bb][:, :512], 8), in1=smw[:, 8:16].unsqueeze(2).to_broadcast([128, 8, 64]), op=ALU.mult),
                     reads=[("ps", bb), "smw"], writes=["BIGA"])
                P.op("dve", lambda e, S1=S1: e.tensor_tensor(out=S1, in0=S1, in1=tmp, op=ALU.add), reads=["BIGA", "BIGB"], writes=["BIGB"])
                bk = bcast(xtok, "xtok")
                P.op("dve", lambda e, bk=bk, si=si: e.tensor_tensor(out=v3(tmp, 8), in0=v3(ps[bk][:, :512], 8), in1=RWS3[:, 16:24, si:si + 1].to_broadcast([128, 8, 64]), op=ALU.mult),
                     reads=[("ps", bk), "RWS"], writes=["BIGA"])
                P.op("dve", lambda e, S1=S1: e.tensor_tensor(out=S1, in0=S1, in1=tmp, op=ALU.add), reads=["BIGA", "BIGB"], writes=["BIGB"])
                for hp in range(2):
                    P.dma("sp", lambda e, si=si, hp=hp, S1=S1: e.dma_start(out=wkv_s[si].rearrange("(j hp) v k -> hp v j k", hp=2)[hp],
                                                                           in_=S1[hp * 64:(hp + 1) * 64, :].rearrange("p (j k) -> p j k", j=8)), reads=["BIGB"])
                br = bcast(xn, "xn")
                P.op("dve", lambda e, br=br, S1=S1: e.tensor_tensor(out=tmp, in0=S1, in1=ps[br][:, :512], op=ALU.mult), reads=[("ps", br), "BIGB"], writes=["BIGA"])
                P.op("dve", lambda e, si=si: e.tensor_reduce(out=YRT[:, :, si], in_=v3(tmp, 8), axis=AX.X, op=ALU.add), reads=["BIGA"], writes=["convT"])
            for half in range(2):
                b = bank()
                for q in range(4):
                    j = half * 4 + q
                    P.op("pe", lambda e, q=q, j=j, b=b: e.transpose(out=ps[b][:16, q * 128:(q + 1) * 128], in_=YRT[:, j, :], identity=ident), reads=["convT", "CM"], writes=[("ps", b)])
                P.op("act", lambda e, b=b, half=half: e.activation(out=Ys[:16, half * 512:(half + 1) * 512], in_=ps[b][:16, :512], func=AF.Copy), reads=[("ps", b)], writes=["BIGA"])
            rwkv_post(16)
            mixer_out(16)
            if os.environ.get("K_DBGS") == "1":
                P.dma("sp", lambda e: e.dma_start(out=y_s, in_=xres[:NS, :]), reads=["xres"])
            if stage >= 5:
                attn_sample()
            if os.environ.get("K_DBGS") == "2":
                P.dma("sp", lambda e: e.dma_start(out=y_s, in_=xres[:NS, :]), reads=["xres"])
            if stage >= 6:
                peer(NS)()
            if stage >= 7:
                final_out(NS, y_s)
        if stage >= 2:
            for half in range(2):
                b = bank()
                for q in range(4):
                    c = half * 4 + q
                    P.op("pe", lambda e, c=c, q=q, b=b: e.transpose(out=ps[b][:, q * 128:(q + 1) * 128], in_=stT[:, c * 128:(c + 1) * 128], identity=ident),
                         reads=["stT", "CM"], writes=[("ps", b)])
                P.op("act", lambda e, b=b: e.activation(out=osb[:], in_=ps[b][:, :512], func=AF.Copy), reads=[("ps", b)], writes=["osb"])
                P.dma("sp", lambda e, half=half: e.dma_start(out=ssm_p[half * 512:(half + 1) * 512, :].rearrange("(q p) n -> p q n", p=128),
                                                             in_=osb[:].rearrange("p (q n) -> p q n", q=4)), reads=["osb"])
        if stage >= 3:
            for half in range(2):
                b = bank()
                for q in range(4):
                    j = half * 4 + q
                    P.op("pe", lambda e, j=j, q=q, b=b: e.transpose(out=ps[b][:64, q * 128:(q + 1) * 128], in_=Hst[:, j, :], identity=ident),
                         reads=["Hst", "CM"], writes=[("ps", b)])
                P.op("act", lambda e, b=b: e.activation(out=osb[:64, :], in_=ps[b][:64, :512], func=AF.Copy), reads=[("ps", b)], writes=["osb"])
                P.dma("sp", lambda e, half=half: e.dma_start(out=wkv_p[half * 8:(half + 1) * 8].rearrange("h v k -> v h k"),
                                                             in_=osb[:64, :].rearrange("p (h k) -> p h k", h=8)), reads=["osb"])
        print('SBUF bytes remaining', nc.sbuf_bytes_remaining)
        P.emit()
    return nc


_CACHE = {}


def kernel(**inp):
    f = lambda k: np.asarray(inp[k], np.float32)
    if "nc" not in _CACHE:
        _CACHE["nc"] = build_program()
    nc = _CACHE["nc"]
    cfm = np.zeros((128, 192), np.float32)
    cfm[:, 0:8] = fm(f("norm_mix_w")[0], 8); cfm[:, 8:16] = fm(f("norm_mem_w")[0], 8)
    cfm[:, 16:24] = fm(f("norm_ffn_w")[0], 8); cfm[:, 24:32] = fm(f("mem_norm_w")[0], 8)
    cw = f("conv_w")[0]
    cfm[:, 32:80] = np.stack([fm(cw[k], 12) for k in range(4)], axis=2).reshape(128, 48)
    cfm[:, 80:92] = fm(f("conv_b")[0], 12)
    cfm[:, 92:118] = fm(f("rwkv_mu")[0], 26)
    cfm[:, 118:126] = fm(f("rwkv_w0")[0], 8); cfm[:, 126:134] = fm(f("rwkv_a0")[0], 8)
    cfm[:, 134:142] = fm(f("rwkv_k_k")[0], 8); cfm[:, 142:150] = fm(f("rwkv_k_a")[0], 8)
    cfm[:, 150:158] = fm(f("rwkv_r_k")[0].reshape(-1), 8)
    cfm[:, 160:168] = fm(f("rwkv_ln_w")[0], 8); cfm[:, 168:176] = fm(f("rwkv_ln_b")[0], 8)
    cfm[:, 176:184] = fm(f("ssd_norm_w")[0], 8)
    ctk = np.zeros((1, 48 + D), np.float32)
    ctk[0, 0:16] = f("dt_bias")[0]; ctk[0, 16:32] = f("a_log")[0]; ctk[0, 32:48] = f("d_skip")[0]
    ctk[0, 48:] = f("norm_final_w")
    r = np.arange(128)
    cmat = np.zeros((128, 6 * 128 + 16), np.float32)
    cmat[:, 768:784] = np.arange(16)[None, :]
    cmat[:, 0:128] = np.eye(128)
    cmat[:, 128:256] = (r[:, None] > r[None, :])
    cmat[:, 256:384] = (r[:, None] <= r[None, :])
    cmat[:, 384:512] = 1.0
    cmat[:, 512:640] = ((r[:, None] // 64) == (r[None, :] // 64))
    cmat[:, 640:768] = (r[:, None] < r[None, :])
    shared = {
        "w_in": f("w_in")[0], "w_out": f("w_out")[0], "w_mk": f("w_mk")[0], "w_mv": f("w_mv")[0],
        "w_mq": f("w_mq")[0], "w_mo": f("w_mo")[0], "w_pq": f("w_pq")[0],
        "sub_keys": f("sub_keys")[0].reshape(16, 128, 128),
        "cfm": cfm, "ctk": ctk, "cmat": cmat,
        "wa2": np.concatenate([f("rwkv_w2")[0], f("rwkv_a2")[0]], axis=0), "g2": f("rwkv_g2")[0],
    }
    if True:
        shared["exp_u"] = f("expert_u")[0]; shared["exp_v"] = f("expert_v")[0]
    in_maps = []
    for c in range(NCORES):
        s = slice(c * NS, (c + 1) * NS)
        m = dict(shared)
        m.update({
            "xp": f("x_prompt")[c], "xs": f("x_sample")[s, 0], "memp": f("mem_prompt")[c],
            "st_ssm": f("state_ssm")[0, s], "st_conv": f("state_conv")[0, s], "st_wkv": f("state_wkv")[0, s],
            "st_shift": f("state_shift")[0, s],
            "ck": f("cache_mem_k")[0, s].reshape(NS, NMEM, D), "cv": f("cache_mem_v")[0, s].reshape(NS, NMEM, D),
        })
        in_maps.append(m)
    res = run_bass_kernel_spmd(nc, in_maps, core_ids=list(range(NCORES))).results
    cat = lambda k: np.stack([r_[k] for r_ in res], axis=0)
    y_prompt = cat("y_p")
    y_sample = np.concatenate([r_["y_s"] for r_ in res], axis=0).reshape(128, 1, D)
    ssm_prompt = cat("ssm_p").reshape(1, 8, 16, 64, 128)
    conv_prompt = cat("conv_p").reshape(1, 8, 3, CONV_DIM)
    wkv_prompt = cat("wkv_p").reshape(1, 8, 16, 64, 64)
    shift_prompt = cat("shift_p").reshape(1, 8, RWP)
    mem_k_prompt = cat("mk_p").reshape(1, 8, NMEM, 4, 256)
    mem_v_prompt = cat("mv_p").reshape(1, 8, NMEM, 4, 256)
    ssm_sample = np.concatenate([r_["ssm_s"] for r_ in res], axis=0).reshape(1, 128, 16, 64, 128)
    conv_sample = np.concatenate([r_["conv_s"] for r_ in res], axis=0).reshape(1, 128, 3, CONV_DIM)
    wkv_sample = np.concatenate([r_["wkv_s"] for r_ in res], axis=0).reshape(1, 128, 16, 64, 64)
    shift_sample = np.concatenate([r_["shift_s"] for r_ in res], axis=0).reshape(1, 128, RWP)
    return (y_prompt, y_sample, ssm_prompt, conv_prompt, wkv_prompt, shift_prompt, mem_k_prompt, mem_v_prompt,
            ssm_sample, conv_sample, wkv_sample, shift_sample)
```

```python
import os
import numpy as np
from contextlib import ExitStack
import concourse.bass as bass
import concourse.mybir as mybir
from concourse.bass_utils import run_bass_kernel_spmd

F32 = mybir.dt.float32
BF16 = mybir.dt.bfloat16
U32 = mybir.dt.uint32
ALU = mybir.AluOpType
AF = mybir.ActivationFunctionType
AX = mybir.AxisListType

NCORES = 8
D = 1024
SEQ = 2048
NCH = SEQ // 128
NS = 16
D_IN = 7952
CONV_DIM = 1536
RWP = 3328
NMEM = 256
EPS = 1e-6

SAME_ENG_SYNC = True
NDMA_SEMS = 12


class Prog:
    def __init__(self, nc):
        self.nc = nc
        self.names = ["pe", "act", "dve", "pool", "sp"]
        self.streams = {k: [] for k in self.names}
        self.count = {k: 0 for k in self.names}
        self.waited = {}
        self.res = {}
        self.dma_ring = {k: 0 for k in self.names}
        self.dma_val = {}

    def _deps(self, reads, writes):
        deps = []
        for r in reads:
            e = self.res.get(r)
            if e and e[0] is not None:
                deps.append(e[0])
        for w in writes:
            e = self.res.get(w)
            if e:
                if e[0] is not None:
                    deps.append(e[0])
                deps.extend(e[1])
        return deps

    def _commit(self, tok, reads, writes):
        for r in reads:
            e = self.res.setdefault(r, [None, []])
            e[1].append(tok)
        for w in writes:
            self.res[w] = [tok, []]

    def _waits(self, engine, deps):
        best = {}
        for (sk, v) in deps:
            if sk == engine and (engine == "pe" or not SAME_ENG_SYNC):
                continue
            if self.waited.get((engine, sk), 0) >= v:
                continue
            if best.get(sk, 0) < v:
                best[sk] = v
        for sk, v in best.items():
            self.waited[(engine, sk)] = v
        return list(best.items())

    def op(self, engine, fn, reads=(), writes=()):
        reads = list(reads)
        writes = list(writes)
        waits = self._waits(engine, self._deps(reads, writes))
        self.count[engine] += 1
        tok = (engine, self.count[engine])
        self.streams[engine].append((fn, waits, (engine, 1)))
        self._commit(tok, reads, writes)
        return tok

    def dma(self, engine, fn, reads=(), writes=()):
        reads = list(reads)
        writes = list(writes)
        ring = self.dma_ring[engine]
        self.dma_ring[engine] = (ring + 1) % NDMA_SEMS
        sk = ("dma", engine, ring)
        prev = self.dma_val.get(sk, 0)
        deps = self._deps(reads, writes)
        if prev:
            deps.append((sk, prev))
        waits = self._waits(engine, deps)
        self.dma_val[sk] = prev + 16
        tok = (sk, prev + 16)
        self.streams[engine].append((fn, waits, (sk, 16)))
        self._commit(tok, reads, writes)
        return tok

    def emit(self):
        nc = self.nc
        with ExitStack() as es:
            sems = {}
            for k in self.names:
                sems[k] = es.enter_context(nc.semaphore("s_" + k))
            for sk in self.dma_val:
                sems[sk] = es.enter_context(nc.semaphore("d_%s_%d" % (sk[1], sk[2])))
            final = dict(self.dma_val)
            for k in self.names:
                if k != "sp" and self.count[k]:
                    final[k] = self.count[k]
            block = es.enter_context(nc.Block())

            def run(e, name):
                for fn, waits, inc in self.streams[name]:
                    for sk, v in waits:
                        e.wait_ge(sems[sk], v)
                    fn(e).then_inc(sems[inc[0]], inc[1])
                if name == "sp":
                    for sk, v in final.items():
                        e.wait_ge(sems[sk], v)

            @block.tensor
            def _(e):
                run(e, "pe")

            @block.scalar
            def _(e):
                run(e, "act")

            @block.vector
            def _(e):
                run(e, "dve")

            @block.gpsimd
            def _(e):
                run(e, "pool")

            @block.sync
            def _(e):
                run(e, "sp")


def fm(v, n):
    return np.ascontiguousarray(np.asarray(v, np.float32).reshape(n, 128).T)


STAGE = int(os.environ.get('K_STAGE', 99))
SUB = int(os.environ.get('K_SUB', 99))
SUB2 = int(os.environ.get('K_SUB2', 99))
SUB3 = int(os.environ.get('K_SUB3', 99))


def build_program(stage=None):
    stage = STAGE if stage is None else stage
    nc = bass.Bass("TRN2", target_bir_lowering=False)

    def din(name, shape):
        return nc.dram_tensor(name, list(shape), F32, kind="ExternalInput").ap()

    def dout(name, shape):
        return nc.dram_tensor(name, list(shape), F32, kind="ExternalOutput").ap()

    xp = din("xp", [SEQ, D]); xs_in = din("xs", [NS, D]); memp = din("memp", [NMEM, D])
    st_ssm = din("st_ssm", [NS, 16, 64, 128]); st_conv = din("st_conv", [NS, 3, CONV_DIM])
    st_wkv = din("st_wkv", [NS, 16, 64, 64]); st_shift = din("st_shift", [NS, RWP])
    ck = din("ck", [NS, NMEM, D]); cv = din("cv", [NS, NMEM, D])
    w_in = din("w_in", [D, D_IN]); w_out = din("w_out", [D, D])
    w_mk = din("w_mk", [D, D]); w_mv = din("w_mv", [D, D]); w_mq = din("w_mq", [D, D]); w_mo = din("w_mo", [D, D])
    w_pq = din("w_pq", [D, 2048]); sub_keys = din("sub_keys", [16, 128, 128])
    if stage >= 0:
        exp_u = din("exp_u", [16384, D]); exp_v = din("exp_v", [16384, D])
    cfm = din("cfm", [128, 192])
    ctk = din("ctk", [1, 48 + D])
    wa2 = din("wa2", [128, D]); g2 = din("g2", [128, D])
    cmat = din("cmat", [128, 6 * 128 + 16])

    ub = nc.dram_tensor("ub_scr", [16384, D], BF16).ap(); vb = nc.dram_tensor("vb_scr", [16384, D], BF16).ap()
    y_p = dout("y_p", [SEQ, D]); y_s = dout("y_s", [NS, D])
    ssm_p = dout("ssm_p", [16 * 64, 128]); conv_p = dout("conv_p", [3, CONV_DIM])
    wkv_p = dout("wkv_p", [16, 64, 64]); shift_p = dout("shift_p", [1, RWP])
    mk_p = dout("mk_p", [NMEM, D]); mv_p = dout("mv_p", [NMEM, D])
    ssm_s = dout("ssm_s", [NS, 16 * 64, 128]); conv_s = dout("conv_s", [NS, 3, CONV_DIM])
    wkv_s = dout("wkv_s", [NS, 16, 64, 64]); shift_s = dout("shift_s", [NS, RWP])

    es = ExitStack()
    with es:
        def sb(name, shape, dt=F32):
            return es.enter_context(nc.sbuf_tensor(name, list(shape), dt))

        P = Prog(nc)
        es.enter_context(nc.allow_non_contiguous_dma(reason="small transposing stores"))
        ps = [es.enter_context(nc.psum_tensor("ps%d" % i, [128, 512], F32)) for i in range(8)]
        bank_ctr = [0]

        def bank():
            b = bank_ctr[0]
            bank_ctr[0] = (b + 1) % 8
            return b

        CM = sb("CM", [128, 6 * 128 + 16]); CF = sb("CF", [128, 192]); CT = sb("CT", [128, 48 + D])
        P.dma("sp", lambda e: e.dma_start(out=CM[:], in_=cmat), writes=["CM"])
        P.dma("sp", lambda e: e.dma_start(out=CF[:], in_=cfm), writes=["CF"])
        P.dma("sp", lambda e: e.dma_start(out=CT[:], in_=ctk.partition_broadcast(128)), writes=["CT"])
        ident = CM[:, 0:128]; M1 = CM[:, 128:256]; M2 = CM[:, 256:384]; ONES = CM[:, 384:512]
        BONES = CM[:, 512:640]; M3 = CM[:, 640:768]; IOTA = CM[:, 768:784]

        xres = sb("xres", [128, D]); xres2 = sb("xres2", [128, D])
        CUR = [xres, "xres"]
        xn = sb("xn", [128, D])
        junk = sb("junk", [128, D])
        st1 = sb("st1", [128, 8])
        hT = sb("hT", [128, 8, 128])
        WB = [sb("wb%d" % i, [128, 8, 512]) for i in range(2)]
        wb_ctr = [0]
        xbcT = sb("xbcT", [128, 12, 131])
        rwT = sb("rwT", [128, 26, 129])
        gateT = sb("gateT", [128, 16, 128])
        ztok = sb("ztok", [128, D])
        dtraw = sb("dtraw", [128, 16])

        def rmsnorm_T(src, nt, wcol, dstT, src_key, dst_key):
            P.op("act", lambda e: e.activation(out=junk[:nt, :], in_=src[:nt, :], func=AF.Square, accum_out=st1[:nt, 0:1]),
                 reads=[src_key], writes=["junk", "st1"])
            P.op("act", lambda e: e.activation(out=st1[:nt, 1:2], in_=st1[:nt, 0:1], func=AF.Sqrt, scale=1.0 / D, bias=EPS),
                 reads=["st1"], writes=["st1"])
            P.op("dve", lambda e: e.reciprocal(out=st1[:nt, 2:3], in_=st1[:nt, 1:2]), reads=["st1"], writes=["st1"])
            P.op("dve", lambda e: e.tensor_scalar(out=xn[:nt, :], in0=src[:nt, :], scalar1=st1[:nt, 2:3], scalar2=None,
                                                  op0=ALU.mult), reads=["st1", src_key], writes=["xn"])
            for half in range(2):
                b = bank()
                for q in range(4):
                    c = half * 4 + q
                    P.op("pe", lambda e, c=c, q=q, b=b: e.transpose(out=ps[b][:, q * 128:q * 128 + nt],
                                                                     in_=xn[:nt, c * 128:(c + 1) * 128], identity=ident[:nt, :nt]),
                         reads=["xn", "CM"], writes=[("ps", b)])
                P.op("dve", lambda e, half=half, b=b: e.tensor_tensor(
                    out=dstT[:, half * 4:half * 4 + 4, :nt],
                    in0=ps[b][:].rearrange("p (q t) -> p q t", q=4)[:, :, :nt],
                    in1=CF[:, wcol + half * 4:wcol + half * 4 + 4].unsqueeze(2).to_broadcast([128, 4, nt]),
                    op=ALU.mult), reads=[("ps", b), "CF"], writes=[dst_key])

        def load_w(wdram, col0, ncols, eng="sp"):
            i = wb_ctr[0]
            wb_ctr[0] = (i + 1) % 2
            P.dma(eng, lambda e: e.dma_start(out=WB[i][:, :, :ncols],
                                             in_=wdram[:, col0:col0 + ncols].rearrange("(kc p) n -> p kc n", p=128)),
                  writes=[("wb", i)])
            return i

        def proj_fm(wdram, col0, nchunks, srcT, src_key, nt, evac, eng="sp"):
            ncols = nchunks * 128
            i = load_w(wdram, col0, ncols, eng)
            b = bank()
            for q in range(nchunks):
                for kc in range(8):
                    P.op("pe", lambda e, q=q, kc=kc: e.matmul(ps[b][:, q * 128:q * 128 + nt],
                                                                lhsT=WB[i][:, kc, q * 128:(q + 1) * 128], rhs=srcT[:, kc, :nt],
                                                                start=(kc == 0), stop=(kc == 7)),
                         reads=[("wb", i), src_key], writes=[("ps", b)])
            evac(b, nchunks)

        def proj_tok(wdram, col0, ncols, srcT, src_key, nt, evac, eng="sp"):
            i = load_w(wdram, col0, ncols, eng)
            b = bank()
            for kc in range(8):
                P.op("pe", lambda e, kc=kc: e.matmul(ps[b][:nt, :ncols], lhsT=srcT[:, kc, :nt], rhs=WB[i][:, kc, :ncols],
                                                       start=(kc == 0), stop=(kc == 7)),
                     reads=[("wb", i), src_key], writes=[("ps", b)])
            evac(b, ncols)

        def psv(b, nchunks, nt):
            return ps[b][:, :nchunks * 128].rearrange("p (q t) -> p q t", q=nchunks)[:, :, :nt]

        def in_proj(nt, t0_xbc, t0_rw):
            for blk in range(2):
                proj_tok(w_in, blk * 512, 512, hT, "hT", nt,
                         lambda b, n, blk=blk: P.op("act", lambda e: e.activation(out=ztok[:nt, blk * 512:(blk + 1) * 512], in_=ps[b][:nt, :512], func=AF.Silu),
                                                    reads=[("ps", b)], writes=["ztok"]))
            for blk in range(3):
                proj_fm(w_in, 1024 + blk * 512, 4, hT, "hT", nt,
                        lambda b, n, blk=blk: P.op("dve", lambda e: e.tensor_copy(out=xbcT[:, blk * 4:blk * 4 + 4, t0_xbc:t0_xbc + nt], in_=psv(b, 4, nt)),
                                                   reads=[("ps", b)], writes=["xbcT"]))
            proj_tok(w_in, 2560, 16, hT, "hT", nt,
                     lambda b, n: P.op("dve", lambda e: e.tensor_copy(out=dtraw[:nt, :], in_=ps[b][:nt, :16]), reads=[("ps", b)], writes=["dtraw"]))
            for blk in range(7):
                n = 4 if blk < 6 else 2
                proj_fm(w_in, 2576 + blk * 512, n, hT, "hT", nt,
                        lambda b, n, blk=blk: P.op("act", lambda e: e.activation(out=rwT[:, blk * 4:blk * 4 + n, t0_rw:t0_rw + nt], in_=psv(b, n, nt), func=AF.Copy),
                                                   reads=[("ps", b)], writes=["rwT"]))
            for blk in range(4):
                proj_fm(w_in, 5904 + blk * 512, 4, hT, "hT", nt,
                        lambda b, n, blk=blk: P.op("act", lambda e: e.activation(out=gateT[:, blk * 4:blk * 4 + 4, :nt], in_=psv(b, 4, nt), func=AF.Sigmoid),
                                                   reads=[("ps", b)], writes=["gateT"]))


        BIGA = sb("BIGA", [128, 2048]); BIGB = sb("BIGB", [128, 2048]); BIGC = sb("BIGC", [128, 2048])
        convT = sb("convT", [128, 12, 128])
        xtok = sb("xtok", [128, 1280])
        sm = sb("sm", [128, 128])
        ANEG = sb("ANEG", [128, 16])
        cbm = sb("cbm", [128, 2, 128])
        xdt = sb("xdt", [128, D]); xdd = sb("xdd", [128, D])
        stT = sb("stT", [128, D])
        ysb = sb("ysb", [128, D])
        mgT = sb("mgT", [128, 8, 128])
        P.op("act", lambda e: e.activation(out=ANEG[:], in_=CT[:, 16:32], func=AF.Exp), reads=["CT"], writes=["ANEG"])
        P.op("dve", lambda e: e.tensor_scalar(out=ANEG[:], in0=ANEG[:], scalar1=-1.0, scalar2=None, op0=ALU.mult), reads=["ANEG"], writes=["ANEG"])
        CWv = CF[:, 32:80].rearrange("p (c k) -> p c k", k=4)

        def v3(t, a):
            return t.rearrange("p (a b) -> p a b", a=a)

        def conv_silu(nt):
            A = BIGA[:, :12 * nt].rearrange("p (c t) -> p c t", c=12)
            Bv = BIGB[:, :12 * nt].rearrange("p (c t) -> p c t", c=12)
            P.op("dve", lambda e: e.tensor_tensor(out=A, in0=xbcT[:, :, 0:nt], in1=CWv[:, :, 0:1].to_broadcast([128, 12, nt]), op=ALU.mult),
                 reads=["xbcT", "CF"], writes=["BIGA"])
            for k in range(1, 4):
                P.op("pool", lambda e, k=k: e.tensor_tensor(out=Bv, in0=xbcT[:, :, k:k + nt], in1=CWv[:, :, k:k + 1].to_broadcast([128, 12, nt]), op=ALU.mult),
                     reads=["xbcT", "CF"], writes=["BIGB"])
                P.op("dve", lambda e: e.tensor_tensor(out=A, in0=A, in1=Bv, op=ALU.add), reads=["BIGA", "BIGB"], writes=["BIGA"])
            for c in range(12):
                P.op("act", lambda e, c=c: e.activation(out=convT[:, c, :nt], in_=A[:, c, :], func=AF.Silu, bias=CF[:, 80 + c:81 + c]),
                     reads=["BIGA", "CF"], writes=["convT"])

        def dt_softplus(nt):
            P.op("dve", lambda e: e.tensor_tensor(out=sm[:nt, 0:16], in0=dtraw[:nt, :], in1=CT[:nt, 0:16], op=ALU.add), reads=["dtraw", "CT"], writes=["sm"])
            P.op("act", lambda e: e.activation(out=sm[:nt, 16:32], in_=sm[:nt, 0:16], func=AF.Exp), reads=["sm"], writes=["sm"])
            P.op("act", lambda e: e.activation(out=sm[:nt, 32:48], in_=sm[:nt, 16:32], func=AF.Ln, bias=1.0), reads=["sm"], writes=["sm"])
            P.op("dve", lambda e: e.tensor_tensor(out=sm[:nt, 48:64], in0=sm[:nt, 32:48], in1=ANEG[:nt, :], op=ALU.mult), reads=["sm", "ANEG"], writes=["sm"])

        def ssd_post(nt):
            P.op("dve", lambda e: e.tensor_tensor(out=ysb[:nt, :], in0=ysb[:nt, :], in1=ztok[:nt, :], op=ALU.mult), reads=["ysb", "ztok"], writes=["ysb"])
            rmsnorm_T(ysb, nt, 176, mgT, "ysb", "mgT")
            P.op("dve", lambda e: e.tensor_tensor(out=mgT[:, :, :nt], in0=mgT[:, :, :nt], in1=gateT[:, 0:8, :nt], op=ALU.mult), reads=["mgT", "gateT"], writes=["mgT"])

        def ssd_chunk():
            conv_silu(128)
            for (cs, c0) in (((0, 1, 2, 3), 0), ((4, 5, 6, 7), 512), ((8, 9), 1024)):
                b = bank()
                for q, c in enumerate(cs):
                    P.op("pe", lambda e, q=q, c=c, b=b: e.transpose(out=ps[b][:, q * 128:(q + 1) * 128], in_=convT[:, c, :], identity=ident),
                         reads=["convT", "CM"], writes=[("ps", b)])
                n = len(cs) * 128
                P.op("act", lambda e, b=b, c0=c0, n=n: e.activation(out=xtok[:, c0:c0 + n], in_=ps[b][:, :n], func=AF.Copy), reads=[("ps", b)], writes=["xtok"])
            dt_softplus(128)
            dt = sm[:, 32:48]; ad = sm[:, 48:64]
            P.op("dve", lambda e: e.tensor_tensor(out=v3(BIGA[:], 16), in0=ad.unsqueeze(2).to_broadcast([128, 16, 128]),
                                                  in1=M2.unsqueeze(1).to_broadcast([128, 16, 128]), op=ALU.mult), reads=["sm", "CM"], writes=["BIGA"])
            for q in range(4):
                b = bank()
                P.op("pe", lambda e, q=q, b=b: e.matmul(ps[b][:, :512], lhsT=M1, rhs=BIGA[:, q * 512:(q + 1) * 512], start=True, stop=True),
                     reads=["BIGA", "CM"], writes=[("ps", b)])
                P.op("act", lambda e, q=q, b=b: e.activation(out=BIGB[:, q * 512:(q + 1) * 512], in_=ps[b][:, :512], func=AF.Exp), reads=[("ps", b)], writes=["BIGB"])
                b = bank()
                P.op("pe", lambda e, q=q, b=b: e.matmul(ps[b][:, :512], lhsT=ONES, rhs=BIGA[:, q * 512:(q + 1) * 512], start=True, stop=True),
                     reads=["BIGA", "CM"], writes=[("ps", b)])
                P.op("act", lambda e, q=q, b=b: e.activation(out=BIGC[:, q * 512:(q + 1) * 512], in_=ps[b][:, :512], func=AF.Exp), reads=[("ps", b)], writes=["BIGC"])
            b = bank()
            for g in range(2):
                P.op("pe", lambda e, g=g, b=b: e.matmul(ps[b][:, g * 128:(g + 1) * 128], lhsT=convT[:, 8 + g, :], rhs=convT[:, 10 + g, :], start=True, stop=True),
                     reads=["convT"], writes=[("ps", b)])
            P.op("dve", lambda e, b=b: e.tensor_tensor(out=cbm[:], in0=v3(ps[b][:, :256], 2), in1=M2.unsqueeze(1).to_broadcast([128, 2, 128]), op=ALU.mult),
                 reads=[("ps", b), "CM"], writes=["cbm"])
            G4 = BIGB[:].rearrange("p (g h l) -> p g h l", g=2, h=8)
            P.op("dve", lambda e: e.tensor_tensor(out=G4, in0=G4, in1=cbm[:].unsqueeze(2).to_broadcast([128, 2, 8, 128]), op=ALU.mult),
                 reads=["BIGB", "cbm"], writes=["BIGB"])
            E4 = BIGC[:].rearrange("p (g h l) -> p g h l", g=2, h=8)
            P.op("pool", lambda e: e.tensor_tensor(out=E4, in0=E4, in1=convT[:, 10:12, :].unsqueeze(2).to_broadcast([128, 2, 8, 128]), op=ALU.mult),
                 reads=["BIGC", "convT"], writes=["BIGC"])
            P.op("dve", lambda e: e.tensor_tensor(out=v3(xdt[:], 16), in0=v3(xtok[:, :D], 16), in1=dt.unsqueeze(2).to_broadcast([128, 16, 64]), op=ALU.mult),
                 reads=["xtok", "sm"], writes=["xdt"])
            yb = [bank(), bank()]
            for h in range(16):
                b = yb[h // 8]; o = (h % 8) * 64
                P.op("pe", lambda e, h=h, b=b, o=o: e.matmul(ps[b][:, o:o + 64], lhsT=BIGB[:, h * 128:(h + 1) * 128], rhs=xdt[:, h * 64:(h + 1) * 64], start=True, stop=False),
                     reads=["BIGB", "xdt"], writes=[("ps", b)])
                P.op("pe", lambda e, h=h, b=b, o=o: e.matmul(ps[b][:, o:o + 64], lhsT=BIGC[:, h * 128:(h + 1) * 128], rhs=stT[:, h * 64:(h + 1) * 64], start=False, stop=True),
                     reads=["BIGC", "stT"], writes=[("ps", b)])
            P.op("pool", lambda e: e.tensor_tensor(out=v3(junk[:], 16), in0=v3(xtok[:, :D], 16), in1=CT[:, 32:48].unsqueeze(2).to_broadcast([128, 16, 64]), op=ALU.mult),
                 reads=["xtok", "CT"], writes=["junk"])
            for hh in range(2):
                P.op("dve", lambda e, hh=hh, yb=yb: e.tensor_tensor(out=ysb[:, hh * 512:(hh + 1) * 512], in0=junk[:, hh * 512:(hh + 1) * 512], in1=ps[yb[hh]][:, :512], op=ALU.add),
                     reads=["junk", ("ps", yb[hh])], writes=["ysb"])
            b = bank()
            P.op("pe", lambda e, b=b: e.matmul(ps[b][:, 0:16], lhsT=M1, rhs=ad, start=True, stop=True), reads=["sm", "CM"], writes=[("ps", b)])
            P.op("pe", lambda e, b=b: e.matmul(ps[b][:, 16:32], lhsT=ONES, rhs=ad, start=True, stop=True), reads=["sm", "CM"], writes=[("ps", b)])
            P.op("act", lambda e, b=b: e.activation(out=sm[:, 64:96], in_=ps[b][:, 0:32], func=AF.Exp), reads=[("ps", b)], writes=["sm"])
            P.op("dve", lambda e: e.tensor_tensor(out=v3(xdd[:], 16), in0=v3(xdt[:], 16), in1=sm[:, 64:80].unsqueeze(2).to_broadcast([128, 16, 64]), op=ALU.mult),
                 reads=["xdt", "sm"], writes=["xdd"])
            P.op("dve", lambda e: e.tensor_tensor(out=v3(junk[:], 16), in0=v3(stT[:], 16), in1=sm[:, 80:96].unsqueeze(2).to_broadcast([128, 16, 64]), op=ALU.mult),
                 reads=["stT", "sm"], writes=["junk"])
            for g in range(2):
                b = bank()
                P.op("pe", lambda e, g=g, b=b: e.matmul(ps[b][:, :512], lhsT=xtok[:, D + g * 128:D + (g + 1) * 128], rhs=xdd[:, g * 512:(g + 1) * 512], start=True, stop=True),
                     reads=["xtok", "xdd"], writes=[("ps", b)])
                P.op("dve", lambda e, g=g, b=b: e.tensor_tensor(out=stT[:, g * 512:(g + 1) * 512], in0=junk[:, g * 512:(g + 1) * 512], in1=ps[b][:, :512], op=ALU.add),
                     reads=["junk", ("ps", b)], writes=["stT"])
            ssd_post(128)


        RWS = sb("RWS", [128, 26 * 128]); RA = sb("RA", [128, 2048]); BK = sb("BK", [128, 2048]); BKH = sb("BKH", [128, 2048])
        MASK4 = sb("MASK4", [128, 4, 128])
        Hst = sb("Hst", [128, 8, 64]); smw = sb("smw", [128, 64])
        for kd in range(4):
            P.op("pool", lambda e, kd=kd: e.tensor_copy(out=MASK4[:, kd, :], in_=(M2 if kd % 2 == 0 else M3)), reads=["CM"], writes=["MASK4"])
        P.op("pool", lambda e: e.memset(Hst[:], 0.0), writes=["Hst"])
        RWS3 = RWS[:].rearrange("p (c t) -> p c t", c=26)
        RA4 = RA[:].rearrange("p (c k t) -> p c k t", c=8, k=2)
        BK4 = BK[:].rearrange("p (c k t) -> p c k t", c=8, k=2)
        BKH4 = BKH[:].rearrange("p (c k t) -> p c k t", c=8, k=2)
        T1 = BIGA[:, 0:1024]; T6 = BIGA[:, 1024:2048]; T2 = BIGB[:, 0:1024]; T4 = BIGB[:, 1024:2048]
        T5 = BIGC[:, 0:1024]; T7 = BIGC[:, 1024:2048]; T3 = xdt
        vtok = xdd; btok = ysb; ktok = junk
        ATq = RWS[:, 0:2048].rearrange("p (h k t) -> p h k t", h=4, k=4)
        MN = [RWS[:, 2048 + i * 512:2048 + (i + 1) * 512] for i in range(2)]
        NN = [xn[:, 0:512], xn[:, 512:1024]]
        TT = [convT[:].rearrange("p c t -> p (c t)")[:, 0:512], convT[:].rearrange("p c t -> p (c t)")[:, 512:1024]]
        RHSs = xtok[:, 0:1024]; Us = T1; Ys = T6

        def bc3(ap2, n):
            return ap2.unsqueeze(2).to_broadcast([128, ap2.shape[1], n])

        def rwkv_pre(nt, prev, cur):
            R3 = RWS3[:, :, :nt]
            wi = wb_ctr[0]; wb_ctr[0] = (wi + 1) % 2
            WBf = WB[wi][:].rearrange("p a b -> p (a b)")
            WA2 = WBf[:, 0:1024]; G2 = WBf[:, 1024:2048]; wkey = ("wb", wi)
            P.dma("sp", lambda e: e.dma_start(out=WA2, in_=wa2), writes=[wkey])
            P.dma("sp", lambda e: e.dma_start(out=G2, in_=g2), writes=[wkey])
            P.op("dve", lambda e: e.tensor_tensor(out=R3, in0=prev, in1=cur, op=ALU.subtract), reads=["rwT"], writes=["RWS"])
            P.op("pool", lambda e: e.tensor_tensor(out=R3, in0=R3, in1=bc3(CF[:, 92:118], nt), op=ALU.mult), reads=["RWS", "CF"], writes=["RWS"])
            P.op("dve", lambda e: e.tensor_tensor(out=R3, in0=R3, in1=cur, op=ALU.add), reads=["RWS", "rwT"], writes=["RWS"])
            P.op("act", lambda e: e.activation(out=RWS3[0:64, 24, :nt], in_=RWS3[0:64, 24, :nt], func=AF.Tanh), reads=["RWS"], writes=["RWS"])
            P.op("act", lambda e: e.activation(out=RWS3[:, 25, :nt], in_=RWS3[:, 25, :nt], func=AF.Sigmoid), reads=["RWS"], writes=["RWS"])

            def v8(t):
                return t.rearrange("p (c t) -> p c t", c=8)[:, :, :nt]
            rT = RWS3[:, 0:8, :nt]; kT = RWS3[:, 8:16, :nt]; vT = RWS3[:, 16:24, :nt]
            for half in range(2):
                for (lo, hi, col, dst, dkey) in ((0, 64, 118, T1, "BIGA"), (64, 128, 126, T2, "BIGB")):
                    b = bank()
                    for q in range(4):
                        c = half * 4 + q
                        P.op("pe", lambda e, q=q, c=c, b=b, lo=lo, hi=hi: e.matmul(ps[b][:, q * 128:q * 128 + nt], lhsT=WA2[lo:hi, c * 128:(c + 1) * 128],
                                                                                    rhs=RWS3[lo:hi, 24, :nt], start=True, stop=True),
                             reads=[wkey, "RWS"], writes=[("ps", b)])
                    dv = v8(dst)[:, half * 4:half * 4 + 4, :]
                    P.op("dve", lambda e, b=b, dv=dv, col=col, half=half: e.tensor_tensor(out=dv, in0=psv(b, 4, nt), in1=bc3(CF[:, col + half * 4:col + half * 4 + 4], nt), op=ALU.add),
                         reads=[("ps", b), "CF"], writes=[dkey])
                    P.op("act", lambda e, dv=dv: e.activation(out=dv, in_=dv, func=AF.Sigmoid), reads=[dkey], writes=[dkey])
                b = bank()
                for q in range(4):
                    c = half * 4 + q
                    P.op("pe", lambda e, q=q, c=c, b=b: e.matmul(ps[b][:, q * 128:q * 128 + nt], lhsT=G2[:, c * 128:(c + 1) * 128], rhs=RWS3[:, 25, :nt], start=True, stop=True),
                         reads=[wkey, "RWS"], writes=[("ps", b)])
                P.op("act", lambda e, b=b, half=half: e.activation(out=v8(T3[:])[:, half * 4:half * 4 + 4, :], in_=psv(b, 4, nt), func=AF.Copy), reads=[("ps", b)], writes=["xdt"])
            t1 = v8(T1); t2 = v8(T2); t4 = v8(T4); t5 = v8(T5); t6 = v8(T6); t7 = v8(T7)
            P.op("pool", lambda e: e.tensor_scalar(out=t1, in0=t1, scalar1=-0.6065306597126334, scalar2=None, op0=ALU.mult), reads=["BIGA"], writes=["BIGA"])
            P.op("dve", lambda e: e.tensor_tensor(out=t4, in0=kT, in1=bc3(CF[:, 134:142], nt), op=ALU.mult), reads=["RWS", "CF"], writes=["BIGB"])
            P.op("pool", lambda e: e.tensor_tensor(out=t7, in0=t4, in1=t4, op=ALU.mult), reads=["BIGB"], writes=["BIGC"])
            for half in range(2):
                b = bank()
                for q in range(4):
                    c = half * 4 + q
                    P.op("pe", lambda e, q=q, c=c, b=b: e.matmul(ps[b][:, q * 128:q * 128 + nt], lhsT=BONES, rhs=t7[:, c, :], start=True, stop=True),
                         reads=["BIGC", "CM"], writes=[("ps", b)])
                P.op("dve", lambda e, b=b, half=half: e.tensor_scalar(out=t5[:, half * 4:half * 4 + 4, :], in0=psv(b, 4, nt), scalar1=1e-24, scalar2=None, op0=ALU.max),
                     reads=[("ps", b)], writes=["BIGC"])
            P.op("act", lambda e: e.activation(out=t5, in_=t5, func=AF.Sqrt), reads=["BIGC"], writes=["BIGC"])
            P.op("dve", lambda e: e.reciprocal(out=t5, in_=t5), reads=["BIGC"], writes=["BIGC"])
            P.op("dve", lambda e: e.tensor_tensor(out=t4, in0=t4, in1=t5, op=ALU.mult), reads=["BIGB", "BIGC"], writes=["BIGB"])
            P.op("dve", lambda e: e.scalar_tensor_tensor(out=t7, in0=t2, scalar=1.0, in1=bc3(CF[:, 142:150], nt), op0=ALU.subtract, op1=ALU.mult),
                 reads=["BIGB", "CF"], writes=["BIGC"])
            P.op("dve", lambda e: e.scalar_tensor_tensor(out=kT, in0=t7, scalar=1.0, in1=kT, op0=ALU.add, op1=ALU.mult), reads=["BIGC", "RWS"], writes=["RWS"])
            P.op("dve", lambda e: e.tensor_tensor(out=t2, in0=t4, in1=t2, op=ALU.mult), reads=["BIGB"], writes=["BIGB"])
            P.op("dve", lambda e: e.tensor_tensor(out=t7, in0=rT, in1=kT, op=ALU.mult), reads=["RWS"], writes=["BIGC"])
            P.op("pool", lambda e: e.tensor_tensor(out=t7, in0=t7, in1=bc3(CF[:, 150:158], nt), op=ALU.mult), reads=["BIGC", "CF"], writes=["BIGC"])
            for half in range(2):
                b = bank()
                for q in range(4):
                    c = half * 4 + q
                    P.op("pe", lambda e, q=q, c=c, b=b: e.matmul(ps[b][:, q * 128:q * 128 + nt], lhsT=BONES, rhs=t7[:, c, :], start=True, stop=True),
                         reads=["BIGC", "CM"], writes=[("ps", b)])
                P.op("dve", lambda e, b=b, half=half: e.tensor_tensor(out=t5[:, half * 4:half * 4 + 4, :], in0=psv(b, 4, nt), in1=vT[:, half * 4:half * 4 + 4, :], op=ALU.mult),
                     reads=[("ps", b), "RWS"], writes=["BIGC"])
            return rT, kT, vT, t1, t2, t4, t5, t6, t7

        def rwkv_post(nt):
            t5 = T5.rearrange("p (c t) -> p c t", c=8)[:, :, :nt]
            y3 = Ys[:nt, :].rearrange("p (h v) -> p h v", h=16)
            P.op("dve", lambda e: e.tensor_reduce(out=sm[:nt, 0:16], in_=y3, axis=AX.X, op=ALU.add), reads=["BIGA"], writes=["sm"])
            P.op("dve", lambda e: e.tensor_scalar(out=sm[:nt, 0:16], in0=sm[:nt, 0:16], scalar1=1.0 / 64, scalar2=None, op0=ALU.mult), reads=["sm"], writes=["sm"])
            P.op("dve", lambda e: e.tensor_tensor(out=y3, in0=y3, in1=sm[:nt, 0:16].unsqueeze(2).to_broadcast([nt, 16, 64]), op=ALU.subtract), reads=["sm", "BIGA"], writes=["BIGA"])
            RH3 = RHSs[:nt, :].rearrange("p (h v) -> p h v", h=16)
            P.op("pool", lambda e: e.tensor_tensor(out=RH3, in0=y3, in1=y3, op=ALU.mult), reads=["BIGA"], writes=["xtok"])
            P.op("dve", lambda e: e.tensor_reduce(out=sm[:nt, 16:32], in_=RH3, axis=AX.X, op=ALU.add), reads=["xtok"], writes=["sm"])
            P.op("act", lambda e: e.activation(out=sm[:nt, 16:32], in_=sm[:nt, 16:32], func=AF.Sqrt, scale=1.0 / 64, bias=64e-5), reads=["sm"], writes=["sm"])
            P.op("dve", lambda e: e.reciprocal(out=sm[:nt, 16:32], in_=sm[:nt, 16:32]), reads=["sm"], writes=["sm"])
            P.op("dve", lambda e: e.tensor_tensor(out=y3, in0=y3, in1=sm[:nt, 16:32].unsqueeze(2).to_broadcast([nt, 16, 64]), op=ALU.mult), reads=["sm", "BIGA"], writes=["BIGA"])
            YT = RHSs.rearrange("p (c t) -> p c t", c=8)[:, :, :nt]
            for half in range(2):
                b = bank()
                for q in range(4):
                    c = half * 4 + q
                    P.op("pe", lambda e, q=q, c=c, b=b: e.transpose(out=ps[b][:, q * 128:q * 128 + nt], in_=Ys[:nt, c * 128:(c + 1) * 128], identity=ident[:nt, :nt]),
                         reads=["BIGA", "CM"], writes=[("ps", b)])
                yv = YT[:, half * 4:half * 4 + 4, :]; cs = slice(half * 4, half * 4 + 4)
                P.op("dve", lambda e, b=b, yv=yv, half=half: e.tensor_tensor(out=yv, in0=psv(b, 4, nt), in1=bc3(CF[:, 160 + half * 4:164 + half * 4], nt), op=ALU.mult),
                     reads=[("ps", b), "CF"], writes=["xtok"])
                P.op("dve", lambda e, yv=yv, half=half: e.tensor_tensor(out=yv, in0=yv, in1=bc3(CF[:, 168 + half * 4:172 + half * 4], nt), op=ALU.add), reads=["xtok", "CF"], writes=["xtok"])
                P.op("dve", lambda e, yv=yv, cs=cs: e.tensor_tensor(out=yv, in0=yv, in1=t5[:, cs, :], op=ALU.add), reads=["xtok", "BIGC"], writes=["xtok"])
                P.op("dve", lambda e, yv=yv, cs=cs: e.tensor_tensor(out=yv, in0=yv, in1=T3[:].rearrange("p (c t) -> p c t", c=8)[:, cs, :nt], op=ALU.mult), reads=["xtok", "xdt"], writes=["xtok"])
                P.op("dve", lambda e, yv=yv, half=half: e.tensor_tensor(out=yv, in0=yv, in1=gateT[:, 8 + half * 4:12 + half * 4, :nt], op=ALU.mult), reads=["xtok", "gateT"], writes=["xtok"])
                P.op("dve", lambda e, yv=yv, cs=cs: e.tensor_tensor(out=mgT[:, cs, :nt], in0=mgT[:, cs, :nt], in1=yv, op=ALU.add), reads=["xtok", "mgT"], writes=["mgT"])

        def mixer_out(nt):
            xres, xk = CUR[0], CUR[1]
            for blk in range(2):
                proj_tok(w_out, blk * 512, 512, mgT, "mgT", nt,
                         lambda b, n, blk=blk: P.op("dve", lambda e: e.tensor_tensor(out=xres[:nt, blk * 512:(blk + 1) * 512], in0=xres[:nt, blk * 512:(blk + 1) * 512],
                                                                                     in1=ps[b][:nt, :512], op=ALU.add), reads=[("ps", b), xk], writes=[xk]))

        def rwkv_chunk():
            nt = 128
            rT, kT, vT, t1, t2, t4, t5, t6, t7 = rwkv_pre(128, rwT[:, :, 0:128], rwT[:, :, 1:129])
            if SUB < 2:
                return
            for c in range(8):
                P.op("dve", lambda e, c=c: e.tensor_tensor_scan(out=T6[:, c * 128:(c + 1) * 128], data0=ONES, data1=T1[:, c * 128:(c + 1) * 128], initial=0.0,
                                                                op0=ALU.mult, op1=ALU.add), reads=["BIGA", "CM"], writes=["BIGA"])
            P.op("act", lambda e: e.activation(out=t7, in_=t6, func=AF.Exp), reads=["BIGA"], writes=["BIGC"])
            P.op("dve", lambda e: e.tensor_tensor(out=RA4[:, :, 0, :], in0=rT, in1=t7, op=ALU.mult), reads=["RWS", "BIGC"], writes=["RA"])
            P.op("dve", lambda e: e.tensor_tensor(out=t7, in0=t6, in1=t1, op=ALU.subtract), reads=["BIGA", "BIGC"], writes=["BIGC"])
            P.op("act", lambda e: e.activation(out=t7, in_=t7, func=AF.Exp), reads=["BIGC"], writes=["BIGC"])
            P.op("dve", lambda e: e.scalar_tensor_tensor(out=RA4[:, :, 1, :], in0=t4, scalar=-1.0, in1=t7, op0=ALU.mult, op1=ALU.mult), reads=["BIGB", "BIGC"], writes=["RA"])
            P.op("act", lambda e: e.activation(out=t7, in_=t6, func=AF.Exp, scale=-1.0), reads=["BIGA", "RA"], writes=["BIGC"])
            P.op("dve", lambda e: e.tensor_tensor(out=BK4[:, :, 0, :], in0=t2, in1=t7, op=ALU.mult), reads=["BIGB", "BIGC"], writes=["BK"])
            P.op("dve", lambda e: e.tensor_tensor(out=BK4[:, :, 1, :], in0=kT, in1=t7, op=ALU.mult), reads=["RWS", "BIGC"], writes=["BK"])
            P.op("dve", lambda e: e.tensor_tensor(out=t7, in0=t6, in1=t6[:, :, 127:128].to_broadcast([128, 8, 128]), op=ALU.subtract), reads=["BIGA", "BK"], writes=["BIGC"])
            P.op("act", lambda e: e.activation(out=t7, in_=t7, func=AF.Exp, scale=-1.0), reads=["BIGC"], writes=["BIGC"])
            P.op("dve", lambda e: e.tensor_tensor(out=BKH4[:, :, 0, :], in0=t2, in1=t7, op=ALU.mult), reads=["BIGB", "BIGC"], writes=["BKH"])
            P.op("dve", lambda e: e.tensor_tensor(out=BKH4[:, :, 1, :], in0=kT, in1=t7, op=ALU.mult), reads=["RWS", "BIGC"], writes=["BKH"])
            P.op("act", lambda e: e.activation(out=smw[:, 0:8], in_=t6[:, :, 127], func=AF.Exp), reads=["BIGA"], writes=["smw"])
            if SUB < 3:
                return
            for (src_fn, dst, dkey, skey) in ((lambda c: RWS3[:, 16 + c, :], vtok, "xdd", "RWS"), (lambda c: BKH4[:, c, 0, :], btok, "ysb", "BKH"),
                                               (lambda c: BKH4[:, c, 1, :], ktok, "junk", "BKH")):
                for half in range(2):
                    b = bank()
                    for q in range(4):
                        c = half * 4 + q
                        P.op("pe", lambda e, q=q, c=c, b=b, src_fn=src_fn: e.transpose(out=ps[b][:, q * 128:(q + 1) * 128], in_=src_fn(c), identity=ident),
                             reads=[skey, "CM"], writes=[("ps", b)])
                    P.op("act", lambda e, b=b, half=half, dst=dst: e.activation(out=dst[:, half * 512:(half + 1) * 512], in_=ps[b][:, :512], func=AF.Copy),
                         reads=[("ps", b)], writes=[dkey])
            if SUB < 4:
                return
            for qd in range(4):
                hs = [4 * qd + i for i in range(4)]
                for hh, h in enumerate(hs):
                    j = h // 2; r0 = (h % 2) * 64
                    b = bank()
                    P.op("pe", lambda e, b=b, j=j, r0=r0: e.matmul(ps[b][:, 0:256], lhsT=BK4[r0:r0 + 64, j, 0, :], rhs=RA[r0:r0 + 64, j * 256:(j + 1) * 256], start=True, stop=True),
                         reads=["BK", "RA"], writes=[("ps", b)])
                    P.op("pe", lambda e, b=b, j=j, r0=r0: e.matmul(ps[b][:, 256:512], lhsT=BK4[r0:r0 + 64, j, 1, :], rhs=RA[r0:r0 + 64, j * 256:(j + 1) * 256], start=True, stop=True),
                         reads=["BK", "RA"], writes=[("ps", b)])
                    P.op("dve", lambda e, b=b, hh=hh: e.tensor_tensor(out=ATq[:, hh, :, :], in0=ps[b][:, :512].rearrange("p (k t) -> p k t", k=4), in1=MASK4[:], op=ALU.mult),
                         reads=[("ps", b), "MASK4"], writes=["RWS"])
                if SUB2 < 2:
                    continue
                bP = [bank(), bank()]
                for hh, h in enumerate(hs):
                    j = h // 2; r0 = (h % 2) * 64; b = bP[hh % 2]; o = (hh // 2) * 128
                    P.op("pe", lambda e, b=b, j=j, r0=r0, o=o: e.matmul(ps[b][:, o:o + 128], lhsT=RA4[r0:r0 + 64, j, 1, :], rhs=BK4[r0:r0 + 64, j, 0, :], start=True, stop=True),
                         reads=["BK", "RA"], writes=[("ps", b)])
                MN0v = MN[0].rearrange("p (a two t) -> p a two t", two=2, t=128)
                for par in range(2):
                    P.op("dve", lambda e, par=par, bP=bP, MN0v=MN0v: e.tensor_tensor(out=MN0v[:, :, par, :], in0=v3(ps[bP[par]][:, :256], 2), in1=M1.unsqueeze(1).to_broadcast([128, 2, 128]), op=ALU.mult),
                         reads=[("ps", bP[par]), "CM"], writes=["RWS"])
                if os.environ.get("K_DBG") and qd < 2:
                    P.dma("sp", lambda e, qd=qd: e.dma_start(out=y_p[1280 + qd * 128:1408 + qd * 128, 0:512], in_=MN[0]), reads=["RWS"])
                if SUB3 >= 2:
                    P.op("dve", lambda e: e.tensor_tensor(out=v3(TT[0], 4), in0=ATq[:, :, 1, :], in1=ident.unsqueeze(1).to_broadcast([128, 4, 128]), op=ALU.add),
                         reads=["RWS", "CM"], writes=["convT"])
                mi = 0; ni = None; ti = 0
                if SUB2 < 3:
                    continue
                for lvl in range(6):
                    bM = bank()
                    for hh in range(4):
                        nl = ATq[:, hh, 1, :] if ni is None else NN[ni][:, hh * 128:(hh + 1) * 128]
                        P.op("pe", lambda e, bM=bM, hh=hh, nl=nl, mi=mi: e.matmul(ps[bM][:, hh * 128:(hh + 1) * 128], lhsT=nl, rhs=MN[mi][:, hh * 128:(hh + 1) * 128], start=True, stop=True),
                             reads=["RWS", "xn", "RWS"], writes=[("ps", bM)])
                    if lvl < 5:
                        bN = bank()
                        for hh in range(4):
                            nl = ATq[:, hh, 1, :] if ni is None else NN[ni][:, hh * 128:(hh + 1) * 128]
                            P.op("pe", lambda e, bN=bN, hh=hh, nl=nl, mi=mi: e.matmul(ps[bN][:, hh * 128:(hh + 1) * 128], lhsT=MN[mi][:, hh * 128:(hh + 1) * 128], rhs=nl, start=True, stop=True),
                                 reads=["RWS", "xn", "RWS"], writes=[("ps", bN)])
                        nn = 0 if ni is None else 1 - ni
                        P.op("dve", lambda e, bN=bN, nn=nn: e.tensor_copy(out=NN[nn], in_=ps[bN][:, :512]), reads=[("ps", bN)], writes=["xn"])
                        ni = nn
                    mn = 1 - mi
                    P.op("act", lambda e, bM=bM, mn=mn: e.activation(out=MN[mn], in_=ps[bM][:, :512], func=AF.Copy), reads=[("ps", bM)], writes=["RWS"])
                    mi = mn
                    bT = bank()
                    for hh in range(4):
                        P.op("pe", lambda e, bT=bT, hh=hh, mi=mi, ti=ti: e.matmul(ps[bT][:, hh * 128:(hh + 1) * 128], lhsT=MN[mi][:, hh * 128:(hh + 1) * 128],
                                                                                   rhs=TT[ti][:, hh * 128:(hh + 1) * 128], start=True, stop=True),
                             reads=["RWS", "convT"], writes=[("ps", bT)])
                    tn = 1 - ti
                    P.op("dve", lambda e, bT=bT, ti=ti, tn=tn: e.tensor_tensor(out=TT[tn], in0=TT[ti], in1=ps[bT][:, :512], op=ALU.add),
                         reads=[("ps", bT), "convT"], writes=["convT"])
                    ti = tn
                if SUB2 < 4:
                    continue
                if os.environ.get("K_DBG") and qd < 2:
                    P.dma("sp", lambda e, qd=qd, ti=ti: e.dma_start(out=y_p[1280 + qd * 128:1408 + qd * 128, 512:1024], in_=TT[ti]), reads=["convT"])
                    P.dma("sp", lambda e, qd=qd: e.dma_start(out=y_p[1536 + qd * 256:1664 + qd * 256, :], in_=RWS[:, 0:1024]), reads=["RWS"])
                    P.dma("sp", lambda e, qd=qd: e.dma_start(out=y_p[1664 + qd * 256:1792 + qd * 256, :], in_=RWS[:, 1024:2048]), reads=["RWS"])
                bR = [bank(), bank()]
                for hh, h in enumerate(hs):
                    j = h // 2; r0 = (h % 2) * 64; b = bR[hh % 2]; o = (hh // 2) * 64
                    P.op("pe", lambda e, b=b, o=o, j=j, r0=r0: e.matmul(ps[b][:, o:o + 64], lhsT=RA4[r0:r0 + 64, j, 1, :], rhs=Hst[r0:r0 + 64, j, :], start=True, stop=False),
                         reads=["RA", "Hst"], writes=[("ps", b)])
                    P.op("pe", lambda e, b=b, o=o, hh=hh, h=h: e.matmul(ps[b][:, o:o + 64], lhsT=ATq[:, hh, 3, :], rhs=vtok[:, h * 64:(h + 1) * 64], start=False, stop=True),
                         reads=["RWS", "xdd"], writes=[("ps", b)])
                for par in range(2):
                    P.op("act", lambda e, par=par, qd=qd, bR=bR: e.activation(out=RHSs[:, qd * 256:(qd + 1) * 256].rearrange("p (a two v) -> p a two v", two=2, v=64)[:, :, par, :],
                                                                       in_=v3(ps[bR[par]][:, :128], 2), func=AF.Copy), reads=[("ps", bR[par])], writes=["xtok"])
                if SUB2 < 5:
                    continue
                bU = bank()
                for hh, h in enumerate(hs):
                    P.op("pe", lambda e, bU=bU, hh=hh, h=h, ti=ti: e.matmul(ps[bU][:, hh * 64:(hh + 1) * 64], lhsT=TT[ti][:, hh * 128:(hh + 1) * 128], rhs=RHSs[:, h * 64:(h + 1) * 64], start=True, stop=True),
                         reads=["convT", "xtok"], writes=[("ps", bU)])
                P.op("act", lambda e, bU=bU, qd=qd: e.activation(out=Us[:, qd * 256:(qd + 1) * 256], in_=ps[bU][:, :256], func=AF.Copy), reads=[("ps", bU)], writes=["BIGA"])
                bY = [bank(), bank()]
                for hh, h in enumerate(hs):
                    j = h // 2; r0 = (h % 2) * 64; b = bY[hh % 2]; o = (hh // 2) * 64
                    P.op("pe", lambda e, b=b, o=o, j=j, r0=r0: e.matmul(ps[b][:, o:o + 64], lhsT=RA4[r0:r0 + 64, j, 0, :], rhs=Hst[r0:r0 + 64, j, :], start=True, stop=False),
                         reads=["RA", "Hst"], writes=[("ps", b)])
                    P.op("pe", lambda e, b=b, o=o, hh=hh, h=h: e.matmul(ps[b][:, o:o + 64], lhsT=ATq[:, hh, 0, :], rhs=Us[:, h * 64:(h + 1) * 64], start=False, stop=False),
                         reads=["RWS", "BIGA"], writes=[("ps", b)])
                    P.op("pe", lambda e, b=b, o=o, hh=hh, h=h: e.matmul(ps[b][:, o:o + 64], lhsT=ATq[:, hh, 2, :], rhs=vtok[:, h * 64:(h + 1) * 64], start=False, stop=True),
                         reads=["RWS", "xdd"], writes=[("ps", b)])
                for par in range(2):
                    P.op("act", lambda e, par=par, qd=qd, bY=bY: e.activation(out=Ys[:, qd * 256:(qd + 1) * 256].rearrange("p (a two v) -> p a two v", two=2, v=64)[:, :, par, :],
                                                                       in_=v3(ps[bY[par]][:, :128], 2), func=AF.Copy), reads=[("ps", bY[par])], writes=["BIGA"])
            if SUB < 5:
                return
            for half in range(2):
                b = bank()
                for q in range(4):
                    j = half * 4 + q
                    P.op("pe", lambda e, b=b, q=q, j=j: e.matmul(ps[b][:, q * 128:(q + 1) * 128], lhsT=btok[:, j * 128:(j + 1) * 128], rhs=Us[:, j * 128:(j + 1) * 128], start=True, stop=False),
                         reads=["ysb", "BIGA"], writes=[("ps", b)])
                    P.op("pe", lambda e, b=b, q=q, j=j: e.matmul(ps[b][:, q * 128:(q + 1) * 128], lhsT=ktok[:, j * 128:(j + 1) * 128], rhs=vtok[:, j * 128:(j + 1) * 128], start=False, stop=True),
                         reads=["junk", "xdd"], writes=[("ps", b)])
                for hp in range(2):
                    r0 = hp * 64
                    hv = Hst[r0:r0 + 64, half * 4:half * 4 + 4, :]
                    P.op("dve", lambda e, r0=r0, half=half, hv=hv: e.tensor_tensor(out=hv, in0=hv, in1=smw[r0:r0 + 64, half * 4:half * 4 + 4].unsqueeze(2).to_broadcast([64, 4, 64]), op=ALU.mult),
                         reads=["Hst", "smw"], writes=["Hst"])
                    P.op("dve", lambda e, b=b, r0=r0, hp=hp, hv=hv: e.tensor_tensor(out=hv, in0=hv, in1=v3(ps[b][r0:r0 + 64, :512], 4)[:, :, hp * 64:(hp + 1) * 64], op=ALU.add),
                         reads=[("ps", b), "Hst"], writes=["Hst"])
            if os.environ.get("K_DBG"):
                P.dma("sp", lambda e: e.dma_start(out=y_p[0:128, :], in_=RHSs), reads=["xtok"])
                P.dma("sp", lambda e: e.dma_start(out=y_p[128:256, :], in_=Us), reads=["BIGA"])
                P.dma("sp", lambda e: e.dma_start(out=y_p[256:384, :], in_=Ys), reads=["BIGA"])
                P.dma("sp", lambda e: e.dma_start(out=y_p[384:512, :], in_=vtok[:]), reads=["xdd"])
                P.dma("sp", lambda e: e.dma_start(out=y_p[512:640, :], in_=btok[:]), reads=["ysb"])
                P.dma("sp", lambda e: e.dma_start(out=y_p[640:768, :], in_=ktok[:]), reads=["junk"])
                P.dma("sp", lambda e: e.dma_start(out=y_p[768:896, :], in_=RA[:, 0:1024]), reads=["RA"])
                P.dma("sp", lambda e: e.dma_start(out=y_p[896:1024, :], in_=RA[:, 1024:2048]), reads=["RA"])
                P.dma("sp", lambda e: e.dma_start(out=y_p[1024:1152, :], in_=BK[:, 0:1024]), reads=["BK"])
                P.dma("sp", lambda e: e.dma_start(out=y_p[1152:1280, :], in_=BK[:, 1024:2048]), reads=["BK"])
            if SUB < 6:
                return
            rwkv_post(128)


        qT = gateT
        PRB = BIGA
        PTt = BIGB
        OSB = BIGC

        def attn_tail(nt):
            xres, xk = CUR[0], CUR[1]
            oT = OSB[:, 1024:2048].rearrange("p (c t) -> p c t", c=8)
            for half in range(2):
                b = bank()
                for q in range(4):
                    c = half * 4 + q
                    P.op("pe", lambda e, q=q, c=c, b=b: e.transpose(out=ps[b][:, q * 128:q * 128 + nt], in_=OSB[:nt, c * 128:(c + 1) * 128], identity=ident[:nt, :nt]),
                         reads=["BIGC", "CM"], writes=[("ps", b)])
                P.op("act", lambda e, b=b, half=half: e.activation(out=oT[:, half * 4:half * 4 + 4, :nt], in_=psv(b, 4, nt), func=AF.Copy), reads=[("ps", b)], writes=["BIGC"])
            for blk in range(2):
                proj_tok(w_mo, blk * 512, 512, oT, "BIGC", nt,
                         lambda b, n, blk=blk: P.op("dve", lambda e: e.tensor_tensor(out=xres[:nt, blk * 512:(blk + 1) * 512], in0=xres[:nt, blk * 512:(blk + 1) * 512],
                                                                                     in1=ps[b][:nt, :512], op=ALU.add), reads=[("ps", b), xk], writes=[xk]))

        def attn_prompt():
            xres, xk = CUR[0], CUR[1]
            nt = 128
            rmsnorm_T(xres, nt, 8, hT, xk, "hT")
            for blk in range(2):
                proj_fm(w_mq, blk * 512, 4, hT, "hT", nt,
                        lambda b, n, blk=blk: P.op("act", lambda e: e.activation(out=qT[:, blk * 4:blk * 4 + 4, :nt], in_=psv(b, 4, nt), func=AF.Copy), reads=[("ps", b)], writes=["gateT"]))
            sb_ = [bank(), bank()]
            for h in range(4):
                b = sb_[h // 2]; o = (h % 2) * 256
                for dc in range(2):
                    P.op("pe", lambda e, h=h, dc=dc, b=b, o=o: e.matmul(ps[b][:nt, o:o + 256], lhsT=qT[:, 2 * h + dc, :nt], rhs=KT[:, 2 * h + dc, :], start=(dc == 0), stop=(dc == 1)),
                         reads=["gateT", "KT"], writes=[("ps", b)])
            for hb in range(2):
                P.op("dve", lambda e, hb=hb, sb_=sb_: e.tensor_reduce(out=sm[:nt, hb * 2:hb * 2 + 2], in_=ps[sb_[hb]][:nt, :512].rearrange("p (h m) -> p h m", h=2), axis=AX.X, op=ALU.max),
                     reads=[("ps", sb_[hb])], writes=["sm"])
            P.op("dve", lambda e: e.tensor_scalar(out=sm[:nt, 4:8], in0=sm[:nt, 0:4], scalar1=-1.0 / 16, scalar2=None, op0=ALU.mult), reads=["sm"], writes=["sm"])
            for h in range(4):
                b = sb_[h // 2]; o = (h % 2) * 256
                P.op("act", lambda e, h=h, b=b, o=o: e.activation(out=PRB[:nt, h * 256:(h + 1) * 256], in_=ps[b][:nt, o:o + 256], func=AF.Exp, scale=1.0 / 16, bias=sm[:nt, 4 + h:5 + h],
                                                                  accum_out=sm[:nt, 8 + h:9 + h]), reads=[("ps", b), "sm"], writes=["BIGA", "sm"])
            P.op("dve", lambda e: e.reciprocal(out=sm[:nt, 12:16], in_=sm[:nt, 8:12]), reads=["sm"], writes=["sm"])
            PT3 = PTt[:].rearrange("p (c t) -> p c t", c=16)
            for half in range(2):
                b = bank()
                for q in range(4):
                    c = half * 4 + q
                    P.op("pe", lambda e, q=q, c=c, b=b: e.transpose(out=ps[b][:, q * 128:q * 128 + nt], in_=PRB[:nt, c * 128:(c + 1) * 128], identity=ident[:nt, :nt]),
                         reads=["BIGA", "CM"], writes=[("ps", b)])
                P.op("act", lambda e, b=b, half=half: e.activation(out=PT3[:, half * 4:half * 4 + 4, :nt], in_=psv(b, 4, nt), func=AF.Copy), reads=[("ps", b)], writes=["BIGB"])
            ob = [bank(), bank()]
            for h in range(4):
                b = ob[h // 2]; o = (h % 2) * 256
                for mb in range(2):
                    P.op("pe", lambda e, h=h, mb=mb, b=b, o=o: e.matmul(ps[b][:nt, o:o + 256], lhsT=PT3[:, h * 2 + mb, :nt], rhs=Vt[:, mb, h * 256:(h + 1) * 256], start=(mb == 0), stop=(mb == 1)),
                         reads=["BIGB", "Vt"], writes=[("ps", b)])
            for h in range(4):
                b = ob[h // 2]; o = (h % 2) * 256
                P.op("act", lambda e, h=h, b=b, o=o: e.activation(out=OSB[:nt, h * 256:(h + 1) * 256], in_=ps[b][:nt, o:o + 256], func=AF.Copy, scale=sm[:nt, 12 + h:13 + h]),
                     reads=[("ps", b), "sm"], writes=["BIGC"])
            attn_tail(nt)


        NEG = -1.0e30

        def fence(keys):
            P.op("pool", lambda e: e.memset(st1[:, 7:8], 0.0), reads=[], writes=list(keys) + ["st1"])

        def top16(nt, vals_in, key_in, nsets, width, Vout, Iout):
            for r in range(2):
                for s_ in range(nsets):
                    src = vals_in[:nt, s_, :]; vo = Vout[:nt, s_, r * 8:(r + 1) * 8]
                    P.op("dve", lambda e, vo=vo, src=src: e.max(out=vo, in_=src), reads=[key_in, ("tk_in", s_), "pk_v"], writes=[("tk_v", s_, r)])
                for s_ in range(nsets):
                    src = vals_in[:nt, s_, :]; vo = Vout[:nt, s_, r * 8:(r + 1) * 8]; io = Iout[:nt, s_, r * 8:(r + 1) * 8]
                    P.op("dve", lambda e, vo=vo, io=io, src=src: e.max_index(out=io, in_max=vo, in_values=src), reads=[key_in, ("tk_in", s_), ("tk_v", s_, r), "pk_i"], writes=[("tk_i", s_, r)])
                if r == 0:
                    for s_ in range(nsets):
                        src = vals_in[:nt, s_, :]; vo = Vout[:nt, s_, 0:8]
                        P.op("dve", lambda e, vo=vo, src=src: e.match_replace(out=src, in_to_replace=vo, in_values=src, imm_value=NEG),
                             reads=[key_in, ("tk_v", s_, 0), ("tk_i", s_, 0)], writes=[("tk_in", s_)])
            fine = [("tk_in", s_) for s_ in range(nsets)] + [("tk_v", s_, r) for s_ in range(nsets) for r in range(2)] + [("tk_i", s_, r) for s_ in range(nsets) for r in range(2)]
            P.op("pool", lambda e: e.memset(st1[:, 6:7], 0.0), reads=fine, writes=fine + [key_in, "pk_v", "pk_i"])

        def peer(nt):
            xres, xk = CUR[0], CUR[1]
            SM = xdd
            V1 = SM[:, 0:256].rearrange("p (s k) -> p s k", s=16); I1 = SM[:, 256:512].bitcast(U32).rearrange("p (s k) -> p s k", s=16)
            I1f = SM[:, 512:768].rearrange("p (s k) -> p s k", s=16)
            TV = SM[:, 768:896].rearrange("p (h k) -> p h k", h=8); TP = SM[:, 896:1024].bitcast(U32).rearrange("p (h k) -> p h k", h=8)
            S2 = ysb
            TPf = S2[:, 0:128].rearrange("p (h k) -> p h k", h=8); Af = S2[:, 128:256].rearrange("p (h k) -> p h k", h=8)
            Bf = S2[:, 256:384].rearrange("p (h k) -> p h k", h=8); E1 = S2[:, 384:512].rearrange("p (h k) -> p h k", h=8)
            E2 = S2[:, 512:640].rearrange("p (h k) -> p h k", h=8); IDX = S2[:, 640:768].bitcast(U32)
            GATE = S2[:, 768:896].rearrange("p (h k) -> p h k", h=8); ACTV = S2[:, 896:1024]
            fence(["xdd", "ysb", "pk_v", "pk_i", "pk_s"])
            rmsnorm_T(xres, nt, 16, hT, xk, "hT")
            H3 = xdt
            for half in range(2):
                b = bank()
                for q in range(4):
                    c = half * 4 + q
                    P.op("pe", lambda e, q=q, c=c, b=b: e.transpose(out=ps[b][:nt, q * 128:(q + 1) * 128], in_=hT[:, c, :nt], identity=ident), reads=["hT", "CM"], writes=[("ps", b)])
                P.op("act", lambda e, b=b, half=half: e.activation(out=H3[:nt, half * 512:(half + 1) * 512], in_=ps[b][:nt, :512], func=AF.Copy), reads=[("ps", b)], writes=["xdt"])
            H3b = xtok[:].bitcast(BF16)[:, 0:1024]
            P.op("pool", lambda e: e.tensor_copy(out=H3b[:nt, :], in_=H3[:nt, :]), reads=["xdt"], writes=["xtok"])
            QT = gateT
            for blk in range(4):
                proj_fm(w_pq, blk * 512, 4, hT, "hT", nt,
                        lambda b, n, blk=blk: P.op("act", lambda e: e.activation(out=QT[:, blk * 4:blk * 4 + 4, :nt], in_=psv(b, 4, nt), func=AF.Copy), reads=[("ps", b)], writes=["gateT"]))
            SKr = BIGA[:].rearrange("p (s d) -> p s d", s=16); SKT = BIGB[:].rearrange("p (s n) -> p s n", s=16)
            P.dma("sp", lambda e: e.dma_start(out=SKr, in_=sub_keys.rearrange("s n d -> n s d")), writes=["BIGA"])
            for grp in range(4):
                b = bank()
                for q in range(4):
                    hc = grp * 4 + q
                    P.op("pe", lambda e, q=q, hc=hc, b=b: e.transpose(out=ps[b][:, q * 128:(q + 1) * 128], in_=SKr[:, hc, :], identity=ident), reads=["BIGA", "CM"], writes=[("ps", b)])
                P.op("act", lambda e, b=b, grp=grp: e.activation(out=SKT[:, grp * 4:grp * 4 + 4, :], in_=psv(b, 4, 128), func=AF.Copy), reads=[("ps", b)], writes=["BIGB"])
            SC = BIGC[:].rearrange("p (s n) -> p s n", s=16)
            for grp in range(4):
                b = bank()
                for q in range(4):
                    hc = grp * 4 + q
                    P.op("pe", lambda e, q=q, hc=hc, b=b: e.matmul(ps[b][:nt, q * 128:(q + 1) * 128], lhsT=QT[:, hc, :nt], rhs=SKT[:, hc, :], start=True, stop=True),
                         reads=["gateT", "BIGB"], writes=[("ps", b)])
                P.op("act", lambda e, b=b, grp=grp: e.activation(out=SC[:nt, grp * 4:grp * 4 + 4, :], in_=ps[b][:nt, :512].rearrange("p (q n) -> p q n", q=4), func=AF.Copy),
                     reads=[("ps", b)], writes=["BIGC"])
            top16(nt, SC, "BIGC", 16, 128, V1, I1)
            P.op("dve", lambda e: e.tensor_copy(out=I1f[:nt], in_=I1[:nt]), reads=["pk_i"], writes=["pk_s"])
            V1h = SM[:, 0:256].rearrange("p (h c k) -> p h c k", h=8, c=2)
            CAND = BIGA[:].rearrange("p (h a b) -> p h a b", h=8, a=16)
            P.op("dve", lambda e: e.tensor_tensor(out=CAND[:nt], in0=V1h[:nt, :, 0, :].unsqueeze(3).to_broadcast([nt, 8, 16, 16]),
                                                  in1=V1h[:nt, :, 1, :].unsqueeze(2).to_broadcast([nt, 8, 16, 16]), op=ALU.add), reads=["pk_v"], writes=["BIGA"])
            top16(nt, BIGA[:].rearrange("p (h c) -> p h c", h=8), "BIGA", 8, 256, TV, TP)
            Au = S2[:, 384:512].bitcast(U32).rearrange("p (h k) -> p h k", h=8); Bu = S2[:, 512:640].bitcast(U32).rearrange("p (h k) -> p h k", h=8)
            P.op("dve", lambda e: e.tensor_scalar(out=Au[:nt], in0=TP[:nt], scalar1=4, scalar2=None, op0=ALU.logical_shift_right), reads=["pk_i"], writes=["pk_s"])
            P.op("dve", lambda e: e.tensor_scalar(out=Bu[:nt], in0=TP[:nt], scalar1=15, scalar2=None, op0=ALU.bitwise_and), reads=["pk_i"], writes=["pk_s"])
            P.op("dve", lambda e: e.tensor_copy(out=Af[:nt], in_=Au[:nt]), reads=["pk_s"], writes=["pk_s"])
            P.op("dve", lambda e: e.tensor_copy(out=Bf[:nt], in_=Bu[:nt]), reads=["pk_s"], writes=["pk_s"])
            I1h = SM[:, 512:768].rearrange("p (h c k) -> p h c k", h=8, c=2)
            OH = BIGB[:].rearrange("p (h k a) -> p h k a", h=8, k=16)
            for (sel, cc, Eo) in ((Af, 0, E1), (Bf, 1, E2)):
                P.op("dve", lambda e, sel=sel: e.tensor_tensor(out=OH[:nt], in0=sel[:nt].unsqueeze(3).to_broadcast([nt, 8, 16, 16]),
                                                               in1=IOTA[:nt].unsqueeze(1).unsqueeze(1).to_broadcast([nt, 8, 16, 16]), op=ALU.is_equal), reads=["pk_s", "CM"], writes=["BIGB"])
                P.op("dve", lambda e, cc=cc: e.tensor_tensor(out=OH[:nt], in0=OH[:nt], in1=I1h[:nt, :, cc, :].unsqueeze(2).to_broadcast([nt, 8, 16, 16]), op=ALU.mult),
                     reads=["BIGB", "pk_s"], writes=["BIGB"])
                P.op("dve", lambda e, Eo=Eo: e.tensor_reduce(out=Eo[:nt], in_=OH[:nt], axis=AX.X, op=ALU.add), reads=["BIGB"], writes=["pk_s"])
            P.op("dve", lambda e: e.scalar_tensor_tensor(out=E1[:nt], in0=E1[:nt], scalar=128.0, in1=E2[:nt], op0=ALU.mult, op1=ALU.add), reads=["pk_s"], writes=["pk_s"])
            P.op("dve", lambda e: e.tensor_copy(out=IDX[:nt, :], in_=S2[:nt, 384:512]), reads=["pk_s"], writes=["pk_idx"])
            P.op("dve", lambda e: e.tensor_tensor(out=GATE[:nt], in0=TV[:nt], in1=TV[:nt, :, 0:1].to_broadcast([nt, 8, 16]), op=ALU.subtract), reads=["pk_v"], writes=["pk_s"])
            P.op("act", lambda e: e.activation(out=GATE[:nt], in_=GATE[:nt], func=AF.Exp), reads=["pk_s"], writes=["pk_s"])
            P.op("dve", lambda e: e.tensor_reduce(out=sm[:nt, 0:8], in_=GATE[:nt], axis=AX.X, op=ALU.add), reads=["pk_s"], writes=["sm"])
            P.op("dve", lambda e: e.reciprocal(out=sm[:nt, 8:16], in_=sm[:nt, 0:8]), reads=["sm"], writes=["sm"])
            P.op("dve", lambda e: e.tensor_tensor(out=GATE[:nt], in0=GATE[:nt], in1=sm[:nt, 8:16].unsqueeze(2).to_broadcast([nt, 8, 16]), op=ALU.mult), reads=["pk_s", "sm"], writes=["pk_s"])
            def apply():
                GB = []
                for ti_, tl in enumerate((RA, BK, BKH)):
                    v16 = tl[:].bitcast(BF16)
                    for q in range(4):
                        GB.append((v16[:, q * 1024:(q + 1) * 1024], ("g", ti_ * 4 + q)))
                fence(["RA", "BK", "BKH"] + [k for _, k in GB])
                gi = 0
                for hk in range(128):
                    buf, gk = GB[gi % len(GB)]; gi += 1
                    P.dma("pool", lambda e, buf=buf, hk=hk: e.indirect_dma_start(out=buf[:nt, :], out_offset=None, in_=ub,
                                                                                  in_offset=bass.IndirectOffsetOnAxis(ap=IDX[:nt, hk:hk + 1], axis=0)),
                          reads=["pk_idx", "ub", "vb"], writes=[gk])
                    P.op("dve", lambda e, buf=buf, hk=hk: e.scalar_tensor_tensor(out=buf[:nt, :], in0=buf[:nt, :], scalar=1.0, in1=H3b[:nt, :], op0=ALU.mult, op1=ALU.mult,
                                                                                accum_out=ACTV[:nt, hk:hk + 1]), reads=[gk, "xtok"], writes=[gk, "pk_a"])
                COEF = ACTV
                P.op("act", lambda e: e.activation(out=COEF[:nt, :], in_=ACTV[:nt, :], func=AF.Gelu), reads=["pk_a"], writes=["pk_a"])
                P.op("dve", lambda e: e.tensor_tensor(out=COEF[:nt, :], in0=COEF[:nt, :], in1=S2[:nt, 768:896], op=ALU.mult), reads=["pk_a", "pk_s"], writes=["pk_a"])
                ACC = junk
                P.op("pool", lambda e: e.memset(ACC[:nt, :], 0.0), writes=["junk"])
                for hk in range(128):
                    buf, gk = GB[gi % len(GB)]; gi += 1
                    P.dma("pool", lambda e, buf=buf, hk=hk: e.indirect_dma_start(out=buf[:nt, :], out_offset=None, in_=vb,
                                                                                  in_offset=bass.IndirectOffsetOnAxis(ap=IDX[:nt, hk:hk + 1], axis=0)),
                          reads=["pk_idx", "ub", "vb"], writes=[gk])
                    P.op("dve", lambda e, buf=buf, hk=hk: e.scalar_tensor_tensor(out=ACC[:nt, :], in0=buf[:nt, :], scalar=COEF[:nt, hk:hk + 1], in1=ACC[:nt, :], op0=ALU.mult, op1=ALU.add),
                         reads=[gk, "pk_a", "junk"], writes=["junk"])
                P.op("dve", lambda e: e.tensor_tensor(out=xres[:nt, :], in0=xres[:nt, :], in1=ACC[:nt, :], op=ALU.add), reads=["junk", xk], writes=[xk])
                fence(["RA", "BK", "BKH", "xdd", "ysb"] + [k for _, k in GB] + ["pk_v", "pk_i", "pk_s", "pk_idx", "pk_a"])

            return apply

        def final_out(nt, dst):
            xres, xk = CUR[0], CUR[1]
            P.op("act", lambda e: e.activation(out=junk[:nt, :], in_=xres[:nt, :], func=AF.Square, accum_out=st1[:nt, 0:1]), reads=[xk], writes=["junk", "st1"])
            P.op("act", lambda e: e.activation(out=st1[:nt, 1:2], in_=st1[:nt, 0:1], func=AF.Sqrt, scale=1.0 / D, bias=EPS), reads=["st1"], writes=["st1"])
            P.op("dve", lambda e: e.reciprocal(out=st1[:nt, 2:3], in_=st1[:nt, 1:2]), reads=["st1"], writes=["st1"])
            P.op("dve", lambda e: e.scalar_tensor_tensor(out=xn[:nt, :], in0=xres[:nt, :], scalar=st1[:nt, 2:3], in1=CT[:nt, 48:48 + D], op0=ALU.mult, op1=ALU.mult),
                 reads=["st1", xk, "CT"], writes=["xn"])
            P.dma("sp", lambda e: e.dma_start(out=dst, in_=xn[:nt, :]), reads=["xn"])


        def attn_sample():
            xres, xk = CUR[0], CUR[1]
            nt = NS
            rmsnorm_T(xres, nt, 8, hT, xk, "hT")
            QS = xdt
            for blk in range(2):
                proj_tok(w_mq, blk * 512, 512, hT, "hT", nt,
                         lambda b, n, blk=blk: P.op("act", lambda e: e.activation(out=QS[:nt, blk * 512:(blk + 1) * 512], in_=ps[b][:nt, :512], func=AF.Copy), reads=[("ps", b)], writes=["xdt"]))
            SELa = BK[:16, 0:2048].rearrange("p (s m) -> p s m", s=16)
            P.op("dve", lambda e: e.tensor_copy(out=SELa, in_=ident[:16, :16].unsqueeze(2).to_broadcast([16, 16, 128])), reads=["CM"], writes=["BK"])
            SCs = xdd[:, 0:128]; PRs = xdd[:, 256:512]; PT2 = xdd[:, 512:640]; OH16 = xdd[:, 640:896].rearrange("p (a b) -> p a b", a=16)
            PZ = RA[:].rearrange("p (s mb h t) -> p s mb h t", s=16, mb=2, h=4)
            PR = BIGC
            for si in range(NS):
                Kb, kkey = (BIGA, "BIGA") if si % 2 == 0 else (BIGB, "BIGB")
                P.dma("sp", lambda e, si=si, Kb=Kb: e.dma_start(out=Kb[:].rearrange("p (mb d) -> p mb d", mb=2), in_=ck[si].rearrange("(mb p) d -> p mb d", p=128)), writes=[kkey])
                qb = [bank(), bank()]
                for half in range(2):
                    P.op("pe", lambda e, half=half, qb=qb, si=si: e.matmul(ps[qb[half]][:, :512], lhsT=SELa[:, si, :], rhs=QS[:nt, half * 512:(half + 1) * 512], start=True, stop=True),
                         reads=["BK", "xdt"], writes=[("ps", qb[half])])
                for mb in range(2):
                    for half in range(2):
                        P.op("dve", lambda e, mb=mb, half=half, qb=qb, Kb=Kb: e.tensor_tensor(out=PR[:, mb * 1024 + half * 512:mb * 1024 + (half + 1) * 512],
                                                                                             in0=Kb[:, mb * 1024 + half * 512:mb * 1024 + (half + 1) * 512], in1=ps[qb[half]][:, :512], op=ALU.mult),
                             reads=[kkey, ("ps", qb[half])], writes=["BIGC"])
                P.op("dve", lambda e, si=si: e.tensor_reduce(out=SCs.rearrange("p (mb s h) -> p mb s h", mb=2, s=16)[:, :, si, :], in_=PR[:].rearrange("p (mb h d) -> p mb h d", mb=2, h=4),
                                                             axis=AX.X, op=ALU.add), reads=["BIGC"], writes=["xdd"])
            b = bank()
            for mb in range(2):
                P.op("pe", lambda e, mb=mb, b=b: e.transpose(out=ps[b][:64, mb * 128:(mb + 1) * 128], in_=SCs[:, mb * 64:(mb + 1) * 64], identity=ident), reads=["xdd", "CM"], writes=[("ps", b)])
            P.op("dve", lambda e, b=b: e.tensor_reduce(out=sm[:64, 0:1], in_=ps[b][:64, :256], axis=AX.X, op=ALU.max), reads=[("ps", b)], writes=["sm"])
            P.op("dve", lambda e: e.tensor_scalar(out=sm[:64, 1:2], in0=sm[:64, 0:1], scalar1=-1.0 / 16, scalar2=None, op0=ALU.mult), reads=["sm"], writes=["sm"])
            P.op("act", lambda e, b=b: e.activation(out=PRs[:64, :], in_=ps[b][:64, :256], func=AF.Exp, scale=1.0 / 16, bias=sm[:64, 1:2], accum_out=sm[:64, 2:3]),
                 reads=[("ps", b), "sm"], writes=["xdd", "sm"])
            P.op("dve", lambda e: e.reciprocal(out=sm[:64, 3:4], in_=sm[:64, 2:3]), reads=["sm"], writes=["sm"])
            P.op("dve", lambda e: e.tensor_scalar(out=PRs[:64, :], in0=PRs[:64, :], scalar1=sm[:64, 3:4], scalar2=None, op0=ALU.mult), reads=["sm", "xdd"], writes=["xdd"])
            b = bank()
            for mb in range(2):
                P.op("pe", lambda e, mb=mb, b=b: e.transpose(out=ps[b][:, mb * 64:(mb + 1) * 64], in_=PRs[:64, mb * 128:(mb + 1) * 128], identity=ident[:64, :64]), reads=["xdd", "CM"], writes=[("ps", b)])
            P.op("act", lambda e, b=b: e.activation(out=PT2, in_=ps[b][:, :128], func=AF.Copy), reads=[("ps", b)], writes=["xdd"])
            P.op("dve", lambda e: e.tensor_tensor(out=OH16, in0=IOTA.unsqueeze(2).to_broadcast([128, 16, 16]), in1=IOTA.unsqueeze(1).to_broadcast([128, 16, 16]), op=ALU.is_equal),
                 reads=["CM"], writes=["xdd"])
            PT4 = PT2.rearrange("p (mb s h) -> p mb s h", mb=2, s=16)
            for mb in range(2):
                for h in range(4):
                    P.op("dve", lambda e, mb=mb, h=h: e.tensor_tensor(out=PZ[:, :, mb, h, :], in0=PT4[:, mb, :, h].unsqueeze(1).to_broadcast([128, 16, 16]), in1=OH16, op=ALU.mult),
                         reads=["xdd"], writes=["RA"])
            ob = [bank(), bank(), bank(), bank()]
            for si in range(NS):
                Vb, vkey = (BIGA, "BIGA") if si % 2 == 0 else (BIGB, "BIGB")
                P.dma("sp", lambda e, si=si, Vb=Vb: e.dma_start(out=Vb[:].rearrange("p (mb d) -> p mb d", mb=2), in_=cv[si].rearrange("(mb p) d -> p mb d", p=128)), writes=[vkey])
                for h in range(4):
                    for mb in range(2):
                        P.op("pe", lambda e, si=si, h=h, mb=mb, ob=ob, Vb=Vb: e.matmul(ps[ob[h]][:nt, 0:256], lhsT=PZ[:, si, mb, h, :],
                                                                                       rhs=Vb[:, mb * 1024 + h * 256:mb * 1024 + (h + 1) * 256],
                                                                                       start=(si == 0 and mb == 0), stop=(si == NS - 1 and mb == 1)),
                             reads=["RA", vkey], writes=[("ps", ob[h])])
            for h in range(4):
                P.op("act", lambda e, h=h, ob=ob: e.activation(out=OSB[:nt, h * 256:(h + 1) * 256], in_=ps[ob[h]][:nt, 0:256], func=AF.Copy), reads=[("ps", ob[h])], writes=["BIGC"])
            attn_tail(nt)


        def store_T(src_fn, nchunks, nrows, dst, skey):
            done = 0
            for (stg, stkey, cap) in ((BIGA, "BIGA", 16), (BIGB, "BIGB", 16)):
                n_here = min(cap, nchunks - done)
                if n_here <= 0:
                    break
                for g in range(0, n_here, 4):
                    n = min(4, n_here - g)
                    b = bank()
                    for q in range(n):
                        c = done + g + q
                        P.op("pe", lambda e, q=q, c=c, b=b: e.transpose(out=ps[b][:nrows, q * 128:(q + 1) * 128], in_=src_fn(c), identity=ident), reads=[skey, "CM"], writes=[("ps", b)])
                    P.op("act", lambda e, b=b, g=g, n=n, stg=stg: e.activation(out=stg[:nrows, g * 128:(g + n) * 128], in_=ps[b][:nrows, :n * 128], func=AF.Copy), reads=[("ps", b)], writes=[stkey])
                P.dma("sp", lambda e, stg=stg, done=done, n_here=n_here: e.dma_start(out=dst[:, done * 128:(done + n_here) * 128], in_=stg[:nrows, 0:n_here * 128]), reads=[stkey])
                done += n_here

        cast_engs = ["act", "dve", "pool"]
        it = 0
        for (src, dstd, dkey) in ((exp_u, ub, "ub"), (exp_v, vb, "vb")):
            sv = src.rearrange("(b p r) d -> b p (r d)", p=128, r=4); dv = dstd.rearrange("(b p r) d -> b p (r d)", p=128, r=4)
            for blk in range(32):
                i = it % 2; it += 1
                stg = WB[i][:].rearrange("p a b -> p (a b)"); big, bkey = ((BIGA, "BIGA"), (BIGB, "BIGB"))[i]
                ob16 = big[:].bitcast(BF16)
                P.dma("sp", lambda e, stg=stg, sv=sv, blk=blk: e.dma_start(out=stg, in_=sv[blk]), writes=[("wb", i)])
                eng = cast_engs[it % 3]
                if eng == "act":
                    P.op("act", lambda e, stg=stg, ob16=ob16: e.activation(out=ob16, in_=stg, func=AF.Copy), reads=[("wb", i)], writes=[bkey])
                else:
                    P.op(eng, lambda e, stg=stg, ob16=ob16: e.tensor_copy(out=ob16, in_=stg), reads=[("wb", i)], writes=[bkey])
                P.dma("sp", lambda e, ob16=ob16, dv=dv, blk=blk: e.dma_start(out=dv[blk], in_=ob16), reads=[bkey], writes=[dkey])

        mT = BIGA[:].rearrange("p (c t) -> p c t", c=8)
        KT = sb("KT", [128, 8, 256]); Vt = sb("Vt", [128, 2, D])
        osb = sb("osb", [128, 512])
        for mb in range(2):
            P.dma("sp", lambda e, mb=mb: e.dma_start(out=xres[:], in_=memp[mb * 128:(mb + 1) * 128, :]), writes=["xres"])
            tmpT = hT
            rmsnorm_T(xres, 128, 24, tmpT, "xres", "hT")
            P.op("dve", lambda e, mb=mb: e.tensor_copy(out=mT[:, :, mb * 128:(mb + 1) * 128], in_=hT[:]), reads=["hT"], writes=["BIGA"])
        for (wd, od) in ((w_mk, mk_p), (w_mv, mv_p)):
            for blk in range(2):
                i = load_w(wd, blk * 512, 512)
                for mb in range(2):
                    b = bank()
                    for kc in range(8):
                        P.op("pe", lambda e, kc=kc, mb=mb, b=b, i=i: e.matmul(ps[b][:, :512], lhsT=mT[:, kc, mb * 128:(mb + 1) * 128], rhs=WB[i][:, kc, :],
                                                                               start=(kc == 0), stop=(kc == 7)),
                             reads=[("wb", i), "BIGA"], writes=[("ps", b)])
                    P.op("act", lambda e, b=b: e.activation(out=osb[:], in_=ps[b][:, :512], func=AF.Copy), reads=[("ps", b)], writes=["osb"])
                    if wd is w_mv:
                        P.op("pool", lambda e, mb=mb, blk=blk: e.tensor_copy(out=Vt[:, mb, blk * 512:(blk + 1) * 512], in_=osb[:]), reads=["osb"], writes=["Vt"])
                    P.dma("sp", lambda e, od=od, mb=mb, blk=blk: e.dma_start(out=od[mb * 128:(mb + 1) * 128, blk * 512:(blk + 1) * 512], in_=osb[:]),
                          reads=["osb"])

        for blk in range(0 if os.environ.get("K_NOKT") else 2):
            i = load_w(w_mk, blk * 512, 512)
            for half in range(2):
                b = bank()
                for q in range(2):
                    cc = half * 2 + q
                    for kc in range(8):
                        P.op("pe", lambda e, kc=kc, q=q, cc=cc, b=b, i=i: e.matmul(ps[b][:, q * 256:(q + 1) * 256], lhsT=WB[i][:, kc, cc * 128:(cc + 1) * 128], rhs=mT[:, kc, :],
                                                                                   start=(kc == 0), stop=(kc == 7)), reads=[("wb", i), "BIGA"], writes=[("ps", b)])
                P.op("act", lambda e, b=b, blk=blk, half=half: e.activation(out=KT[:, blk * 4 + half * 2:blk * 4 + half * 2 + 2, :], in_=ps[b][:, :512].rearrange("p (q m) -> p q m", q=2), func=AF.Copy),
                     reads=[("ps", b)], writes=["KT"])

        P.op("pool", lambda e: e.memset(xbcT[:, :, 0:3], 0.0), writes=["xbcT"])
        P.op("pool", lambda e: e.memset(rwT[:, :, 0:1], 0.0), writes=["rwT"])
        P.op("pool", lambda e: e.memset(stT[:], 0.0), writes=["stT"])
        nch = int(os.environ.get('K_NCH', NCH))
        XRS = [[xres, "xres"], [xres2, "xres2"]]

        def stage_a(ch):
            xr, xk_ = CUR[0], CUR[1]
            P.dma("sp", lambda e, ch=ch, xr=xr: e.dma_start(out=xr[:], in_=xp[ch * 128:(ch + 1) * 128, :]), writes=[xk_])
            rmsnorm_T(xr, 128, 0, hT, xk_, "hT")
            in_proj(128, 3, 1)

        CUR[:] = XRS[0]
        stage_a(0)
        for ch in range(nch):
            mine = XRS[ch % 2]; other = XRS[(ch + 1) % 2]
            CUR[:] = mine
            ssd_chunk()
            rwkv_chunk()
            mixer_out(128)
            P.op("dve", lambda e: e.tensor_copy(out=xbcT[:, :, 0:3], in_=xbcT[:, :, 128:131]), reads=["xbcT"], writes=["xbcT"])
            P.op("dve", lambda e: e.tensor_copy(out=rwT[:, :, 0:1], in_=rwT[:, :, 128:129]), reads=["rwT"], writes=["rwT"])
            attn_prompt()
            ap = peer(128)
            if ch + 1 < nch:
                CUR[:] = other
                stage_a(ch + 1)
                CUR[:] = mine
            ap()
            final_out(128, y_p[ch * 128:(ch + 1) * 128, :])
        CUR[:] = XRS[0]
        store_T(lambda c: xbcT[:, c, 0:3], 12, 3, conv_p, "xbcT")
        store_T(lambda c: rwT[:, c, 0:1], 26, 1, shift_p, "rwT")


        if stage >= 4:
            SCT = BKH[:, 0:576].rearrange("p (c t) -> p c t", c=12)
            YST = convT[:, 0:8, 16:32]; YRT = convT[:, 0:8, 32:48]
            SEL = BK[:16, 0:2048].rearrange("p (s m) -> p s m", s=16)
            P.dma("sp", lambda e: e.dma_start(out=xres[:16, :], in_=xs_in), writes=["xres"])
            rmsnorm_T(xres, 16, 0, hT, "xres", "hT")
            in_proj(16, 3, 1)
            P.dma("sp", lambda e: e.dma_start(out=conv_s[:, 0:2, :], in_=st_conv[:, 1:3, :]))
            store_T(lambda c: xbcT[:, c, 3:19], 12, NS, conv_s[:, 2, :], "xbcT")
            store_T(lambda c: rwT[:, c, 1:17], 26, NS, shift_s, "rwT")
            P.dma("sp", lambda e: e.dma_start(out=ysb[:48, :], in_=st_conv.rearrange("s k c -> (s k) c")[:, 0:1024]), writes=["ysb"])
            P.dma("sp", lambda e: e.dma_start(out=xdd[:48, 0:512], in_=st_conv.rearrange("s k c -> (s k) c")[:, 1024:1536]), writes=["xdd"])
            for grp in range(3):
                b = bank()
                for q in range(4):
                    c = grp * 4 + q
                    src = ysb[:48, c * 128:(c + 1) * 128] if c < 8 else xdd[:48, (c - 8) * 128:(c - 7) * 128]
                    P.op("pe", lambda e, q=q, b=b, src=src: e.transpose(out=ps[b][:, q * 48:(q + 1) * 48], in_=src, identity=ident[:48, :48]),
                         reads=["ysb", "xdd", "CM"], writes=[("ps", b)])
                P.op("act", lambda e, b=b, grp=grp: e.activation(out=SCT[:, grp * 4:(grp + 1) * 4, :], in_=ps[b][:, :192].rearrange("p (q t) -> p q t", q=4), func=AF.Copy),
                     reads=[("ps", b)], writes=["BKH"])
            A = BIGA[:, :192].rearrange("p (c t) -> p c t", c=12); Bv = BIGB[:, :192].rearrange("p (c t) -> p c t", c=12)
            SC4 = SCT.rearrange("p c (s k) -> p c s k", k=3)
            P.op("dve", lambda e: e.tensor_tensor(out=A, in0=xbcT[:, :, 3:19], in1=CWv[:, :, 3:4].to_broadcast([128, 12, 16]), op=ALU.mult), reads=["xbcT", "CF"], writes=["BIGA"])
            for k in range(3):
                P.op("dve", lambda e, k=k: e.tensor_tensor(out=Bv, in0=SC4[:, :, :, k], in1=CWv[:, :, k:k + 1].to_broadcast([128, 12, 16]), op=ALU.mult), reads=["BKH", "CF"], writes=["BIGB"])
                P.op("dve", lambda e: e.tensor_tensor(out=A, in0=A, in1=Bv, op=ALU.add), reads=["BIGA", "BIGB"], writes=["BIGA"])
            for c in range(12):
                P.op("act", lambda e, c=c: e.activation(out=convT[:, c, :16], in_=A[:, c, :], func=AF.Silu, bias=CF[:, 80 + c:81 + c]), reads=["BIGA", "CF"], writes=["convT"])
            dt_softplus(16)
            xS = xdt
            xS2 = xn
            for grp in range(3):
                b = bank()
                for q in range(4):
                    c = grp * 4 + q
                    P.op("pe", lambda e, q=q, c=c, b=b: e.transpose(out=ps[b][:16, q * 128:(q + 1) * 128], in_=convT[:, c, :16], identity=ident),
                         reads=["convT", "CM"], writes=[("ps", b)])
                dst = xS[:16, grp * 512:(grp + 1) * 512] if grp < 2 else xS2[:16, 0:512]
                P.op("act", lambda e, b=b, dst=dst: e.activation(out=dst, in_=ps[b][:16, :512], func=AF.Copy), reads=[("ps", b)], writes=["xdt", "xn"])
            P.op("act", lambda e: e.activation(out=sm[:16, 64:80], in_=sm[:16, 48:64], func=AF.Exp), reads=["sm"], writes=["sm"])
            P.op("dve", lambda e: e.tensor_copy(out=v3(junk[:16, :], 16), in_=sm[:16, 64:80].unsqueeze(2).to_broadcast([16, 16, 64])), reads=["sm"], writes=["junk"])
            P.op("dve", lambda e: e.tensor_tensor(out=v3(xtok[:16, 0:1024], 16), in0=v3(xS[:16, :], 16), in1=sm[:16, 32:48].unsqueeze(2).to_broadcast([16, 16, 64]), op=ALU.mult),
                 reads=["xdt", "sm"], writes=["xtok"])
            DT2 = RA[:, 0:256].rearrange("p (w j s) -> p w j s", w=2, j=8)
            b = bank()
            for w, (srct, skey) in enumerate(((junk, "junk"), (xtok, "xtok"))):
                for j in range(8):
                    P.op("pe", lambda e, w=w, j=j, b=b, srct=srct: e.transpose(out=ps[b][:, (w * 8 + j) * 16:(w * 8 + j + 1) * 16], in_=srct[:16, j * 128:(j + 1) * 128], identity=ident[:16, :16]),
                         reads=[skey, "CM"], writes=[("ps", b)])
            P.op("act", lambda e, b=b: e.activation(out=RA[:, 0:256], in_=ps[b][:, :256], func=AF.Copy), reads=[("ps", b)], writes=["RA"])
            P.op("dve", lambda e: e.tensor_copy(out=SEL, in_=ident[:16, :16].unsqueeze(2).to_broadcast([16, 16, 128])), reads=["CM"], writes=["BK"])
            for si in range(NS):
                Sin = BIGA[:, (si % 2) * 1024:(si % 2 + 1) * 1024]; Sout = BIGC[:, (si % 2) * 1024:(si % 2 + 1) * 1024]
                tA = BIGB[:, 0:1024]; tB = BIGB[:, 1024:2048]
                P.dma("sp", lambda e, si=si, Sin=Sin: e.dma_start(out=Sin.rearrange("p (j n) -> p j n", j=8), in_=st_ssm[si].rearrange("h p n -> (h p) n").rearrange("(j q) n -> q j n", q=128)),
                      writes=["BIGA"])
                b = bank()
                P.op("pe", lambda e, b=b, si=si: e.matmul(ps[b][:, :512], lhsT=SEL[:, si, :], rhs=xS2[:16, 0:512], start=True, stop=True), reads=["BK", "xn"], writes=[("ps", b)])
                P.op("dve", lambda e, si=si, Sin=Sin, tA=tA: e.tensor_tensor(out=v3(tA, 8), in0=v3(Sin, 8), in1=DT2[:, 0, :, si:si + 1].to_broadcast([128, 8, 128]), op=ALU.mult),
                     reads=["BIGA", "RA"], writes=["BIGB"])
                P.op("dve", lambda e, si=si, b=b, tB=tB: e.tensor_tensor(out=tB.rearrange("p (g j n) -> p g j n", g=2, j=4),
                                                                           in0=ps[b][:, 0:256].rearrange("p (g n) -> p g n", g=2).unsqueeze(2).to_broadcast([128, 2, 4, 128]),
                                                                           in1=DT2[:, 1, :, si].rearrange("p (g j) -> p g j", g=2).unsqueeze(3).to_broadcast([128, 2, 4, 128]), op=ALU.mult),
                     reads=[("ps", b), "RA"], writes=["BIGB"])
                P.op("dve", lambda e, Sout=Sout, tA=tA, tB=tB: e.tensor_tensor(out=Sout, in0=tA, in1=tB, op=ALU.add), reads=["BIGB"], writes=["BIGC"])
                P.dma("sp", lambda e, si=si, Sout=Sout: e.dma_start(out=ssm_s[si].rearrange("(j q) n -> q j n", q=128), in_=Sout.rearrange("p (j n) -> p j n", j=8)), reads=["BIGC"])
                P.op("dve", lambda e, si=si, b=b, Sout=Sout, tA=tA: e.tensor_tensor(out=tA.rearrange("p (g j n) -> p g j n", g=2, j=4), in0=Sout.rearrange("p (g j n) -> p g j n", g=2, j=4),
                                                                                     in1=ps[b][:, 256:512].rearrange("p (g n) -> p g n", g=2).unsqueeze(2).to_broadcast([128, 2, 4, 128]), op=ALU.mult),
                     reads=[("ps", b), "BIGC"], writes=["BIGB"])
                P.op("dve", lambda e, si=si, tA=tA: e.tensor_reduce(out=YST[:, :, si], in_=v3(tA, 8), axis=AX.X, op=ALU.add), reads=["BIGB"], writes=["convT"])
            P.op("pool", lambda e: e.tensor_tensor(out=v3(junk[:16, :], 16), in0=v3(xS[:16, 0:1024], 16), in1=CT[:16, 32:48].unsqueeze(2).to_broadcast([16, 16, 64]), op=ALU.mult),
                 reads=["xdt", "CT"], writes=["junk"])
            for half in range(2):
                b = bank()
                for q in range(4):
                    j = half * 4 + q
                    P.op("pe", lambda e, q=q, j=j, b=b: e.transpose(out=ps[b][:16, q * 128:(q + 1) * 128], in_=YST[:, j, :], identity=ident), reads=["convT", "CM"], writes=[("ps", b)])
                P.op("dve", lambda e, b=b, half=half: e.tensor_tensor(out=ysb[:16, half * 512:(half + 1) * 512], in0=junk[:16, half * 512:(half + 1) * 512], in1=ps[b][:16, :512], op=ALU.add),
                     reads=[("ps", b), "junk"], writes=["ysb"])
            ssd_post(16)
            P.dma("sp", lambda e: e.dma_start(out=BIGA[:16, 0:2048], in_=st_shift[:, 0:2048]), writes=["BIGA"])
            P.dma("sp", lambda e: e.dma_start(out=BIGB[:16, 0:1280], in_=st_shift[:, 2048:3328]), writes=["BIGB"])
            b = bank()
            for c in range(26):
                src = BIGA[:16, c * 128:(c + 1) * 128] if c < 16 else BIGB[:16, (c - 16) * 128:(c - 15) * 128]
                P.op("pe", lambda e, c=c, b=b, src=src: e.transpose(out=ps[b][:, c * 16:(c + 1) * 16], in_=src, identity=ident[:16, :16]), reads=["BIGA", "BIGB", "CM"], writes=[("ps", b)])
            P.op("act", lambda e, b=b: e.activation(out=rwT[:, :, 32:48], in_=ps[b][:, :416].rearrange("p (c t) -> p c t", c=26), func=AF.Copy), reads=[("ps", b)], writes=["rwT"])
            rT, kT, vT, t1, t2, t4, t5, t6, t7 = rwkv_pre(16, rwT[:, :, 32:48], rwT[:, :, 1:17])
            P.op("act", lambda e: e.activation(out=t1, in_=t1, func=AF.Exp), reads=["BIGA"], writes=["BIGA"])
            TKs = [(t1, xdd, "xdd", "BIGA", 1.0), (t4, ysb, "ysb", "BIGB", -1.0), (t2, junk, "junk", "BIGB", 1.0), (kT, xtok, "xtok", "RWS", 1.0), (rT, xn, "xn", "RWS", 1.0)]
            for (src3, dst, dkey, skey, scl) in TKs:
                for half in range(2):
                    b = bank()
                    for q in range(4):
                        c = half * 4 + q
                        P.op("pe", lambda e, q=q, c=c, b=b, src3=src3: e.transpose(out=ps[b][:16, q * 128:(q + 1) * 128], in_=src3[:, c, :], identity=ident), reads=[skey, "CM"], writes=[("ps", b)])
                    P.op("act", lambda e, b=b, half=half, dst=dst, scl=scl: e.activation(out=dst[:16, half * 512:(half + 1) * 512], in_=ps[b][:16, :512], func=AF.Copy, scale=scl),
                         reads=[("ps", b)], writes=[dkey])
            SELH = [RA[:16, 0:2048].rearrange("p (s m) -> p s m", s=16), BKH[:16, 0:2048].rearrange("p (s m) -> p s m", s=16)]
            for hp, key in ((0, "RA"), (1, "BKH")):
                P.op("pool", lambda e, hp=hp: e.memset(SELH[hp], 0.0), writes=[key])
                P.op("dve", lambda e, hp=hp: e.tensor_copy(out=SELH[hp][:, :, hp * 64:(hp + 1) * 64], in_=ident[:16, :16].unsqueeze(2).to_broadcast([16, 16, 64])), reads=["CM"], writes=[key])
            tmp = BIGA[:, 0:512]
            for si in range(NS):
                SV = BIGB[:, (si % 2) * 512:(si % 2 + 1) * 512]; S1 = BIGB[:, 1024 + (si % 2) * 512:1024 + (si % 2 + 1) * 512]
                for hp in range(2):
                    P.dma("sp", lambda e, si=si, hp=hp, SV=SV: e.dma_start(out=SV[hp * 64:(hp + 1) * 64, :].rearrange("p (j k) -> p j k", j=8),
                                                                           in_=st_wkv[si].rearrange("(j hp) v k -> hp v j k", hp=2)[hp]), writes=["BIGB"])

                def bcast(Xt, xkey, si=si):
                    b = bank()
                    for hp in range(2):
                        P.op("pe", lambda e, hp=hp, b=b, Xt=Xt, si=si: e.matmul(ps[b][:, :512], lhsT=SELH[hp][:, si, :],
                                                                                rhs=Xt[:16, 0:1024].rearrange("s (j hp k) -> s j hp k", hp=2, k=64)[:, :, hp, :],
                                                                                start=(hp == 0), stop=(hp == 1)), reads=["RA", "BKH", xkey], writes=[("ps", b)])
                    return b
                ba = bcast(ysb, "ysb")
                P.op("dve", lambda e, ba=ba, SV=SV: e.tensor_tensor(out=tmp, in0=SV, in1=ps[ba][:, :512], op=ALU.mult), reads=[("ps", ba), "BIGB"], writes=["BIGA"])
                P.op("dve", lambda e: e.tensor_reduce(out=smw[:, 8:16], in_=v3(tmp, 8), axis=AX.X, op=ALU.add), reads=["BIGA"], writes=["smw"])
                bw = bcast(xdd, "xdd")
                P.op("dve", lambda e, bw=bw, SV=SV, S1=S1: e.tensor_tensor(out=S1, in0=SV, in1=ps[bw][:, :512], op=ALU.mult), reads=[("ps", bw), "BIGB"], writes=["BIGB"])
                bb = bcast(junk, "junk")
                P.op("dve", lambda e, bb=bb: e.tensor_tensor(out=v3(tmp, 8), in0=v3(ps[bb][:, :512], 8), in1=smw[:, 8:16].unsqueeze(2).to_broadcast([128, 8, 64]), op=ALU.mult),
                     reads=[("ps", bb), "smw"], writes=["BIGA"])
                P.op("dve", lambda e, S1=S1: e.tensor_tensor(out=S1, in0=S1, in1=tmp, op=ALU.add), reads=["BIGA", "BIGB"], writes=["BIGB"])
                bk = bcast(xtok, "xtok")
                P.op("dve", lambda e, bk=bk, si=si: e.tensor_tensor(out=v3(tmp, 8), in0=v3(ps[bk][:, :512], 8), in1=RWS3[:, 16:24, si:si + 1].to_broadcast([128, 8, 64]), op=ALU.mult),
                     reads=[("ps", bk), "RWS"], writes=["BIGA"])
                P.op("dve", lambda e, S1=S1: e.tensor_tensor(out=S1, in0=S1, in1=tmp, op=ALU.add), reads=["BIGA", "BIGB"], writes=["BIGB"])
                for hp in range(2):
                    P.dma("sp", lambda e, si=si, hp=hp, S1=S1: e.dma_start(out=wkv_s[si].rearrange("(j hp) v k -> hp v j k", hp=2)[hp],
                                                                           in_=S1[hp * 64:(hp + 1) * 64, :].rearrange("p (j k) -> p j k", j=8)), reads=["BIGB"])
                br = bcast(xn, "xn")
                P.op("dve", lambda e, br=br, S1=S1: e.tensor_tensor(out=tmp, in0=S1, in1=ps[br][:, :512], op=ALU.mult), reads=[("ps", br), "BIGB"], writes=["BIGA"])
                P.op("dve", lambda e, si=si: e.tensor_reduce(out=YRT[:, :, si], in_=v3(tmp, 8), axis=AX.X, op=ALU.add), reads=["BIGA"], writes=["convT"])
            for half in range(2):
                b = bank()
                for q in range(4):
                    j = half * 4 + q
                    P.op("pe", lambda e, q=q, j=j, b=b: e.transpose(out=ps[b][:16, q * 128:(q + 1) * 128], in_=YRT[:, j, :], identity=ident), reads=["convT", "CM"], writes=[("ps", b)])
                P.op("act", lambda e, b=b, half=half: e.activation(out=Ys[:16, half * 512:(half + 1) * 512], in_=ps[b][:16, :512], func=AF.Copy), reads=[("ps", b)], writes=["BIGA"])
            rwkv_post(16)
            mixer_out(16)
            if os.environ.get("K_DBGS") == "1":
                P.dma("sp", lambda e: e.dma_start(out=y_s, in_=xres[:NS, :]), reads=["xres"])
            if stage >= 5:
                attn_sample()
            if os.environ.get("K_DBGS") == "2":
                P.dma("sp", lambda e: e.dma_start(out=y_s, in_=xres[:NS, :]), reads=["xres"])
            if stage >= 6:
                peer(NS)()
            if stage >= 7:
                final_out(NS, y_s)
        if stage >= 2:
            for half in range(2):
                b = bank()
                for q in range(4):
                    c = half * 4 + q
                    P.op("pe", lambda e, c=c, q=q, b=b: e.transpose(out=ps[b][:, q * 128:(q + 1) * 128], in_=stT[:, c * 128:(c + 1) * 128], identity=ident),
                         reads=["stT", "CM"], writes=[("ps", b)])
                P.op("act", lambda e, b=b: e.activation(out=osb[:], in_=ps[b][:, :512], func=AF.Copy), reads=[("ps", b)], writes=["osb"])
                P.dma("sp", lambda e, half=half: e.dma_start(out=ssm_p[half * 512:(half + 1) * 512, :].rearrange("(q p) n -> p q n", p=128),
                                                             in_=osb[:].rearrange("p (q n) -> p q n", q=4)), reads=["osb"])
        if stage >= 3:
            for half in range(2):
                b = bank()
                for q in range(4):
                    j = half * 4 + q
                    P.op("pe", lambda e, j=j, q=q, b=b: e.transpose(out=ps[b][:64, q * 128:(q + 1) * 128], in_=Hst[:, j, :], identity=ident),
                         reads=["Hst", "CM"], writes=[("ps", b)])
                P.op("act", lambda e, b=b: e.activation(out=osb[:64, :], in_=ps[b][:64, :512], func=AF.Copy), reads=[("ps", b)], writes=["osb"])
                P.dma("sp", lambda e, half=half: e.dma_start(out=wkv_p[half * 8:(half + 1) * 8].rearrange("h v k -> v h k"),
                                                             in_=osb[:64, :].rearrange("p (h k) -> p h k", h=8)), reads=["osb"])
        print('SBUF bytes remaining', nc.sbuf_bytes_remaining)
        P.emit()
    return nc


_CACHE = {}


def kernel(**inp):
    f = lambda k: np.asarray(inp[k], np.float32)
    if "nc" not in _CACHE:
        _CACHE["nc"] = build_program()
    nc = _CACHE["nc"]
    cfm = np.zeros((128, 192), np.float32)
    cfm[:, 0:8] = fm(f("norm_mix_w")[0], 8); cfm[:, 8:16] = fm(f("norm_mem_w")[0], 8)
    cfm[:, 16:24] = fm(f("norm_ffn_w")[0], 8); cfm[:, 24:32] = fm(f("mem_norm_w")[0], 8)
    cw = f("conv_w")[0]
    cfm[:, 32:80] = np.stack([fm(cw[k], 12) for k in range(4)], axis=2).reshape(128, 48)
    cfm[:, 80:92] = fm(f("conv_b")[0], 12)
    cfm[:, 92:118] = fm(f("rwkv_mu")[0], 26)
    cfm[:, 118:126] = fm(f("rwkv_w0")[0], 8); cfm[:, 126:134] = fm(f("rwkv_a0")[0], 8)
    cfm[:, 134:142] = fm(f("rwkv_k_k")[0], 8); cfm[:, 142:150] = fm(f("rwkv_k_a")[0], 8)
    cfm[:, 150:158] = fm(f("rwkv_r_k")[0].reshape(-1), 8)
    cfm[:, 160:168] = fm(f("rwkv_ln_w")[0], 8); cfm[:, 168:176] = fm(f("rwkv_ln_b")[0], 8)
    cfm[:, 176:184] = fm(f("ssd_norm_w")[0], 8)
    ctk = np.zeros((1, 48 + D), np.float32)
    ctk[0, 0:16] = f("dt_bias")[0]; ctk[0, 16:32] = f("a_log")[0]; ctk[0, 32:48] = f("d_skip")[0]
    ctk[0, 48:] = f("norm_final_w")
    r = np.arange(128)
    cmat = np.zeros((128, 6 * 128 + 16), np.float32)
    cmat[:, 768:784] = np.arange(16)[None, :]
    cmat[:, 0:128] = np.eye(128)
    cmat[:, 128:256] = (r[:, None] > r[None, :])
    cmat[:, 256:384] = (r[:, None] <= r[None, :])
    cmat[:, 384:512] = 1.0
    cmat[:, 512:640] = ((r[:, None] // 64) == (r[None, :] // 64))
    cmat[:, 640:768] = (r[:, None] < r[None, :])
    shared = {
        "w_in": f("w_in")[0], "w_out": f("w_out")[0], "w_mk": f("w_mk")[0], "w_mv": f("w_mv")[0],
        "w_mq": f("w_mq")[0], "w_mo": f("w_mo")[0], "w_pq": f("w_pq")[0],
        "sub_keys": f("sub_keys")[0].reshape(16, 128, 128),
        "cfm": cfm, "ctk": ctk, "cmat": cmat,
        "wa2": np.concatenate([f("rwkv_w2")[0], f("rwkv_a2")[0]], axis=0), "g2": f("rwkv_g2")[0],
    }
    if True:
        shared["exp_u"] = f("expert_u")[0]; shared["exp_v"] = f("expert_v")[0]
    in_maps = []
    for c in range(NCORES):
        s = slice(c * NS, (c + 1) * NS)
        m = dict(shared)
        m.update({
            "xp": f("x_prompt")[c], "xs": f("x_sample")[s, 0], "memp": f("mem_prompt")[c],
            "st_ssm": f("state_ssm")[0, s], "st_conv": f("state_conv")[0, s], "st_wkv": f("state_wkv")[0, s],
            "st_shift": f("state_shift")[0, s],
            "ck": f("cache_mem_k")[0, s].reshape(NS, NMEM, D), "cv": f("cache_mem_v")[0, s].reshape(NS, NMEM, D),
        })
        in_maps.append(m)
    res = run_bass_kernel_spmd(nc, in_maps, core_ids=list(range(NCORES))).results
    cat = lambda k: np.stack([r_[k] for r_ in res], axis=0)
    y_prompt = cat("y_p")
    y_sample = np.concatenate([r_["y_s"] for r_ in res], axis=0).reshape(128, 1, D)
    ssm_prompt = cat("ssm_p").reshape(1, 8, 16, 64, 128)
    conv_prompt = cat("conv_p").reshape(1, 8, 3, CONV_DIM)
    wkv_prompt = cat("wkv_p").reshape(1, 8, 16, 64, 64)
    shift_prompt = cat("shift_p").reshape(1, 8, RWP)
    mem_k_prompt = cat("mk_p").reshape(1, 8, NMEM, 4, 256)
    mem_v_prompt = cat("mv_p").reshape(1, 8, NMEM, 4, 256)
    ssm_sample = np.concatenate([r_["ssm_s"] for r_ in res], axis=0).reshape(1, 128, 16, 64, 128)
    conv_sample = np.concatenate([r_["conv_s"] for r_ in res], axis=0).reshape(1, 128, 3, CONV_DIM)
    wkv_sample = np.concatenate([r_["wkv_s"] for r_ in res], axis=0).reshape(1, 128, 16, 64, 64)
    shift_sample = np.concatenate([r_["shift_s"] for r_ in res], axis=0).reshape(1, 128, RWP)
    return (y_prompt, y_sample, ssm_prompt, conv_prompt, wkv_prompt, shift_prompt, mem_k_prompt, mem_v_prompt,
            ssm_sample, conv_sample, wkv_sample, shift_sample)
```
